# Optimizing a Trainium2 kernel written in Bass

```python
import jax, jax.numpy as jnp
from jax import lax
import numpy as np

D_MODEL = 1024
BATCH = 32
SEQ = 2048
DEPTH = 1

GRID_W = 64
HEAD_DIM = 128
LA_HEADS = 8
LA_DIM = LA_HEADS * HEAD_DIM
CONV_K = 5
CHUNK = 64
ATT_HEADS = 8
KV_HEADS = 2
GQA_GROUP = ATT_HEADS // KV_HEADS
ATT_Q_DIM = ATT_HEADS * HEAD_DIM
ATT_KV_DIM = KV_HEADS * HEAD_DIM
Q_BLOCK = 128
ROPE_THETA = 10000.0
MEM_TOKENS = 256
MEM_HEADS = 4
MEM_HEAD_DIM = D_MODEL // MEM_HEADS
D_FF = -(-(8 * D_MODEL) // (3 * 256)) * 256
EPS = 1e-6
IN_SPLITS = (3 * LA_DIM, LA_DIM, 2 * LA_HEADS, 2 * LA_HEADS,
             ATT_Q_DIM, ATT_KV_DIM, ATT_KV_DIM, 2 * D_MODEL)
IN_DIM = 3 * LA_DIM + LA_DIM + 4 * LA_HEADS + ATT_Q_DIM + 2 * ATT_KV_DIM + 2 * D_MODEL

kernel_name = "hybrid_gdn_axial_gqa_memory_encoder"


def _rms_norm(x, w):
    xf = x.astype(jnp.float32)
    y = xf * lax.rsqrt(jnp.mean(xf * xf, axis=-1, keepdims=True) + EPS)
    return (y * w.astype(jnp.float32)).astype(x.dtype)


def _l2_norm(x):
    xf = x.astype(jnp.float32)
    return xf * lax.rsqrt(jnp.sum(xf * xf, axis=-1, keepdims=True) + EPS)


def _split_cols(a, sizes):
    out, start = [], 0
    for s in sizes:
        out.append(a[..., start:start + s])
        start += s
    return out


def _centred_depthwise_conv(x, w):
    pad = (CONV_K - 1) // 2
    return lax.conv_general_dilated(
        x, w[:, None, :].astype(x.dtype), window_strides=(1,), padding=[(pad, pad)],
        dimension_numbers=("NWC", "WIO", "NWC"), feature_group_count=x.shape[-1])


def _gated_delta_chunked(q, k, v, g, beta):
    B, H, T, D = q.shape
    N = T // CHUNK
    q = (q * D ** -0.5).reshape(B, H, N, CHUNK, D)
    k = k.reshape(B, H, N, CHUNK, D)
    v = v.reshape(B, H, N, CHUNK, D)
    g = jnp.cumsum(g.reshape(B, H, N, CHUNK), axis=-1)
    beta = beta.reshape(B, H, N, CHUNK)
    k_beta = k * beta[..., None]
    v_beta = v * beta[..., None]
    tri = jnp.tril(jnp.ones((CHUNK, CHUNK), dtype=bool))
    strict = jnp.tril(jnp.ones((CHUNK, CHUNK), dtype=bool), -1)
    diff = g[..., :, None] - g[..., None, :]
    decay = jnp.where(tri, jnp.exp(jnp.where(tri, diff, 0.0)), 0.0)
    L = jnp.where(strict, jnp.einsum("bhncd,bhnsd->bhncs", k_beta, k) * decay, 0.0)
    eye = jnp.eye(CHUNK, dtype=q.dtype)
    t_inv = lax.linalg.triangular_solve(eye + L, jnp.broadcast_to(eye, L.shape),
                                        left_side=True, lower=True, unit_diagonal=True)
    u = jnp.einsum("bhncs,bhnsd->bhncd", t_inv, v_beta)
    w = jnp.einsum("bhncs,bhnsd->bhncd", t_inv, k_beta * jnp.exp(g)[..., None])
    qk = jnp.where(tri, jnp.einsum("bhncd,bhnsd->bhncs", q, k) * decay, 0.0)
    g_last = g[..., -1]
    q_g = q * jnp.exp(g)[..., None]
    k_d = k * jnp.exp(g_last[..., None] - g)[..., None]
    d_last = jnp.exp(g_last)

    def step(S, xs):
        qg_c, kd_c, u_c, w_c, qk_c, dl_c = xs
        v_new = u_c - jnp.einsum("bhcd,bhde->bhce", w_c, S)
        o = jnp.einsum("bhcd,bhde->bhce", qg_c, S) + jnp.einsum("bhcs,bhse->bhce", qk_c, v_new)
        S = S * dl_c[..., None, None] + jnp.einsum("bhcd,bhce->bhde", kd_c, v_new)
        return S, o

    xs = tuple(jnp.moveaxis(a, 2, 0) for a in (q_g, k_d, u, w, qk, d_last))
    S0 = jnp.zeros((B, H, D, D), q.dtype)
    _, o = lax.scan(step, S0, xs)
    return jnp.moveaxis(o, 0, 2).reshape(B, H, T, D)


def _mixer_a(qkv, z, a_raw, b_raw, conv_w, a_log, dt_bias, norm_w, w_out_a):
    B, T, _ = qkv.shape
    f32 = jnp.float32
    qkv = jax.nn.silu(_centred_depthwise_conv(qkv, conv_w))
    q, k, v = jnp.split(qkv, 3, axis=-1)
    heads = lambda a: a.reshape(B, T, LA_HEADS, HEAD_DIM).transpose(0, 2, 1, 3)
    q = _l2_norm(heads(q))
    k = _l2_norm(heads(k))
    v = heads(v).astype(f32)
    dirs = lambda a: a.astype(f32).reshape(B, T, 2, LA_HEADS).transpose(2, 0, 3, 1)
    g = -jnp.exp(a_log.astype(f32))[:, None, :, None] * jax.nn.softplus(
        dirs(a_raw) + dt_bias.astype(f32)[:, None, :, None])
    beta = jax.nn.sigmoid(dirs(b_raw))
    o_fwd = _gated_delta_chunked(q, k, v, g[0], beta[0])
    flip = lambda a: jnp.flip(a, axis=2)
    o_bwd = flip(_gated_delta_chunked(flip(q), flip(k), flip(v),
                                      jnp.flip(g[1], axis=-1), jnp.flip(beta[1], axis=-1)))
    o = (o_fwd + o_bwd).transpose(0, 2, 1, 3)
    o = _rms_norm(o, norm_w) * jax.nn.silu(z.reshape(B, T, LA_HEADS, HEAD_DIM).astype(f32))
    return o.reshape(B, T, LA_DIM).astype(qkv.dtype) @ w_out_a


def _axial_rope_tables(T, dtype):
    rows = T // GRID_W
    row = jnp.repeat(jnp.arange(rows, dtype=jnp.float32), GRID_W)
    col = jnp.tile(jnp.arange(GRID_W, dtype=jnp.float32), rows)
    half = HEAD_DIM // 2
    inv_freq = ROPE_THETA ** (-jnp.arange(0, half, 2, dtype=jnp.float32) / half)
    ang_r = (row[:, None] * inv_freq)[:, None, :]
    ang_c = (col[:, None] * inv_freq)[:, None, :]
    return (jnp.cos(ang_r).astype(dtype), jnp.sin(ang_r).astype(dtype),
            jnp.cos(ang_c).astype(dtype), jnp.sin(ang_c).astype(dtype))


def _rotate(x, cos, sin):
    x1, x2 = jnp.split(x, 2, axis=-1)
    return jnp.concatenate([x1 * cos - x2 * sin, x2 * cos + x1 * sin], axis=-1)


def _axial_rope(x, cos_r, sin_r, cos_c, sin_c):
    x_row, x_col = jnp.split(x, 2, axis=-1)
    return jnp.concatenate([_rotate(x_row, cos_r, sin_r), _rotate(x_col, cos_c, sin_c)], axis=-1)


def _block_attention(q, k, v):
    B, T = q.shape[:2]
    nb = T // Q_BLOCK
    qb = q.reshape(B, nb, Q_BLOCK, KV_HEADS, GQA_GROUP, HEAD_DIM).transpose(1, 0, 2, 3, 4, 5)
    scale = HEAD_DIM ** -0.5

    def one_block(q_blk):
        s = jnp.einsum("bqkgd,btkd->bkgqt", q_blk, k).astype(jnp.float32) * scale
        p = jax.nn.softmax(s, axis=-1).astype(v.dtype)
        return jnp.einsum("bkgqt,btkd->bqkgd", p, v)

    o = lax.map(one_block, qb)
    return o.transpose(1, 0, 2, 3, 4, 5).reshape(B, T, ATT_Q_DIM)


def _mixer_b(q, k, v, q_norm_w, k_norm_w, w_out_b):
    B, T, _ = q.shape
    q = _rms_norm(q.reshape(B, T, ATT_HEADS, HEAD_DIM), q_norm_w)
    k = _rms_norm(k.reshape(B, T, KV_HEADS, HEAD_DIM), k_norm_w)
    v = v.reshape(B, T, KV_HEADS, HEAD_DIM)
    tables = _axial_rope_tables(T, q.dtype)
    q = _axial_rope(q, *tables).reshape(B, T, KV_HEADS, GQA_GROUP, HEAD_DIM)
    k = _axial_rope(k, *tables)
    return _block_attention(q, k, v) @ w_out_b


def _memory_attention(h, mem_n, w_mq, w_mkv, w_mo):
    B, T, _ = h.shape
    M = mem_n.shape[1]
    q = (h @ w_mq).reshape(B, T, MEM_HEADS, MEM_HEAD_DIM)
    k, v = jnp.split(mem_n @ w_mkv, 2, axis=-1)
    k = k.reshape(B, M, MEM_HEADS, MEM_HEAD_DIM)
    v = v.reshape(B, M, MEM_HEADS, MEM_HEAD_DIM)
    s = jnp.einsum("bthd,bmhd->bhtm", q, k).astype(jnp.float32) * MEM_HEAD_DIM ** -0.5
    p = jax.nn.softmax(s, axis=-1).astype(v.dtype)
    o = jnp.einsum("bhtm,bmhd->bthd", p, v).reshape(B, T, D_MODEL)
    return o @ w_mo


def _swiglu(h, w_in, w_out):
    gate, up = jnp.split(h @ w_in, 2, axis=-1)
    return (jax.nn.silu(gate) * up) @ w_out


def setup_inputs(seed: int = 0) -> dict:
    key = jax.random.key(seed)
    ks = iter(jax.random.split(key, 40))
    f32 = jnp.float32

    def dense(fan_in, fan_out):
        return jax.random.normal(next(ks), (DEPTH, fan_in, fan_out), f32) * fan_in ** -0.5

    def gain(n):
        return 1.0 + 0.02 * jax.random.normal(next(ks), (DEPTH, n), f32)

    x = jax.random.normal(next(ks), (BATCH, SEQ, D_MODEL), f32)
    mem = jax.random.normal(next(ks), (BATCH, MEM_TOKENS, D_MODEL), f32)
    mix_pre_norm = gain(D_MODEL)
    w_in = dense(D_MODEL, IN_DIM)
    conv_w = jax.random.normal(next(ks), (DEPTH, CONV_K, 3 * LA_DIM), f32) * CONV_K ** -0.5
    la_a_log = jnp.log(jax.random.uniform(next(ks), (DEPTH, 2, LA_HEADS), f32, 1.0, 16.0))
    dt = jnp.exp(jax.random.uniform(next(ks), (DEPTH, 2, LA_HEADS), f32,
                                    float(np.log(1e-3)), float(np.log(1e-1))))
    la_dt_bias = dt + jnp.log(-jnp.expm1(-dt))
    la_norm_w = gain(HEAD_DIM)
    w_out_a = dense(LA_DIM, D_MODEL)
    q_norm_w = gain(HEAD_DIM)
    k_norm_w = gain(HEAD_DIM)
    w_out_b = dense(ATT_Q_DIM, D_MODEL)
    b_gate = 0.01 * jax.random.normal(next(ks), (DEPTH, 2 * D_MODEL), f32)
    w_out = dense(D_MODEL, D_MODEL)
    mix_post_norm = gain(D_MODEL)
    mem_pre_norm = gain(D_MODEL)
    mem_kv_norm = gain(D_MODEL)
    w_mq = dense(D_MODEL, D_MODEL)
    w_mkv = dense(D_MODEL, 2 * D_MODEL)
    w_mo = dense(D_MODEL, D_MODEL)
    mem_post_norm = gain(D_MODEL)
    ffn_pre_norm = gain(D_MODEL)
    w_ffn_in = dense(D_MODEL, 2 * D_FF)
    w_ffn_out = dense(D_FF, D_MODEL)
    ffn_post_norm = gain(D_MODEL)
    return {"x": x, "mem": mem,
            "mix_pre_norm": mix_pre_norm, "w_in": w_in, "conv_w": conv_w,
            "la_a_log": la_a_log, "la_dt_bias": la_dt_bias, "la_norm_w": la_norm_w,
            "w_out_a": w_out_a, "q_norm_w": q_norm_w, "k_norm_w": k_norm_w,
            "w_out_b": w_out_b, "b_gate": b_gate, "w_out": w_out, "mix_post_norm": mix_post_norm,
            "mem_pre_norm": mem_pre_norm, "mem_kv_norm": mem_kv_norm, "w_mq": w_mq,
            "w_mkv": w_mkv, "w_mo": w_mo, "mem_post_norm": mem_post_norm,
            "ffn_pre_norm": ffn_pre_norm, "w_ffn_in": w_ffn_in, "w_ffn_out": w_ffn_out,
            "ffn_post_norm": ffn_post_norm}


def reference(x, mem, mix_pre_norm, w_in, conv_w, la_a_log, la_dt_bias, la_norm_w, w_out_a,
              q_norm_w, k_norm_w, w_out_b, b_gate, w_out, mix_post_norm,
              mem_pre_norm, mem_kv_norm, w_mq, w_mkv, w_mo, mem_post_norm,
              ffn_pre_norm, w_ffn_in, w_ffn_out, ffn_post_norm):
    for l in range(DEPTH):
        h = _rms_norm(x, mix_pre_norm[l])
        proj = h @ w_in[l]
        qkv_a, z_a, a_raw, b_raw, q_b, k_b, v_b, gates = _split_cols(proj, IN_SPLITS)
        y_a = _mixer_a(qkv_a, z_a, a_raw, b_raw, conv_w[l], la_a_log[l], la_dt_bias[l],
                       la_norm_w[l], w_out_a[l])
        y_b = _mixer_b(q_b, k_b, v_b, q_norm_w[l], k_norm_w[l], w_out_b[l])
        g_a, g_b = jnp.split(jax.nn.sigmoid(gates + b_gate[l]), 2, axis=-1)
        mixed = (g_a * y_a + g_b * y_b) @ w_out[l]
        x = x + _rms_norm(mixed, mix_post_norm[l])
        h = _rms_norm(x, mem_pre_norm[l])
        mem_n = _rms_norm(mem, mem_kv_norm[l])
        x = x + _rms_norm(_memory_attention(h, mem_n, w_mq[l], w_mkv[l], w_mo[l]), mem_post_norm[l])
        h = _rms_norm(x, ffn_pre_norm[l])
        x = x + _rms_norm(_swiglu(h, w_ffn_in[l], w_ffn_out[l]), ffn_post_norm[l])
    return x
```

```python
import math
from contextlib import ExitStack
import numpy as np
import ml_dtypes
import concourse.bass as bass
import concourse.mybir as mybir
from concourse.alu_op_type import AluOpType as ALU
from concourse.bass_utils import run_bass_kernel_spmd

F32 = mybir.dt.float32
BF16 = mybir.dt.bfloat16
AF = mybir.ActivationFunctionType

D = 1024
KC = 8
HD = 128
NH = 8
DFF = 2816
FC = DFF // 128
MEM = 256
IN_DIM = 7712
C_QKV, C_Z, C_AB, C_QB, C_KB, C_VB, C_G = 0, 3072, 4096, 4128, 5152, 5408, 5664
EPS = 1e-6
NEG = -1.0e5
N_CORES = 8


class Buf:
    __slots__ = ("name", "writer", "readers", "dsem", "dcount")

    def __init__(self, name):
        self.name = name
        self.writer = None
        self.readers = {}
        self.dsem = None
        self.dcount = 0


class Sched:
    SAME_ENGINE_SYNC = True

    def __init__(self, nc, es):
        self.nc = nc
        self.es = es
        self.engs = {"pe": nc.tensor, "act": nc.scalar, "dve": nc.vector,
                     "pool": nc.gpsimd, "sp": nc.sync}
        self.sem = {}
        self.count = {}
        for e in self.engs:
            self.sem[e] = es.enter_context(nc.semaphore("sem_" + e))
            self.count[e] = 0
        self.pending = {e: False for e in self.engs}
        self.waited = {e: {} for e in self.engs}
        self.n_inst = 0
        self.n_wait = 0
        self.out_tokens = []
        self.dsems = {}

    def _wait(self, e, tok):
        key, sem, val, src = tok
        if src == e and (e == "pe" or e == "sp" or not self.SAME_ENGINE_SYNC):
            return
        w = self.waited[e]
        if w.get(key, 0) >= val:
            return
        self.engs[e].wait_ge(sem, val)
        self.n_wait += 1
        w[key] = val

    def _deps(self, e, reads, writes):
        for b in reads:
            if b.writer is not None:
                self._wait(e, b.writer)
        for b in writes:
            if b.writer is not None:
                self._wait(e, b.writer)
            for tok in b.readers.values():
                self._wait(e, tok)

    def _commit(self, tok, reads, writes):
        for b in reads:
            b.readers[tok[0]] = tok
        for b in writes:
            b.writer = tok
            b.readers = {}

    def op(self, e, fn, reads=(), writes=(), inc=True):
        self._deps(e, reads, writes)
        inst = fn(self.engs[e])
        if inc:
            self.count[e] += 1
            inst.then_inc(self.sem[e], 1)
            tok = (e, self.sem[e], self.count[e], e)
            self.pending[e] = False
        else:
            tok = (e, self.sem[e], self.count[e] + 1, e)
            self.pending[e] = True
        self._commit(tok, reads, writes)
        self.n_inst += 1
        return inst

    def dma(self, q, out_ap, in_ap, reads=(), writes=(), sembuf=None, is_output=False, **kw):
        self._deps(q, reads, writes)
        b = sembuf if sembuf is not None else (writes[0] if writes else reads[0])
        if b.dsem is None:
            b.dsem = self.es.enter_context(self.nc.semaphore("d_" + b.name))
            self.dsems[b.name] = b
        b.dcount += 16
        inst = self.engs[q].dma_start(out=out_ap, in_=in_ap, **kw)
        inst.then_inc(b.dsem, 16)
        tok = ("d_" + b.name, b.dsem, b.dcount, None)
        self._commit(tok, reads, writes)
        if is_output:
            self.out_tokens.append(tok)
        self.n_inst += 1
        return inst

    def barrier(self):
        assert not any(self.pending.values())
        toks = [(e, self.sem[e], self.count[e], e) for e in self.engs if self.count[e] > 0]
        for b in self.dsems.values():
            if not b.name.startswith("wb_"):
                toks.append(("d_" + b.name, b.dsem, b.dcount, None))
        for e in self.engs:
            for tok in toks:
                if tok[3] != e:
                    self._wait(e, tok)

    def finish(self):
        last = {}
        for tok in self.out_tokens:
            last[tok[0]] = tok
        for tok in last.values():
            self._wait("sp", tok)


def _cpack_layout(T):
    ent = [("ident", 128), ("ones", 128), ("ucum", 128), ("lcum", 128),
           ("maskf", 128), ("maskb", 128),
           ("cosr", T // 64), ("sinr", T // 64), ("cosc", 64), ("sinc", 64),
           ("wpre_mix", 8), ("wpre_mem", 8), ("wpre_ffn", 8), ("wpre_kv", 8),
           ("convw", 24 * 5), ("lanorm", 1), ("qnorm", 1), ("knorm", 1), ("bgate", 16),
           ("alog", 16), ("dtb", 16)]
    off, o = {}, 0
    for n, c in ent:
        off[n] = (o, c)
        o += c
    return off, o


def _make_cpack(p, T):
    off, ncol = _cpack_layout(T)
    cp = np.zeros((128, ncol), np.float32)

    def put(name, arr):
        o, c = off[name]
        cp[:, o:o + c] = np.asarray(arr, np.float32).reshape(128, c)

    idx = np.arange(128)
    put("ident", np.eye(128))
    put("ones", np.ones((128, 128)))
    partner = np.where((idx % 64) < 32, idx + 32, idx - 32)
    pm = np.zeros((128, 128), np.float32)
    pm[partner, idx] = 1.0
    pp, ff = idx[:, None], idx[None, :]
    put("ucum", (pp <= ff))
    put("lcum", (pp >= ff))
    put("maskf", np.where(ff >= pp, 0.0, NEG))
    put("maskb", np.where(ff <= pp, 0.0, NEG))
    cbf = np.concatenate([np.eye(128), np.ones((128, 128)), pm, np.tile(np.eye(128), (1, 4)),
                          np.tile((ff > pp).astype(np.float32), (1, 4)),
                          np.tile((ff < pp).astype(np.float32), (1, 4))], axis=1).astype(ml_dtypes.bfloat16)
    half = HD // 2
    inv_freq = (np.float32(10000.0) ** (-np.arange(0, half, 2, dtype=np.float32) / np.float32(half))).astype(np.float32)
    rows = np.arange(T // 64, dtype=np.float32)
    cols = np.arange(64, dtype=np.float32)
    cosr = np.zeros((128, T // 64), np.float32); sinr = np.zeros_like(cosr)
    cosc = np.zeros((128, 64), np.float32); sinc = np.zeros_like(cosc)
    for d in range(64):
        i = d % 32
        sg = -1.0 if d < 32 else 1.0
        ang_r = (rows * inv_freq[i]).astype(np.float32)
        ang_c = (cols * inv_freq[i]).astype(np.float32)
        cosr[d] = np.cos(ang_r); sinr[d] = sg * np.sin(ang_r)
        cosc[64 + d] = np.cos(ang_c); sinc[64 + d] = sg * np.sin(ang_c)
    put("cosr", cosr); put("sinr", sinr); put("cosc", cosc); put("sinc", sinc)
    fm = lambda v: np.asarray(v, np.float32).reshape(-1, 128).T
    put("wpre_mix", fm(p["mix_pre_norm"][0]))
    put("wpre_mem", fm(p["mem_pre_norm"][0]))
    put("wpre_ffn", fm(p["ffn_pre_norm"][0]))
    put("wpre_kv", fm(p["mem_kv_norm"][0]))
    cw = np.asarray(p["conv_w"][0], np.float32)
    put("convw", cw.reshape(5, 24, 128).transpose(2, 1, 0))
    put("lanorm", np.asarray(p["la_norm_w"][0]).reshape(128, 1))
    put("qnorm", np.asarray(p["q_norm_w"][0]).reshape(128, 1))
    put("knorm", np.asarray(p["k_norm_w"][0]).reshape(128, 1))
    put("bgate", fm(p["b_gate"][0]))
    put("alog", np.tile(np.asarray(p["la_a_log"][0], np.float32).reshape(1, 16), (128, 1)))
    put("dtb", np.tile(np.asarray(p["la_dt_bias"][0], np.float32).reshape(1, 16), (128, 1)))
    return cp, np.ascontiguousarray(cbf)


W_SPECS = [("w_in", D, IN_DIM), ("w_out_a", D, D), ("w_out_b", D, D), ("w_out", D, D),
           ("w_mq", D, D), ("w_mkv", D, 2 * D), ("w_mo", D, D),
           ("w_ffn_in", D, 2 * DFF), ("w_ffn_out", DFF, D)]


class Kern:
    def __init__(self, NSEQ, T, dbg=False, inv32=False):
        self.NSEQ, self.T, self.dbg, self.inv32 = NSEQ, T, dbg, inv32
        self.TT = T // 128
        self.BLK = min(512, T)
        self.NB = T // self.BLK
        self.TPB = self.BLK // 128
        self.nc = nc = bass.Bass("TRN2", target_bir_lowering=False)
        self.es = es = ExitStack()
        self.S = Sched(nc, es)
        self.off, self.ncol = _cpack_layout(T)
        dt = nc.dram_tensor
        self.x_d = dt("x", [NSEQ, T, D], F32, kind="ExternalInput").ap()
        self.mem_d = dt("mem", [NSEQ, MEM, D], F32, kind="ExternalInput").ap()
        self.cp_d = dt("cpack", [128, self.ncol], F32, kind="ExternalInput").ap()
        self.pnw_d = dt("pnw", [3, 128, D], F32, kind="ExternalInput").ap()
        self.cbf_d = dt("cbf", [128, 1920], BF16, kind="ExternalInput").ap()
        self.out_d = dt("out", [NSEQ, T, D], F32, kind="ExternalOutput").ap()
        self.w_d, self.wb_d, self.wb_buf = {}, {}, {}
        for n, r, c in W_SPECS:
            self.w_d[n] = dt(n, [r, c], F32, kind="ExternalInput").ap()
            self.wb_d[n] = dt(n + "_bf", [r, c], BF16, kind="Internal").ap()
            self.wb_buf[n] = Buf("wb_" + n)
        if dbg:
            self.dbg_oa = dt("dbg_oa", [128, 8, T], F32, kind="ExternalOutput").ap()
            self.dbg_ob = dt("dbg_ob", [128, 8, T], F32, kind="ExternalOutput").ap()
            self.dbg_x1 = dt("dbg_x1", [T, D], F32, kind="ExternalOutput").ap()
            self.dbg_x2 = dt("dbg_x2", [T, D], F32, kind="ExternalOutput").ap()
        self.ARENA = 53200
        ARENA_OFF = 16544
        self.R32W = 12 * self.BLK + 128
        self.LIM = self.ARENA - self.R32W
        self.arena = nc.alloc_sbuf_tensor_at("arena", [128, self.LIM], F32, offset=ARENA_OFF)
        self.tailf = nc.alloc_sbuf_tensor_at("tailf", [128, self.R32W], F32, offset=ARENA_OFF + 4 * self.LIM)
        self.r32 = nc.alloc_sbuf_tensor_at("r32", [128, self.R32W], mybir.dt.float32r,
                                           offset=ARENA_OFF + 4 * (self.ARENA - self.R32W))[:]
        self.top = 0
        self.topB = self.LIM
        self.no_tail = True
        self.ps = [es.enter_context(nc.psum_tensor("ps%d" % i, [128, 512], F32)) for i in range(8)]
        self.psb = [Buf("ps%d" % i) for i in range(8)]
        self.ps_rr = 0
        self.ps_base, self.ps_n = 2, 6
        self.uid = 0

    def f32(self, n):
        if self.top + n <= self.LIM:
            a = self.arena[:, self.top:self.top + n]
            self.top += n
            return a
        assert not self.no_tail and self.topB + n <= self.ARENA, ("SBUF arena overflow", self.top, self.topB, n)
        a = self.tailf[:, self.topB - self.LIM:self.topB - self.LIM + n]
        self.topB += n
        return a

    def bf(self, n):
        assert n % 2 == 0
        return self.f32(n // 2).bitcast(BF16)

    def buf(self, name):
        self.uid += 1
        return Buf("%s_%d" % (name, self.uid))

    def pst(self):
        i = self.ps_base + (self.ps_rr % self.ps_n)
        self.ps_rr += 1
        return self.ps[i][:], self.psb[i]

    def cc(self, name):
        o, c = self.off[name]
        return self.cp[:, o:o + c]

    def mm(self, out, lhsT, rhs, start, stop, R, W, inc=True):
        self.S.op("pe", lambda e: e.matmul(out, lhsT=lhsT, rhs=rhs, start=start, stop=stop),
                  reads=R, writes=W, inc=inc)

    def act(self, out, in_, func, R, W, **kw):
        self.S.op("act", lambda e: e.activation(out=out, in_=in_, func=func, **kw), reads=R, writes=W)

    def tt(self, eng, out, in0, in1, op, R, W):
        self.S.op(eng, lambda e: e.tensor_tensor(out=out, in0=in0, in1=in1, op=op), reads=R, writes=W)

    def stt(self, out, in0, scalar, in1, op0, op1, R, W):
        self.S.op("dve", lambda e: e.scalar_tensor_tensor(out=out, in0=in0, scalar=scalar, in1=in1,
                                                           op0=op0, op1=op1), reads=R, writes=W)

    def cpy(self, eng, out, in_, R, W):
        if eng == "act":
            self.act(out, in_, AF.Copy, R, W)
        else:
            self.S.op(eng, lambda e: e.tensor_copy(out=out, in_=in_), reads=R, writes=W)

    def rsqrt_act(self, out, in_, scale, R, W, tmp, tmpb, post_bias=0.0):
        self.act(tmp, in_, AF.Ln, R, [tmpb], scale=scale, bias=self.epsc)
        if post_bias == 0.0:
            self.act(out, tmp, AF.Exp, [tmpb], W, scale=-0.5)
        else:
            self.act(out, tmp, AF.Exp, [tmpb], W, scale=-0.5, bias=self.qsc)

    def setup(self):
        S = self.S
        self.cp = self.f32(self.ncol)
        self.cpb = Buf("cpack")
        S.dma("sp", self.cp, self.cp_d[:, :], writes=[self.cpb])
        for n, r, c in W_SPECS:
            for c0 in range(0, c, 2048):
                c1 = min(c, c0 + 2048)
                S.dma("pool", self.wb_d[n][:, c0:c1], self.w_d[n][:, c0:c1],
                      writes=[self.wb_buf[n]], sembuf=self.wb_buf[n])
        cbf = self.bf(1920)
        self.cbb = Buf("constbf")
        S.dma("sp", cbf, self.cbf_d[:, :], writes=[self.cbb])
        self.ident_bf = cbf[:, 0:128]; self.ones_bf = cbf[:, 128:256]; self.perm_bf = cbf[:, 256:384]
        self.ident4_bf = cbf[:, 384:896]; self.strf_bf = cbf[:, 896:1408]; self.strb_bf = cbf[:, 1408:1920]
        cb, R = self.cbb, [self.cpb]
        sm = self.f32(24)
        self.epsc = sm[:, 0:1]; self.qsc = sm[:, 1:2]; self.nega = sm[:, 8:24]
        S.op("dve", lambda e: e.memset(self.epsc, EPS), writes=[cb])
        S.op("dve", lambda e: e.memset(self.qsc, math.log(HD ** -0.5)), writes=[cb])
        self.act(self.nega, self.cc("alog"), AF.Exp, R, [cb])
        S.op("dve", lambda e: e.tensor_scalar(out=self.nega, in0=self.nega, scalar1=-1.0, scalar2=None,
                                              op0=ALU.mult), reads=[cb], writes=[cb])
        self.CB = [self.cpb, self.cbb]
        T = self.T
        self.hT = self.bf(KC * T).rearrange("p (k t) -> p k t", k=KC)
        self.hTb = [Buf("hT%d" % i) for i in range(self.TT)]
        self.R1 = self.f32(8 * T)
        self.oaT = self.R1[:, 0:4 * T].bitcast(BF16).rearrange("p (k t) -> p k t", k=8)
        self.obT = self.R1[:, 4 * T:8 * T].bitcast(BF16).rearrange("p (k t) -> p k t", k=8)
        self.xTM = self.R1[:, 0:8 * T].rearrange("p (i c) -> p i c", c=D)
        self.oab = [[Buf("oa%d_%d" % (h, b)) for b in range(self.NB)] for h in range(8)]
        self.obb = [[Buf("ob%d_%d" % (h, b)) for b in range(self.NB)] for h in range(8)]
        self.xb = [Buf("x%d" % i) for i in range(self.TT)]
        n = 2 * self.TT * 8
        v4 = lambda a: a.rearrange("p (d i h) -> p d i h", d=2, i=self.TT)
        self.gam = v4(self.f32(n)); self.ngam = v4(self.f32(n)); self.eg = v4(self.f32(n))
        self.ekd = v4(self.f32(n)); self.dl = v4(self.f32(n)); self.beta = v4(self.f32(n))
        self.nbeta = v4(self.f32(n))
        self.decb = Buf("decay")
        self.NW = 4
        self.wslot = [self.bf(1024).rearrange("p (k n) -> p k n", k=8) for _ in range(self.NW)]
        self.wslotb = [Buf("ws%d" % i) for i in range(self.NW)]
        self.ws_rr = 0
        self.st = [self.f32(8) for _ in range(2)]
        self.stb = [Buf("st%d" % i) for i in range(2)]
        self.nrr = 0
        self.mark = self.top

    def alloc_norm_tmps(self, need_xin):
        if need_xin:
            self.xin = [self.f32(D) for _ in range(2)]
            self.xinb = [self.buf("xin") for i in range(2)]
        self.xn = [self.bf(D) for _ in range(2)]
        self.xnb = [self.buf("xn") for i in range(2)]
        self.junk = self.bf(D); self.junkb = self.buf("junk")
        self.ptmp = [self.f32(512) for _ in range(2)]
        self.ptmpb = [self.buf("ptmp") for _ in range(2)]

    def newphase(self, no_tail=False):
        self.S.barrier()
        self.top = self.mark
        self.topB = self.LIM
        self.no_tail = no_tail

    def wload(self, name, c0, ncols=128):
        i = self.ws_rr
        self.ws_rr = (self.ws_rr + 1) % self.NW
        src = self.wb_d[name].rearrange("(k p) n -> p k n", p=128)[:, :, c0:c0 + ncols]
        dst = self.wslot[i][:, :, 0:ncols]
        self.S.dma("sp", dst, src, reads=[self.wb_buf[name]], writes=[self.wslotb[i]])
        return dst, self.wslotb[i]

    def prefetch(self, items, LA=4):
        loaded = {}
        nxt = 0
        for i in range(len(items)):
            while nxt < min(len(items), i + LA):
                loaded[nxt] = self.wload(*items[nxt])
                nxt += 1
            ap, b = loaded.pop(i)
            yield i, ap, b

    def norm_tile(self, src, srcb, wcol, dst3, dstb, tcol):
        r = self.nrr
        self.nrr ^= 1
        st, stb, xn, xnb = self.st[r], self.stb[r], self.xn[r], self.xnb[r]
        self.act(self.junk, src, AF.Square, [srcb], [self.junkb, stb], accum_out=st[:, 0:1])
        self.act(st[:, 1:2], st[:, 0:1], AF.Ln, [stb] + self.CB, [stb], scale=1.0 / D, bias=self.epsc)
        self.act(st[:, 2:3], st[:, 1:2], AF.Exp, [stb], [stb], scale=-0.5)
        self.act(xn, src, AF.Copy, [srcb, stb], [xnb], scale=st[:, 2:3])
        for hf in range(2):
            pa, pb = self.pst()
            pv = pa.rearrange("p (k t) -> p k t", k=4)
            for k in range(4):
                kc = hf * 4 + k
                self.mm(pv[:, k, :], xn[:, kc * 128:(kc + 1) * 128], self.ident_bf, True, True,
                        [xnb] + self.CB, [pb])
            self.tt("dve", dst3[:, hf * 4:hf * 4 + 4, tcol:tcol + 128], pv,
                    wcol[:, hf * 4:hf * 4 + 4].unsqueeze(2).to_broadcast([128, 4, 128]), ALU.mult,
                    [pb] + self.CB, [dstb])

    def norm_from_dram(self, s, wname):
        for i in range(self.TT):
            r = i % 2
            self.S.dma("sp", self.xin[r], self.x_d[s, i * 128:(i + 1) * 128, :], writes=[self.xinb[r]])
            self.norm_tile(self.xin[r], self.xinb[r], self.cc(wname), self.hT, self.hTb[i], i * 128)

    def norm_from_x(self, wname):
        for i in range(self.TT):
            self.norm_tile(self.xTM[:, i, :], self.xb[i], self.cc(wname), self.hT, self.hTb[i], i * 128)

    def hblk(self, b):
        return [self.hTb[i] for i in range(b * self.TPB, (b + 1) * self.TPB)]

    def bsl(self, b):
        return slice(b * self.BLK, (b + 1) * self.BLK)

    def proj_fm(self, w, wb, rhs3, rhsb, b, n=None):
        pa, pb = self.pst()
        n = self.BLK if n is None else n
        o = pa[:, 0:n]
        nk = w.shape[1]
        for k in range(nk):
            self.mm(o, w[:, k, :], rhs3[:, k, b * self.BLK:b * self.BLK + n], k == 0, k == nk - 1,
                    [wb] + rhsb, [pb], inc=(k == nk - 1))
        return o, pb

    def phase_decay(self):
        S, TT = self.S, self.TT
        wab = self.bf(8 * 32).rearrange("p (k n) -> p k n", k=8)
        wabb = self.buf("wab")
        S.dma("sp", wab, self.wb_d["w_in"].rearrange("(k p) n -> p k n", p=128)[:, :, C_AB:C_AB + 32],
              reads=[self.wb_buf["w_in"]], writes=[wabb])
        n16 = TT * 16
        xa = self.f32(n16); ax = self.f32(n16); g = self.f32(n16); bt = self.f32(n16)
        tb = self.buf("dtmp")
        v3 = lambda a: a.rearrange("p (i c) -> p i c", c=16)
        pa, pb = self.pst()
        abp = pa[:, 0:TT * 32].rearrange("p (i c) -> p i c", c=32)
        for i in range(TT):
            for k in range(KC):
                self.mm(abp[:, i, :], self.hT[:, k, i * 128:(i + 1) * 128], wab[:, k, :], k == 0, k == KC - 1,
                        [self.hTb[i], wabb], [pb], inc=(k == KC - 1))
        dtb = self.cc("dtb").unsqueeze(1).to_broadcast([128, TT, 16])
        negab = self.nega.unsqueeze(1).to_broadcast([128, TT, 16])
        self.tt("dve", v3(xa), abp[:, :, 0:16], dtb, ALU.add, [pb] + self.CB, [tb])
        S.op("dve", lambda e: e.tensor_scalar(out=ax, in0=xa, scalar1=-1.0, scalar2=None, op0=ALU.mult),
             reads=[tb], writes=[tb])
        self.tt("dve", ax, ax, xa, ALU.min, [tb], [tb])
        self.act(ax, ax, AF.Exp, [tb], [tb])
        self.act(ax, ax, AF.Ln, [tb], [tb], bias=1.0)
        self.stt(xa, xa, 0.0, ax, ALU.max, ALU.add, [tb], [tb])
        self.tt("dve", v3(g), v3(xa), negab, ALU.mult, [tb] + self.CB, [tb])
        self.act(v3(bt), abp[:, :, 16:32], AF.Sigmoid, [pb], [tb])
        g3, b3 = v3(g), v3(bt)
        pg, pgb = self.pst()
        n8 = TT * 8
        gps = pg[:, 0:2 * n8].rearrange("p (d i h) -> p d i h", d=2, i=TT)
        tps = pg[:, 2 * n8:4 * n8].rearrange("p (d i h) -> p d i h", d=2, i=TT)
        for d in range(2):
            cum = self.cc("ucum") if d == 0 else self.cc("lcum")
            self.mm(gps[:, d], cum, g3[:, :, d * 8:(d + 1) * 8], True, True, [tb] + self.CB, [pgb])
            self.mm(tps[:, d], self.cc("ones"), g3[:, :, d * 8:(d + 1) * 8], True, True, [tb] + self.CB, [pgb])
        db = self.decb
        for d in range(2):
            self.cpy("dve", self.gam[:, d], gps[:, d], [pgb], [db])
            S.op("dve", lambda e, d=d: e.tensor_scalar(out=self.ngam[:, d], in0=gps[:, d], scalar1=-1.0,
                                                       scalar2=None, op0=ALU.mult), reads=[pgb], writes=[db])
            self.act(self.eg[:, d], gps[:, d], AF.Exp, [pgb], [db])
            self.tt("dve", self.ekd[:, d], tps[:, d], self.gam[:, d], ALU.subtract, [pgb, db], [db])
            self.act(self.ekd[:, d], self.ekd[:, d], AF.Exp, [db], [db])
            self.act(self.dl[:, d], tps[:, d], AF.Exp, [pgb], [db])
            self.cpy("dve", self.beta[:, d], b3[:, :, d * 8:(d + 1) * 8], [tb], [db])
            S.op("dve", lambda e, d=d: e.tensor_scalar(out=self.nbeta[:, d], in0=b3[:, :, d * 8:(d + 1) * 8],
                                                       scalar1=-1.0, scalar2=None, op0=ALU.mult),
                 reads=[tb], writes=[db])

    def mixer_a(self):
        S, T, TT, BLK, NB, TPB = self.S, self.T, self.TT, self.BLK, self.NB, self.TPB
        F32R = mybir.dt.float32r
        xtop = [4 * T]
        def xf32(n):
            a_ = self.R1[:, xtop[0]:xtop[0] + n]
            xtop[0] += n
            assert xtop[0] <= 8 * T
            return a_
        xbf = lambda n: xf32(n // 2).bitcast(BF16)
        pre = self.f32(T + 4); preb = self.buf("pre")
        oT = pre[:, 0:T]; oTb = [self.buf("oT") for _ in range(TT)]
        cv = self.f32(T); cvb = self.buf("cv")
        qT = self.bf(T); kT = self.bf(T); vF = self.bf(T)
        qTb = [self.buf("qT") for _ in range(NB)]; kTb = [self.buf("kT") for _ in range(NB)]
        vFb = self.buf("vF")
        r3 = lambda a_: a_.rearrange("p (i d) -> p i d", d=128)
        ub = [r3(cv), r3(xf32(T))]
        ubb = [[self.buf("ub") for _ in range(NB)] for _ in range(2)]
        qg = [vF, xbf(T)]; qgb = [[self.buf("qg") for _ in range(NB)] for _ in range(2)]
        kd = [r3(self.bf(T)), r3(xbf(T))]; kdb = [self.buf("kd") for _ in range(2)]
        wT = [self.bf(T), xbf(T)]; wTb = [[self.buf("wT") for _ in range(NB)] for _ in range(2)]
        qkE = [self.bf(T), xbf(T)]; qkEb = [[self.buf("qkE") for _ in range(NB)] for _ in range(2)]
        kTM = r3(xbf(T)); kTMb = self.buf("kTM")
        vTM = r3(self.bf(T)); vTMb = self.buf("vTM")
        kg = r3(xbf(T)); kgb = self.buf("kg")
        Sst = [self.f32(128) for _ in range(2)]; Sbf = [self.bf(128) for _ in range(2)]
        Sb = [self.buf("S") for _ in range(2)]; Sbfb = [self.buf("Sbf") for _ in range(2)]
        vnew = [[self.bf(128) for _ in range(2)] for _ in range(2)]
        vnewb = [[self.buf("vnew") for _ in range(2)] for _ in range(2)]
        E = [self.bf(BLK) for _ in range(2)]; Eb = [self.buf("E") for _ in range(2)]
        Es = [self.bf(BLK) for _ in range(2)]; Esb = [self.buf("Es") for _ in range(2)]
        NG = 2
        rtop = [0]
        def rbuf(n):
            a_ = self.r32[:, rtop[0]:rtop[0] + n]
            rtop[0] += n
            assert rtop[0] <= self.R32W
            return a_
        Bm = [[rbuf(BLK) for _ in range(NG)] for e_ in range(2)]
        BTm = [[rbuf(BLK) for _ in range(NG)] for _ in range(2)]
        Pm = [[rbuf(BLK) for _ in range(NG)] for _ in range(2)]
        Bmb = [[self.buf("B") for _ in range(NG)] for _ in range(2)]
        BTmb = [[self.buf("BT") for _ in range(NG)] for _ in range(2)]
        Pmb = [[self.buf("P") for _ in range(NG)] for _ in range(2)]
        P16 = [self.bf(BLK) for _ in range(2)]; P16b = [self.buf("P16") for _ in range(2)]
        identr = rbuf(128); identrb = self.buf("identr")
        tmp = [self.f32(BLK) for _ in range(3)]; tmpb = [self.buf("tmpA") for _ in range(3)]
        sqb16 = self.bf(BLK); sqb16b = self.buf("sq16")
        rr = lambda a_: a_
        S.op("pool", lambda e: e.memset(pre[:, 0:2], 0.0), writes=[preb])
        S.op("pool", lambda e: e.memset(pre[:, T + 2:T + 4], 0.0), writes=[preb])
        self.cpy("dve", rr(identr), self.cc("ident"), self.CB, [identrb])
        cw = self.cc("convw").rearrange("p (c j) -> p c j", j=5)
        CB = self.CB
        v4 = lambda a_: a_.rearrange("p (i c) -> p i c", c=128)
        NL = 6
        gf = NL % NG
        ident3 = self.cc("ident").unsqueeze(1).to_broadcast([128, TPB, 128])

        for h in range(NH):
            items = [("w_in", C_QKV + t * 1024 + h * 128) for t in range(3)] + [("w_in", C_Z + h * 128)]
            wl = list(self.prefetch(items, LA=4))
            for t in range(3):
                _, w, wb = wl[t]
                for b in range(NB):
                    o, pb = self.proj_fm(w, wb, self.hT, self.hblk(b), b)
                    self.cpy("act", pre[:, 2 + b * BLK:2 + (b + 1) * BLK], o, [pb], [preb] + oTb)
                cch = t * 8 + h
                S.op("dve", lambda e: e.tensor_scalar(out=cv, in0=pre[:, 0:T], scalar1=cw[:, cch, 0:1], scalar2=None,
                                                      op0=ALU.mult), reads=[preb] + CB, writes=[cvb] + ubb[0])
                for j in range(1, 5):
                    self.stt(cv, pre[:, j:j + T], cw[:, cch, j:j + 1], cv, ALU.mult, ALU.add, [preb, cvb] + CB, [cvb])
                if t == 2:
                    self.act(vF, cv, AF.Silu, [cvb], [vFb] + qgb[0])
                    continue
                self.act(cv, cv, AF.Silu, [cvb], [cvb])
                dst, dstb = (qT, qTb) if t == 0 else (kT, kTb)
                for b in range(NB):
                    sl = self.bsl(b)
                    self.act(sqb16, cv[:, sl], AF.Square, [cvb], [sqb16b])
                    pa, pb = self.pst()
                    self.mm(pa[:, 0:BLK], self.ones_bf, sqb16, True, True, [sqb16b] + CB, [pb])
                    self.rsqrt_act(tmp[1], pa[:, 0:BLK], 1.0, [pb] + CB, [tmpb[1]], tmp[0], tmpb[0],
                                   post_bias=(1.0 if t == 0 else 0.0))
                    self.tt("dve", dst[:, sl], cv[:, sl], tmp[1], ALU.mult, [cvb, tmpb[1]], [dstb[b]])
            for b in range(NB):
                for (src, srcb, dst3, dstb) in ((kT, [kTb[b]], kTM, kTMb), (vF, [vFb], vTM, vTMb)):
                    pa, pb = self.pst()
                    pv = v4(pa[:, 0:BLK])
                    for j in range(TPB):
                        i = b * TPB + j
                        self.mm(pv[:, j, :], src[:, i * 128:(i + 1) * 128], self.ident_bf, True, True,
                                srcb + CB, [pb])
                    self.cpy("act", dst3[:, b * TPB:(b + 1) * TPB, :], pv, [pb], [dstb])
            for d in range(2):
                mask = self.cc("maskf") if d == 0 else self.cc("maskb")
                strm = self.strf_bf if d == 0 else self.strb_bf
                colb = lambda a_, i: a_[:, d, i, h:h + 1]
                ubw = (lambda b: [ubb[0][b], cvb]) if d == 0 else (lambda b: [ubb[1][b]])
                qgw = (lambda b: [qgb[0][b], vFb]) if d == 0 else (lambda b: [qgb[1][b]])
                self.tt("pool", kg, kTM, self.eg[:, d, :, h:h + 1].to_broadcast([128, TT, 128]), ALU.mult,
                        [kTMb, self.decb], [kgb])
                self.tt("pool", kd[d], kTM, self.ekd[:, d, :, h:h + 1].to_broadcast([128, TT, 128]), ALU.mult,
                        [kTMb, self.decb], [kdb[d]])
                for b0 in range(0, NB, 2):
                    blks = list(range(b0, min(NB, b0 + 2)))
                    for b in blks:
                        e_i = b % 2
                        pa, pb = self.pst()
                        pq, pqb = self.pst()
                        pv, pqv = v4(pa[:, 0:BLK]), v4(pq[:, 0:BLK])
                        for j in range(TPB):
                            i = b * TPB + j
                            self.mm(pv[:, j, :], colb(self.gam, i).to_broadcast([128, 128]), self.cc("ident"),
                                    True, False, [self.decb] + CB, [pb], inc=False)
                            self.mm(pv[:, j, :], self.cc("ident"), mask, False, True, CB, [pb])
                            self.mm(pqv[:, j, :], colb(self.eg, i).to_broadcast([128, 128]), self.cc("ident"),
                                    True, True, [self.decb] + CB, [pqb])
                        for j in range(TPB):
                            i = b * TPB + j
                            self.act(E[e_i][:, j * 128:(j + 1) * 128], pv[:, j, :], AF.Exp, [pb, self.decb],
                                     [Eb[e_i]], bias=colb(self.ngam, i), scale=1.0)
                        self.tt("dve", qg[d][:, self.bsl(b)], pq[:, 0:BLK], qT[:, self.bsl(b)], ALU.mult,
                                [pqb, qTb[b]], qgw(b))
                        self.tt("pool", Es[e_i], E[e_i], strm[:, 0:BLK], ALU.mult, [Eb[e_i]] + CB, [Esb[e_i]])
                        pk, pkb = self.pst()
                        pkv = v4(pk[:, 0:BLK])
                        for j in range(TPB):
                            i = b * TPB + j
                            ks = kT[:, i * 128:(i + 1) * 128]
                            self.mm(pkv[:, j, :], ks, ks, True, True, [kTb[b]], [pkb])
                        for j in range(TPB):
                            i = b * TPB + j
                            self.stt(rr(BTm[e_i][0][:, j * 128:(j + 1) * 128]), pkv[:, j, :], colb(self.nbeta, i),
                                     Es[e_i][:, j * 128:(j + 1) * 128], ALU.mult, ALU.mult,
                                     [pkb, Esb[e_i], self.decb], [BTmb[e_i][0]])
                        pk2, pk2b = self.pst()
                        pk2v = v4(pk2[:, 0:BLK])
                        for j in range(TPB):
                            i = b * TPB + j
                            self.mm(pk2v[:, j, :], kT[:, i * 128:(i + 1) * 128], qT[:, i * 128:(i + 1) * 128],
                                    True, True, [kTb[b], qTb[b]], [pk2b])
                        self.tt("dve", qkE[d][:, self.bsl(b)], pk2[:, 0:BLK], E[e_i], ALU.mult, [pk2b, Eb[e_i]],
                                [qkEb[d][b]])
                        pt, ptb = self.pst()
                        ptv = v4(pt[:, 0:BLK])
                        for j in range(TPB):
                            self.mm(ptv[:, j, :], rr(BTm[e_i][0][:, j * 128:(j + 1) * 128]), rr(identr), True, True,
                                    [BTmb[e_i][0], identrb], [ptb])
                        self.cpy("act", rr(Bm[e_i][0]), pt[:, 0:BLK], [ptb], [Bmb[e_i][0]])
                        self.tt("pool", v4(Pm[e_i][0]), v4(BTm[e_i][0].bitcast(F32)), ident3, ALU.add,
                                [BTmb[e_i][0]] + CB, [Pmb[e_i][0]])
                    for lv in range(NL):
                        g0, g1 = lv % NG, (lv + 1) % NG
                        for b in blks:
                            e_i = b % 2
                            pB, pBb = self.pst()
                            pBv = v4(pB[:, 0:BLK])
                            for j in range(TPB):
                                sl = slice(j * 128, (j + 1) * 128)
                                self.mm(pBv[:, j, :], rr(BTm[e_i][g0][:, sl]), rr(Bm[e_i][g0][:, sl]), True, True,
                                        [BTmb[e_i][g0], Bmb[e_i][g0]], [pBb])
                            if lv < NL - 1:
                                pT, pTb = self.pst()
                                pTv = v4(pT[:, 0:BLK])
                                for j in range(TPB):
                                    sl = slice(j * 128, (j + 1) * 128)
                                    self.mm(pTv[:, j, :], rr(Bm[e_i][g0][:, sl]), rr(BTm[e_i][g0][:, sl]), True, True,
                                            [BTmb[e_i][g0], Bmb[e_i][g0]], [pTb])
                            self.cpy("act", rr(Bm[e_i][g1]), pB[:, 0:BLK], [pBb], [Bmb[e_i][g1]])
                            if lv < NL - 1:
                                self.cpy("dve", rr(BTm[e_i][g1]), pT[:, 0:BLK], [pTb], [BTmb[e_i][g1]])
                        for b in blks:
                            e_i = b % 2
                            pP, pPb = self.pst()
                            pPv = v4(pP[:, 0:BLK])
                            for j in range(TPB):
                                sl = slice(j * 128, (j + 1) * 128)
                                self.mm(pPv[:, j, :], rr(identr), rr(Pm[e_i][g0][:, sl]), True, False,
                                        [Pmb[e_i][g0], identrb], [pPb], inc=False)
                                self.mm(pPv[:, j, :], rr(Bm[e_i][g1][:, sl]), rr(Pm[e_i][g0][:, sl]), False, True,
                                        [Pmb[e_i][g0], Bmb[e_i][g1]], [pPb])
                            if lv < NL - 1:
                                self.cpy("act" if (b % 2 == 0) else "dve", rr(Pm[e_i][g1]), pP[:, 0:BLK], [pPb],
                                         [Pmb[e_i][g1]])
                            else:
                                self.cpy("act" if (b % 2 == 0) else "dve", P16[e_i], pP[:, 0:BLK], [pPb],
                                         [P16b[e_i]])
                    for b in blks:
                        e_i = b % 2
                        Pf, Pfb = P16[e_i], P16b[e_i]
                        pu, pub = self.pst()
                        puv = v4(pu[:, 0:BLK])
                        pw, pwb = self.pst()
                        pwv = v4(pw[:, 0:BLK])
                        for j in range(TPB):
                            i = b * TPB + j
                            sl = slice(j * 128, (j + 1) * 128)
                            self.mm(puv[:, j, :], Pf[:, sl], vTM[:, i, :], True, True, [Pfb, vTMb], [pub])
                            self.mm(pwv[:, j, :], kg[:, i, :], Pf[:, sl], True, True, [Pfb, kgb], [pwb])
                        self.tt("dve", ub[d][:, b * TPB:(b + 1) * TPB, :], puv,
                                self.beta[:, d, b * TPB:(b + 1) * TPB, h:h + 1].to_broadcast([128, TPB, 128]),
                                ALU.mult, [pub, self.decb], ubw(b))
                        self.cpy("act", wT[d][:, self.bsl(b)], pw[:, 0:BLK], [pwb], [wTb[d][b]])
            for d in range(2):
                S.op("pool", lambda e: e.memset(Sst[d], 0.0), writes=[Sb[d]])
                S.op("pool", lambda e: e.memset(Sbf[d], 0.0), writes=[Sbfb[d]])
            written = [False] * TT
            for n_i in range(TT):
                for d in range(2):
                    i = n_i if d == 0 else TT - 1 - n_i
                    b = i // TPB
                    sl = slice(i * 128, (i + 1) * 128)
                    vn, vnb = vnew[d][n_i % 2], vnewb[d][n_i % 2]
                    colb = lambda a_, i_: a_[:, d, i_, h:h + 1]
                    p1, p1b = self.pst()
                    self.mm(p1[:, 0:128], wT[d][:, sl], Sbf[d], True, True, [wTb[d][b], Sbfb[d]], [p1b])
                    self.stt(vn, p1[:, 0:128], colb(self.nbeta, i), ub[d][:, i, :], ALU.mult, ALU.add,
                             [p1b, ubb[d][b], self.decb] + ([cvb] if d == 0 else []), [vnb])
                    p3, p3b = self.pst()
                    self.mm(p3[:, 0:128], kd[d][:, i, :], vn, True, True, [kdb[d], vnb], [p3b])
                    p2, p2b = self.pst()
                    self.mm(p2[:, 0:128], Sbf[d], qg[d][:, sl], True, False, [Sbfb[d], qgb[d][b]] + ([vFb] if d == 0 else []),
                            [p2b], inc=False)
                    self.mm(p2[:, 0:128], vn, qkE[d][:, sl], False, True, [vnb, qkEb[d][b]], [p2b])
                    self.stt(Sst[d], Sst[d], colb(self.dl, i), p3[:, 0:128], ALU.mult, ALU.add,
                             [Sb[d], p3b, self.decb], [Sb[d]])
                    self.cpy("act", Sbf[d], Sst[d], [Sb[d]], [Sbfb[d]])
                    if not written[i]:
                        self.cpy("act", oT[:, sl], p2[:, 0:128], [p2b], [oTb[i], preb])
                        written[i] = True
                    else:
                        self.tt("dve", oT[:, sl], p2[:, 0:128], oT[:, sl], ALU.add, [p2b, oTb[i]], [oTb[i], preb])
            _, wz, wzb = wl[3]
            for b in range(NB):
                sl = self.bsl(b)
                ob_ = [oTb[i] for i in range(b * TPB, (b + 1) * TPB)]
                self.act(sqb16, oT[:, sl], AF.Square, ob_, [sqb16b])
                pa, pb = self.pst()
                self.mm(pa[:, 0:BLK], self.ones_bf, sqb16, True, True, [sqb16b] + CB, [pb])
                self.rsqrt_act(tmp[1], pa[:, 0:BLK], 1.0 / HD, [pb] + CB, [tmpb[1]], tmp[0], tmpb[0])
                o, pzb = self.proj_fm(wz, wzb, self.hT, self.hblk(b), b)
                self.act(tmp[2], o, AF.Silu, [pzb], [tmpb[2]])
                self.stt(tmp[1], oT[:, sl], self.cc("lanorm"), tmp[1], ALU.mult, ALU.mult,
                         ob_ + [tmpb[1]] + CB, [tmpb[1]])
                self.tt("pool", self.oaT[:, h, sl], tmp[1], tmp[2], ALU.mult, [tmpb[1], tmpb[2]], [self.oab[h][b]])

    def rope_norm(self, o, pb, wname, dst, dstb, b, tmp, tmpb, t16, t16b):
        BLK, CB = self.BLK, self.CB
        self.act(t16[0], o, AF.Square, [pb], [t16b[0]])
        pa, pab = self.pst()
        self.mm(pa[:, 0:BLK], self.ones_bf, t16[0], True, True, [t16b[0]] + CB, [pab])
        self.rsqrt_act(tmp[1], pa[:, 0:BLK], 1.0 / HD, [pab] + CB, [tmpb[1]], tmp[0], tmpb[0])
        self.stt(tmp[0], o, self.cc(wname), tmp[1], ALU.mult, ALU.mult, [pb, tmpb[1], tmpb[0]] + CB, [tmpb[0]])
        self.cpy("act", t16[1], tmp[0], [tmpb[0]], [t16b[1]])
        ps_, psb_ = self.pst()
        self.mm(ps_[:, 0:BLK], self.perm_bf, t16[1], True, True, [t16b[1]] + CB, [psb_])
        nr = BLK // 64
        r0 = b * nr
        cosr = self.cc("cosr")[0:64, r0:r0 + nr].unsqueeze(2).to_broadcast([64, nr, 64])
        sinr = self.cc("sinr")[0:64, r0:r0 + nr].unsqueeze(2).to_broadcast([64, nr, 64])
        cosc = self.cc("cosc")[64:128, :].unsqueeze(1).to_broadcast([64, nr, 64])
        sinc = self.cc("sinc")[64:128, :].unsqueeze(1).to_broadcast([64, nr, 64])
        r3 = lambda a, lo: a[lo:lo + 64, 0:BLK].rearrange("p (r c) -> p r c", c=64)
        self.tt("pool", r3(tmp[1], 0), r3(tmp[0], 0), cosr, ALU.mult, [tmpb[0], tmpb[1]] + CB, [tmpb[1]])
        self.tt("pool", r3(tmp[1], 64), r3(tmp[0], 64), cosc, ALU.mult, [tmpb[0], tmpb[1]] + CB, [tmpb[1]])
        self.tt("dve", r3(tmp[2], 0), r3(ps_, 0), sinr, ALU.mult, [psb_] + CB, [tmpb[2]])
        self.tt("dve", r3(tmp[2], 64), r3(ps_, 64), sinc, ALU.mult, [psb_] + CB, [tmpb[2]])
        self.tt("pool", dst, tmp[1], tmp[2], ALU.add, [tmpb[1], tmpb[2]], dstb)

    def mixer_b(self):
        S, T, TT, BLK, NB, TPB, CB = self.S, self.T, self.TT, self.BLK, self.NB, self.TPB, self.CB
        kTr = self.bf(2 * T).rearrange("p (g t) -> p g t", g=2)
        kTrb = [[self.buf("kTr") for _ in range(NB)] for _ in range(2)]
        vB = self.bf(TT * 256).rearrange("p (i c) -> p i c", c=256)
        vBb = self.buf("vB")
        wv = self.bf(8 * 256).rearrange("p (k n) -> p k n", k=8); wvb = self.buf("wv")
        qTr = [self.bf(BLK) for _ in range(2)]; qTrb = [self.buf("qTr") for _ in range(2)]
        Ej = [self.bf(BLK) for _ in range(3)]; Ejb = [self.buf("Ej") for _ in range(3)]
        tmp = [self.f32(BLK) for _ in range(3)]; tmpb = [self.buf("tmpB") for _ in range(3)]
        t16 = [self.bf(BLK) for _ in range(2)]; t16b = [self.buf("t16") for _ in range(2)]
        S.dma("sp", wv, self.wb_d["w_in"].rearrange("(k p) n -> p k n", p=128)[:, :, C_VB:C_VB + 256],
              reads=[self.wb_buf["w_in"]], writes=[wvb])
        items = [("w_in", C_KB + g * 128) for g in range(2)] + [("w_in", C_QB + hq * 128) for hq in range(8)]
        pf = self.prefetch(items, LA=3)
        for g in range(2):
            _, w, wb = next(pf)
            for b in range(NB):
                o, pb = self.proj_fm(w, wb, self.hT, self.hblk(b), b)
                self.rope_norm(o, pb, "knorm", kTr[:, g, self.bsl(b)], [kTrb[g][b]], b, tmp, tmpb, t16, t16b)
        for i in range(TT):
            pa, pb = self.pst()
            for k in range(KC):
                self.mm(pa[:, 0:256], self.hT[:, k, i * 128:(i + 1) * 128], wv[:, k, :], k == 0, k == KC - 1,
                        [self.hTb[i], wvb], [pb], inc=(k == KC - 1))
            self.cpy("act", vB[:, i, :], pa[:, 0:256], [pb], [vBb])
        self.ps_base, self.ps_n = 4, 4
        accs = [((self.ps[0][:], self.psb[0]), (self.ps[1][:], self.psb[1])),
                ((self.ps[2][:], self.psb[2]), (self.ps[3][:], self.psb[3]))]
        seq = [(hq, b) for hq in range(8) for b in range(NB)]
        wq = {}

        def prep(n):
            hq, b = seq[n]
            if b == 0:
                wq[hq] = next(pf)[1:]
            w, wb = wq[hq]
            o, pb = self.proj_fm(w, wb, self.hT, self.hblk(b), b)
            self.rope_norm(o, pb, "qnorm", qTr[n % 2], [qTrb[n % 2]], b, tmp, tmpb, t16, t16b)

        prep(0)
        for n in range(len(seq)):
            hq, b = seq[n]
            g = hq // 4
            qi = n % 2
            (acc0, acc0b), (acc1, acc1b) = accs[n % 2]

            def issue_sc(j):
                ei = j % 3
                sc, scb = self.pst()
                self.mm(sc[:, 0:BLK], kTr[:, g, j * 128:(j + 1) * 128], qTr[qi], True, True,
                        [kTrb[g][j // TPB], qTrb[qi]], [scb])
                self.act(Ej[ei], sc[:, 0:BLK], AF.Exp, [scb], [Ejb[ei]], scale=HD ** -0.5)

            issue_sc(0)
            for j in range(TT):
                if j + 1 < TT:
                    issue_sc(j + 1)
                ei = j % 3
                self.mm(acc0[:, 0:BLK], vB[:, j, g * 128:(g + 1) * 128], Ej[ei], j == 0, j == TT - 1,
                        [vBb, Ejb[ei]], [acc0b], inc=(j == TT - 1))
                self.mm(acc1[:, 0:BLK], self.ones_bf, Ej[ei], j == 0, j == TT - 1,
                        [Ejb[ei]] + CB, [acc1b])
            if n + 1 < len(seq):
                prep(n + 1)
            rt, rtb = tmp[n % 2], tmpb[n % 2]
            self.act(rt, acc1[:, 0:BLK], AF.Ln, [acc1b], [rtb])
            self.act(rt, rt, AF.Exp, [rtb], [rtb], scale=-1.0)
            self.tt("dve", self.obT[:, hq, self.bsl(b)], acc0[:, 0:BLK], rt, ALU.mult,
                    [acc0b, rtb], [self.obb[hq][b]])
        self.ps_base, self.ps_n = 2, 6

    def post_tile(self, z0, z0b, z1, z1b, pnw, pnwb, i, tmp, tmpb):
        r = self.nrr
        self.nrr ^= 1
        st, stb = self.st[r], self.stb[r]
        self.act(self.junk[:, 0:512], z0, AF.Square, [z0b], [self.junkb, stb], accum_out=st[:, 0:1])
        self.act(self.junk[:, 512:1024], z1, AF.Square, [z1b], [self.junkb, stb], accum_out=st[:, 1:2])
        self.tt("dve", st[:, 2:3], st[:, 0:1], st[:, 1:2], ALU.add, [stb], [stb])
        self.act(st[:, 3:4], st[:, 2:3], AF.Ln, [stb] + self.CB, [stb], scale=1.0 / D, bias=self.epsc)
        self.act(st[:, 4:5], st[:, 3:4], AF.Exp, [stb], [stb], scale=-0.5)
        for hf, (z, zb) in enumerate(((z0, z0b), (z1, z1b))):
            self.stt(self.ptmp[hf], z, st[:, 4:5], pnw[:, hf * 512:(hf + 1) * 512], ALU.mult, ALU.mult,
                     [zb, stb, pnwb], [self.ptmpb[hf]])
            xs = self.xTM[:, i, hf * 512:(hf + 1) * 512]
            self.tt("pool", xs, xs, self.ptmp[hf], ALU.add, [self.xb[i], self.ptmpb[hf]], [self.xb[i]])

    def out_proj_tile(self, lhs3, lhsb, col0, wsb, wsbb, nk, pnw, pnwb, i, tmp, tmpb):
        accs = ((self.ps[0][:], self.psb[0]), (self.ps[1][:], self.psb[1]))
        for hf in range(2):
            z, zb = accs[hf]
            for k in range(nk):
                self.mm(z, lhs3[:, k, col0:col0 + 128], wsb[:, k, hf * 512:(hf + 1) * 512], k == 0, k == nk - 1,
                        lhsb + [wsbb], [zb], inc=(k == nk - 1))
        self.post_tile(accs[0][0], accs[0][1], accs[1][0], accs[1][1], pnw, pnwb, i, tmp, tmpb)

    def phase_merge(self, s):
        S, T, TT, BLK, NB, TPB, CB = self.S, self.T, self.TT, self.BLK, self.NB, self.TPB, self.CB
        mixT = self.bf(8 * T).rearrange("p (k t) -> p k t", k=8)
        mixb = [[self.buf("mix") for _ in range(NB)] for _ in range(8)]
        wout = self.bf(8 * D).rearrange("p (k n) -> p k n", k=8); woutb = self.buf("wout")
        pnw = self.f32(D); pnwb = self.buf("pnw")
        tmp = [self.f32(BLK) for _ in range(4)]; tmpb = [self.buf("tmpM") for _ in range(4)]
        S.dma("sp", wout, self.wb_d["w_out"].rearrange("(k p) n -> p k n", p=128), reads=[self.wb_buf["w_out"]],
              writes=[woutb])
        S.dma("sp", pnw, self.pnw_d[0], writes=[pnwb])
        items = []
        for m in range(8):
            items += [("w_out_a", m * 128), ("w_in", C_G + m * 128), ("w_out_b", m * 128), ("w_in", C_G + D + m * 128)]
        pf = self.prefetch(items, LA=1)
        bg = self.cc("bgate")
        for m in range(8):
            wa, wab_ = next(pf)[1:]
            wga, wgab = next(pf)[1:]
            wbm, wbb = next(pf)[1:]
            wgb, wgbb = next(pf)[1:]
            for b in range(NB):
                ya, yab = self.proj_fm(wa, wab_, self.oaT, [self.oab[k][b] for k in range(8)], b)
                ga, gab = self.proj_fm(wga, wgab, self.hT, self.hblk(b), b)
                self.act(tmp[0], ga, AF.Sigmoid, [gab] + CB, [tmpb[0]], bias=bg[:, m:m + 1])
                self.tt("dve", tmp[1], ya, tmp[0], ALU.mult, [yab, tmpb[0]], [tmpb[1]])
                yb, ybb = self.proj_fm(wbm, wbb, self.obT, [self.obb[k][b] for k in range(8)], b)
                gb, gbb = self.proj_fm(wgb, wgbb, self.hT, self.hblk(b), b)
                self.act(tmp[2], gb, AF.Sigmoid, [gbb] + CB, [tmpb[2]], bias=bg[:, 8 + m:9 + m])
                self.tt("dve", tmp[3], yb, tmp[2], ALU.mult, [ybb, tmpb[2]], [tmpb[3]])
                self.tt("pool", mixT[:, m, self.bsl(b)], tmp[1], tmp[3], ALU.add, [tmpb[1], tmpb[3]], [mixb[m][b]])
        if self.dbg and s == 0:
            for (src, dstd, bb) in ((self.oaT, self.dbg_oa, self.oab), (self.obT, self.dbg_ob, self.obb)):
                for k in range(8):
                    for b in range(NB):
                        self.cpy("dve", tmp[0], src[:, k, self.bsl(b)], [bb[k][b]], [tmpb[0]])
                        S.dma("sp", dstd[:, k, self.bsl(b)], tmp[0], reads=[tmpb[0]], is_output=True)
        S.barrier()
        for i in range(TT):
            S.dma("sp", self.xTM[:, i, :], self.x_d[s, i * 128:(i + 1) * 128, :], writes=[self.xb[i]])
        for i in range(TT):
            b = i // TPB
            self.out_proj_tile(mixT, [mixb[k][b] for k in range(8)], i * 128, wout, woutb, 8, pnw, pnwb, i, tmp, tmpb)
        if self.dbg and s == 0:
            for i in range(TT):
                S.dma("sp", self.dbg_x1[i * 128:(i + 1) * 128, :], self.xTM[:, i, :], reads=[self.xb[i]], is_output=True)

    def phase_mem(self, s):
        S, T, TT, BLK, NB, TPB, CB = self.S, self.T, self.TT, self.BLK, self.NB, self.TPB, self.CB
        memT = self.bf(8 * MEM).rearrange("p (k t) -> p k t", k=8); memTb = [self.buf("memT") for _ in range(2)]
        kmT = self.bf(8 * MEM).rearrange("p (k t) -> p k t", k=8); kmTb = self.buf("kmT")
        vm = self.bf(2 * D).rearrange("p (i c) -> p i c", c=D); vmb = self.buf("vm")
        wbig = self.bf(8 * D).rearrange("p (k n) -> p k n", k=8); wbigb = self.buf("wbig")
        pnw = self.f32(D); pnwb = self.buf("pnw2")
        qmT = self.bf(8 * BLK).rearrange("p (k t) -> p k t", k=8); qmTb = [self.buf("qmT") for _ in range(8)]
        omT = self.bf(8 * BLK).rearrange("p (k t) -> p k t", k=8); omTb = [self.buf("omT") for _ in range(8)]
        Ej = [self.bf(BLK) for _ in range(4)]; Ejb = [self.buf("Em") for _ in range(4)]
        tmp = [self.f32(BLK) for _ in range(2)]; tmpb = [self.buf("tmpC") for _ in range(2)]
        S.dma("sp", wbig, self.wb_d["w_mkv"].rearrange("(k p) n -> p k n", p=128)[:, :, D:2 * D],
              reads=[self.wb_buf["w_mkv"]], writes=[wbigb])
        S.dma("sp", pnw, self.pnw_d[1], writes=[pnwb])
        for mt in range(2):
            r = mt % 2
            S.dma("sp", self.xin[r], self.mem_d[s, mt * 128:(mt + 1) * 128, :], writes=[self.xinb[r]])
            self.norm_tile(self.xin[r], self.xinb[r], self.cc("wpre_kv"), memT, memTb[mt], mt * 128)
        items = [("w_mkv", c * 128) for c in range(8)]
        for _ in range(NB):
            items += [("w_mq", c * 128) for c in range(8)]
        pf = self.prefetch(items, LA=4)
        for c in range(8):
            _, w, wb = next(pf)
            pa, pb = self.pst()
            for k in range(KC):
                self.mm(pa[:, 0:MEM], w[:, k, :], memT[:, k, :], k == 0, k == KC - 1, [wb] + memTb, [pb],
                        inc=(k == KC - 1))
            self.cpy("act", kmT[:, c, :], pa[:, 0:MEM], [pb], [kmTb])
        for mt in range(2):
            for hf in range(2):
                pa, pb = self.pst()
                for k in range(KC):
                    self.mm(pa, memT[:, k, mt * 128:(mt + 1) * 128], wbig[:, k, hf * 512:(hf + 1) * 512], k == 0,
                            k == KC - 1, [memTb[mt], wbigb], [pb], inc=(k == KC - 1))
                self.cpy("act", vm[:, mt, hf * 512:(hf + 1) * 512], pa, [pb], [vmb])
        S.dma("sp", wbig, self.wb_d["w_mo"].rearrange("(k p) n -> p k n", p=128), reads=[self.wb_buf["w_mo"]],
              writes=[wbigb])
        self.norm_from_x("wpre_mem")
        for b in range(NB):
            for c in range(8):
                _, w, wb = next(pf)
                o, pb = self.proj_fm(w, wb, self.hT, self.hblk(b), b)
                self.cpy("act", qmT[:, c, :], o, [pb], [qmTb[c]])
            for hm in range(4):
                for mt in range(2):
                    sc, scb = self.pst()
                    for dc in range(2):
                        c = 2 * hm + dc
                        self.mm(sc[:, 0:BLK], kmT[:, c, mt * 128:(mt + 1) * 128], qmT[:, c, :], dc == 0, dc == 1,
                                [kmTb, qmTb[c]], [scb], inc=(dc == 1))
                    ei = (hm % 2) * 2 + mt
                    self.act(Ej[ei], sc[:, 0:BLK], AF.Exp, [scb], [Ejb[ei]], scale=256 ** -0.5)
                e0, e1 = (hm % 2) * 2, (hm % 2) * 2 + 1
                sm, smb = self.pst()
                for mt, ei in ((0, e0), (1, e1)):
                    self.mm(sm[:, 0:BLK], self.ones_bf, Ej[ei], mt == 0, mt == 1, [Ejb[ei]] + CB, [smb], inc=(mt == 1))
                self.act(tmp[0], sm[:, 0:BLK], AF.Ln, [smb], [tmpb[0]])
                self.act(tmp[0], tmp[0], AF.Exp, [tmpb[0]], [tmpb[0]], scale=-1.0)
                for ec in range(2):
                    c = 2 * hm + ec
                    po, pob = self.pst()
                    for mt, ei in ((0, e0), (1, e1)):
                        self.mm(po[:, 0:BLK], vm[:, mt, c * 128:(c + 1) * 128], Ej[ei], mt == 0, mt == 1,
                                [vmb, Ejb[ei]], [pob], inc=(mt == 1))
                    self.tt("dve", omT[:, c, :], po[:, 0:BLK], tmp[0], ALU.mult, [pob, tmpb[0]], [omTb[c]])
            for j in range(TPB):
                i = b * TPB + j
                self.out_proj_tile(omT, omTb, j * 128, wbig, wbigb, 8, pnw, pnwb, i, tmp, tmpb)
        if self.dbg and s == 0:
            for i in range(TT):
                S.dma("sp", self.dbg_x2[i * 128:(i + 1) * 128, :], self.xTM[:, i, :], reads=[self.xb[i]], is_output=True)

    def phase_ffn(self, s):
        S, T, TT, BLK, NB, TPB, CB = self.S, self.T, self.TT, self.BLK, self.NB, self.TPB, self.CB
        wfo = self.bf(FC * D).rearrange("p (k n) -> p k n", k=FC); wfob = self.buf("wfo")
        aT = self.bf(FC * BLK).rearrange("p (k t) -> p k t", k=FC); aTb = [self.buf("aT") for _ in range(FC)]
        pnw = self.f32(D); pnwb = self.buf("pnw3")
        tmp = [self.f32(BLK) for _ in range(2)]; tmpb = [self.buf("tmpF") for _ in range(2)]
        S.dma("sp", wfo, self.wb_d["w_ffn_out"].rearrange("(k p) n -> p k n", p=128), reads=[self.wb_buf["w_ffn_out"]],
              writes=[wfob])
        S.dma("sp", pnw, self.pnw_d[2], writes=[pnwb])
        self.norm_from_x("wpre_ffn")
        items = []
        for _ in range(NB):
            for fc in range(FC):
                items += [("w_ffn_in", fc * 128), ("w_ffn_in", DFF + fc * 128)]
        pf = self.prefetch(items, LA=3)
        for b in range(NB):
            for fc in range(FC):
                wg, wgb = next(pf)[1:]
                wu, wub = next(pf)[1:]
                g, gb = self.proj_fm(wg, wgb, self.hT, self.hblk(b), b)
                u, ub_ = self.proj_fm(wu, wub, self.hT, self.hblk(b), b)
                t_i = fc % 2
                self.act(tmp[t_i], g, AF.Silu, [gb], [tmpb[t_i]])
                self.tt("dve", aT[:, fc, :], u, tmp[t_i], ALU.mult, [ub_, tmpb[t_i]], [aTb[fc]])
            for j in range(TPB):
                i = b * TPB + j
                self.out_proj_tile(aT, aTb, j * 128, wfo, wfob, FC, pnw, pnwb, i, tmp, tmpb)
                S.dma("sp", self.out_d[s, i * 128:(i + 1) * 128, :], self.xTM[:, i, :], reads=[self.xb[i]],
                      is_output=True)

    def emit(self):
        self.setup()
        for s in range(self.NSEQ):
            self.newphase()
            self.alloc_norm_tmps(True)
            self.norm_from_dram(s, "wpre_mix")
            self.phase_decay()
            self.newphase(no_tail=True)
            self.mixer_a()
            self.newphase()
            self.mixer_b()
            self.newphase()
            self.alloc_norm_tmps(False)
            self.phase_merge(s)
            self.newphase()
            self.alloc_norm_tmps(True)
            self.phase_mem(s)
            self.newphase()
            self.alloc_norm_tmps(False)
            self.phase_ffn(s)
        self.S.finish()
        self.es.close()
        return self.nc


_PROG_CACHE = {}


def _host_inputs(inputs, T):
    cp, cbf = _make_cpack(inputs, T)
    pnw = np.stack([np.tile(np.asarray(inputs[n][0], np.float32).reshape(1, D), (128, 1))
                    for n in ("mix_post_norm", "mem_post_norm", "ffn_post_norm")], 0)
    shared = {"cpack": cp, "cbf": cbf, "pnw": np.ascontiguousarray(pnw)}
    for n, r, c in W_SPECS:
        shared[n] = np.ascontiguousarray(np.asarray(inputs[n][0], np.float32))
    return shared


def run(inputs, n_cores=N_CORES, dbg=False, inv32=False):
    x = np.asarray(inputs["x"], np.float32)
    mem = np.asarray(inputs["mem"], np.float32)
    B, T, _ = x.shape
    assert B % n_cores == 0
    nseq = B // n_cores
    key = (nseq, T, dbg, inv32)
    if key not in _PROG_CACHE:
        _PROG_CACHE[key] = Kern(nseq, T, dbg=dbg, inv32=inv32).emit()
    nc = _PROG_CACHE[key]
    shared = _host_inputs(inputs, T)
    in_maps = []
    for c in range(n_cores):
        m = dict(shared)
        m["x"] = np.ascontiguousarray(x[c * nseq:(c + 1) * nseq])
        m["mem"] = np.ascontiguousarray(mem[c * nseq:(c + 1) * nseq])
        in_maps.append(m)
    res = run_bass_kernel_spmd(nc, in_maps, core_ids=list(range(n_cores)))
    out = np.concatenate([np.asarray(r["out"], np.float32) for r in res.results], axis=0)
    return out, res


def kernel(**inputs):
    out, _ = run(inputs)
    return out
```

```python
import math
from contextlib import ExitStack
import numpy as np
import ml_dtypes
import concourse.bass as bass
import concourse.mybir as mybir
from concourse.alu_op_type import AluOpType as ALU
from concourse.bass_utils import run_bass_kernel_spmd

F32 = mybir.dt.float32
BF16 = mybir.dt.bfloat16
AF = mybir.ActivationFunctionType

D = 1024
KC = 8
HD = 128
NH = 8
DFF = 2816
FC = DFF // 128
MEM = 256
IN_DIM = 7712
C_QKV, C_Z, C_AB, C_QB, C_KB, C_VB, C_G = 0, 3072, 4096, 4128, 5152, 5408, 5664
EPS = 1e-6
NEG = -1.0e5
N_CORES = 8


class Buf:
    __slots__ = ("name", "writer", "readers", "dsem", "dcount")

    def __init__(self, name):
        self.name = name
        self.writer = None
        self.readers = {}
        self.dsem = None
        self.dcount = 0


class Sched:
    SAME_ENGINE_SYNC = True

    def __init__(self, nc, es):
        self.nc = nc
        self.es = es
        self.engs = {"pe": nc.tensor, "act": nc.scalar, "dve": nc.vector,
                     "pool": nc.gpsimd, "sp": nc.sync}
        self.sem = {}
        self.count = {}
        for e in self.engs:
            self.sem[e] = es.enter_context(nc.semaphore("sem_" + e))
            self.count[e] = 0
        self.pending = {e: False for e in self.engs}
        self.waited = {e: {} for e in self.engs}
        self.n_inst = 0
        self.n_wait = 0
        self.out_tokens = []
        self.dsems = {}

    def _wait(self, e, tok):
        key, sem, val, src = tok
        if src == e and (e == "pe" or e == "sp" or not self.SAME_ENGINE_SYNC):
            return
        w = self.waited[e]
        if w.get(key, 0) >= val:
            return
        self.engs[e].wait_ge(sem, val)
        self.n_wait += 1
        w[key] = val

    def _deps(self, e, reads, writes):
        for b in reads:
            if b.writer is not None:
                self._wait(e, b.writer)
        for b in writes:
            if b.writer is not None:
                self._wait(e, b.writer)
            for tok in b.readers.values():
                self._wait(e, tok)

    def _commit(self, tok, reads, writes):
        for b in reads:
            b.readers[tok[0]] = tok
        for b in writes:
            b.writer = tok
            b.readers = {}

    def op(self, e, fn, reads=(), writes=(), inc=True):
        self._deps(e, reads, writes)
        inst = fn(self.engs[e])
        if inc:
            self.count[e] += 1
            inst.then_inc(self.sem[e], 1)
            tok = (e, self.sem[e], self.count[e], e)
            self.pending[e] = False
        else:
            tok = (e, self.sem[e], self.count[e] + 1, e)
            self.pending[e] = True
        self._commit(tok, reads, writes)
        self.n_inst += 1
        return inst

    def dma(self, q, out_ap, in_ap, reads=(), writes=(), sembuf=None, is_output=False, **kw):
        self._deps(q, reads, writes)
        b = sembuf if sembuf is not None else (writes[0] if writes else reads[0])
        if b.dsem is None:
            b.dsem = self.es.enter_context(self.nc.semaphore("d_" + b.name))
            self.dsems[b.name] = b
        b.dcount += 16
        inst = self.engs[q].dma_start(out=out_ap, in_=in_ap, **kw)
        inst.then_inc(b.dsem, 16)
        tok = ("d_" + b.name, b.dsem, b.dcount, None)
        self._commit(tok, reads, writes)
        if is_output:
            self.out_tokens.append(tok)
        self.n_inst += 1
        return inst

    def barrier(self):
        assert not any(self.pending.values())
        toks = [(e, self.sem[e], self.count[e], e) for e in self.engs if self.count[e] > 0]
        for b in self.dsems.values():
            if not b.name.startswith("wb_"):
                toks.append(("d_" + b.name, b.dsem, b.dcount, None))
        for e in self.engs:
            for tok in toks:
                if tok[3] != e:
                    self._wait(e, tok)

    def finish(self):
        last = {}
        for tok in self.out_tokens:
            last[tok[0]] = tok
        for tok in last.values():
            self._wait("sp", tok)


def _cpack_layout(T):
    ent = [("ident", 128), ("ones", 128), ("ucum", 128), ("lcum", 128),
           ("maskf", 128), ("maskb", 128),
           ("cosr", T // 64), ("sinr", T // 64), ("cosc", 64), ("sinc", 64),
           ("wpre_mix", 8), ("wpre_mem", 8), ("wpre_ffn", 8), ("wpre_kv", 8),
           ("convw", 24 * 5), ("lanorm", 1), ("qnorm", 1), ("knorm", 1), ("bgate", 16),
           ("alog", 16), ("dtb", 16)]
    off, o = {}, 0
    for n, c in ent:
        off[n] = (o, c)
        o += c
    return off, o


def _make_cpack(p, T):
    off, ncol = _cpack_layout(T)
    cp = np.zeros((128, ncol), np.float32)

    def put(name, arr):
        o, c = off[name]
        cp[:, o:o + c] = np.asarray(arr, np.float32).reshape(128, c)

    idx = np.arange(128)
    put("ident", np.eye(128))
    put("ones", np.ones((128, 128)))
    partner = np.where((idx % 64) < 32, idx + 32, idx - 32)
    pm = np.zeros((128, 128), np.float32)
    pm[partner, idx] = 1.0
    pp, ff = idx[:, None], idx[None, :]
    put("ucum", (pp <= ff))
    put("lcum", (pp >= ff))
    put("maskf", np.where(ff >= pp, 0.0, NEG))
    put("maskb", np.where(ff <= pp, 0.0, NEG))
    pb_, fb_ = pp // 32, ff // 32
    bd = (pb_ == fb_)
    m1f = ((pb_ == 1) & (fb_ == 0)) | ((pb_ == 3) & (fb_ == 2))
    m2f = (pb_ >= 2) & (fb_ <= 1)
    cbf = np.concatenate([np.eye(128), np.ones((128, 128)), pm, np.tile(np.eye(128), (1, 4)),
                          np.tile((ff > pp).astype(np.float32), (1, 4)),
                          np.tile((ff < pp).astype(np.float32), (1, 4)),
                          bd, m1f, m2f, m1f.T, m2f.T], axis=1).astype(ml_dtypes.bfloat16)
    half = HD // 2
    inv_freq = (np.float32(10000.0) ** (-np.arange(0, half, 2, dtype=np.float32) / np.float32(half))).astype(np.float32)
    rows = np.arange(T // 64, dtype=np.float32)
    cols = np.arange(64, dtype=np.float32)
    cosr = np.zeros((128, T // 64), np.float32); sinr = np.zeros_like(cosr)
    cosc = np.zeros((128, 64), np.float32); sinc = np.zeros_like(cosc)
    for d in range(64):
        i = d % 32
        sg = -1.0 if d < 32 else 1.0
        ang_r = (rows * inv_freq[i]).astype(np.float32)
        ang_c = (cols * inv_freq[i]).astype(np.float32)
        cosr[d] = np.cos(ang_r); sinr[d] = sg * np.sin(ang_r)
        cosc[64 + d] = np.cos(ang_c); sinc[64 + d] = sg * np.sin(ang_c)
    put("cosr", cosr); put("sinr", sinr); put("cosc", cosc); put("sinc", sinc)
    fm = lambda v: np.asarray(v, np.float32).reshape(-1, 128).T
    put("wpre_mix", fm(p["mix_pre_norm"][0]))
    put("wpre_mem", fm(p["mem_pre_norm"][0]))
    put("wpre_ffn", fm(p["ffn_pre_norm"][0]))
    put("wpre_kv", fm(p["mem_kv_norm"][0]))
    cw = np.asarray(p["conv_w"][0], np.float32)
    put("convw", cw.reshape(5, 24, 128).transpose(2, 1, 0))
    put("lanorm", np.asarray(p["la_norm_w"][0]).reshape(128, 1))
    put("qnorm", np.asarray(p["q_norm_w"][0]).reshape(128, 1))
    put("knorm", np.asarray(p["k_norm_w"][0]).reshape(128, 1))
    put("bgate", fm(p["b_gate"][0]))
    put("alog", np.tile(np.asarray(p["la_a_log"][0], np.float32).reshape(1, 16), (128, 1)))
    put("dtb", np.tile(np.asarray(p["la_dt_bias"][0], np.float32).reshape(1, 16), (128, 1)))
    return cp, np.ascontiguousarray(cbf)


W_SPECS = [("w_in", D, IN_DIM), ("w_out_a", D, D), ("w_out_b", D, D), ("w_out", D, D),
           ("w_mq", D, D), ("w_mkv", D, 2 * D), ("w_mo", D, D),
           ("w_ffn_in", D, 2 * DFF), ("w_ffn_out", DFF, D)]


class Kern:
    def __init__(self, NSEQ, T, dbg=False, inv32=False):
        self.NSEQ, self.T, self.dbg, self.inv32 = NSEQ, T, dbg, inv32
        self.TT = T // 128
        self.BLK = min(512, T)
        self.NB = T // self.BLK
        self.TPB = self.BLK // 128
        self.nc = nc = bass.Bass("TRN2", target_bir_lowering=False)
        self.es = es = ExitStack()
        self.S = Sched(nc, es)
        self.off, self.ncol = _cpack_layout(T)
        dt = nc.dram_tensor
        self.x_d = dt("x", [NSEQ, T, D], F32, kind="ExternalInput").ap()
        self.mem_d = dt("mem", [NSEQ, MEM, D], F32, kind="ExternalInput").ap()
        self.cp_d = dt("cpack", [128, self.ncol], F32, kind="ExternalInput").ap()
        self.pnw_d = dt("pnw", [3, 128, D], F32, kind="ExternalInput").ap()
        self.cbf_d = dt("cbf", [128, 2560], BF16, kind="ExternalInput").ap()
        self.out_d = dt("out", [NSEQ, T, D], F32, kind="ExternalOutput").ap()
        self.w_d, self.wb_d, self.wb_buf = {}, {}, {}
        for n, r, c in W_SPECS:
            self.w_d[n] = dt(n, [r, c], F32, kind="ExternalInput").ap()
            self.wb_d[n] = dt(n + "_bf", [r, c], BF16, kind="Internal").ap()
            self.wb_buf[n] = Buf("wb_" + n)
        if dbg:
            self.dbg_oa = dt("dbg_oa", [128, 8, T], F32, kind="ExternalOutput").ap()
            self.dbg_ob = dt("dbg_ob", [128, 8, T], F32, kind="ExternalOutput").ap()
            self.dbg_x1 = dt("dbg_x1", [T, D], F32, kind="ExternalOutput").ap()
            self.dbg_x2 = dt("dbg_x2", [T, D], F32, kind="ExternalOutput").ap()
        self.ARENA = 53200
        ARENA_OFF = 16544
        self.R32W = 12 * self.BLK + 128
        self.LIM = self.ARENA
        self.arena = es.enter_context(nc.sbuf_tensor("arena", [128, self.ARENA], F32))
        self.top = 0
        self.topB = self.LIM
        self.no_tail = True
        self.ps = [es.enter_context(nc.psum_tensor("ps%d" % i, [128, 512], F32)) for i in range(8)]
        self.psb = [Buf("ps%d" % i) for i in range(8)]
        self.ps_rr = 0
        self.ps_base, self.ps_n = 2, 6
        self.uid = 0

    def f32(self, n):
        if self.top + n <= self.LIM:
            a = self.arena[:, self.top:self.top + n]
            self.top += n
            return a
        assert not self.no_tail and self.topB + n <= self.ARENA, ("SBUF arena overflow", self.top, self.topB, n)
        a = self.tailf[:, self.topB - self.LIM:self.topB - self.LIM + n]
        self.topB += n
        return a

    def bf(self, n):
        assert n % 2 == 0
        return self.f32(n // 2).bitcast(BF16)

    def buf(self, name):
        self.uid += 1
        return Buf("%s_%d" % (name, self.uid))

    def pst(self):
        i = self.ps_base + (self.ps_rr % self.ps_n)
        self.ps_rr += 1
        return self.ps[i][:], self.psb[i]

    def cc(self, name):
        o, c = self.off[name]
        return self.cp[:, o:o + c]

    def mm(self, out, lhsT, rhs, start, stop, R, W, inc=True):
        self.S.op("pe", lambda e: e.matmul(out, lhsT=lhsT, rhs=rhs, start=start, stop=stop),
                  reads=R, writes=W, inc=inc)

    def act(self, out, in_, func, R, W, **kw):
        self.S.op("act", lambda e: e.activation(out=out, in_=in_, func=func, **kw), reads=R, writes=W)

    def tt(self, eng, out, in0, in1, op, R, W):
        self.S.op(eng, lambda e: e.tensor_tensor(out=out, in0=in0, in1=in1, op=op), reads=R, writes=W)

    def stt(self, out, in0, scalar, in1, op0, op1, R, W):
        self.S.op("dve", lambda e: e.scalar_tensor_tensor(out=out, in0=in0, scalar=scalar, in1=in1,
                                                           op0=op0, op1=op1), reads=R, writes=W)

    def cpy(self, eng, out, in_, R, W):
        if eng == "act":
            self.act(out, in_, AF.Copy, R, W)
        else:
            self.S.op(eng, lambda e: e.tensor_copy(out=out, in_=in_), reads=R, writes=W)

    def rsqrt_act(self, out, in_, scale, R, W, tmp, tmpb, post_bias=0.0):
        self.act(tmp, in_, AF.Ln, R, [tmpb], scale=scale, bias=self.epsc)
        if post_bias == 0.0:
            self.act(out, tmp, AF.Exp, [tmpb], W, scale=-0.5)
        else:
            self.act(out, tmp, AF.Exp, [tmpb], W, scale=-0.5, bias=self.qsc)

    def setup(self):
        S = self.S
        self.cp = self.f32(self.ncol)
        self.cpb = Buf("cpack")
        S.dma("sp", self.cp, self.cp_d[:, :], writes=[self.cpb])
        for n, r, c in W_SPECS:
            for c0 in range(0, c, 2048):
                c1 = min(c, c0 + 2048)
                S.dma("pool", self.wb_d[n][:, c0:c1], self.w_d[n][:, c0:c1],
                      writes=[self.wb_buf[n]], sembuf=self.wb_buf[n])
        cbf = self.bf(2560)
        self.cbb = Buf("constbf")
        S.dma("sp", cbf, self.cbf_d[:, :], writes=[self.cbb])
        self.ident_bf = cbf[:, 0:128]; self.ones_bf = cbf[:, 128:256]; self.perm_bf = cbf[:, 256:384]
        self.ident4_bf = cbf[:, 384:896]; self.strf_bf = cbf[:, 896:1408]; self.strb_bf = cbf[:, 1408:1920]
        self.bd_bf = cbf[:, 1920:2048]
        self.m1_bf = [cbf[:, 2048:2176], cbf[:, 2304:2432]]
        self.m2_bf = [cbf[:, 2176:2304], cbf[:, 2432:2560]]
        cb, R = self.cbb, [self.cpb]
        sm = self.f32(24)
        self.epsc = sm[:, 0:1]; self.qsc = sm[:, 1:2]; self.nega = sm[:, 8:24]
        S.op("dve", lambda e: e.memset(self.epsc, EPS), writes=[cb])
        S.op("dve", lambda e: e.memset(self.qsc, math.log(HD ** -0.5)), writes=[cb])
        self.act(self.nega, self.cc("alog"), AF.Exp, R, [cb])
        S.op("dve", lambda e: e.tensor_scalar(out=self.nega, in0=self.nega, scalar1=-1.0, scalar2=None,
                                              op0=ALU.mult), reads=[cb], writes=[cb])
        self.CB = [self.cpb, self.cbb]
        T = self.T
        self.hT = self.bf(KC * T).rearrange("p (k t) -> p k t", k=KC)
        self.hTb = [Buf("hT%d" % i) for i in range(self.TT)]
        self.R1 = self.f32(8 * T)
        self.oaT = self.R1[:, 0:4 * T].bitcast(BF16).rearrange("p (k t) -> p k t", k=8)
        self.obT = self.R1[:, 4 * T:8 * T].bitcast(BF16).rearrange("p (k t) -> p k t", k=8)
        self.xTM = self.R1[:, 0:8 * T].rearrange("p (i c) -> p i c", c=D)
        self.oab = [[Buf("oa%d_%d" % (h, b)) for b in range(self.NB)] for h in range(8)]
        self.obb = [[Buf("ob%d_%d" % (h, b)) for b in range(self.NB)] for h in range(8)]
        self.xb = [Buf("x%d" % i) for i in range(self.TT)]
        n = 2 * self.TT * 8
        v4 = lambda a: a.rearrange("p (d i h) -> p d i h", d=2, i=self.TT)
        self.gam = v4(self.f32(n)); self.ngam = v4(self.f32(n)); self.eg = v4(self.f32(n))
        self.ekd = v4(self.f32(n)); self.dl = v4(self.f32(n)); self.beta = v4(self.f32(n))
        self.nbeta = v4(self.f32(n))
        self.decb = Buf("decay")
        self.NW = 4
        self.wslot = [self.bf(1024).rearrange("p (k n) -> p k n", k=8) for _ in range(self.NW)]
        self.wslotb = [Buf("ws%d" % i) for i in range(self.NW)]
        self.ws_rr = 0
        self.st = [self.f32(8) for _ in range(2)]
        self.stb = [Buf("st%d" % i) for i in range(2)]
        self.nrr = 0
        self.mark = self.top

    def alloc_norm_tmps(self, need_xin):
        if need_xin:
            self.xin = [self.f32(D) for _ in range(2)]
            self.xinb = [self.buf("xin") for i in range(2)]
        self.xn = [self.bf(D) for _ in range(2)]
        self.xnb = [self.buf("xn") for i in range(2)]
        self.junk = self.bf(D); self.junkb = self.buf("junk")
        self.ptmp = [self.f32(512) for _ in range(2)]
        self.ptmpb = [self.buf("ptmp") for _ in range(2)]

    def newphase(self, no_tail=False):
        self.S.barrier()
        self.top = self.mark
        self.topB = self.LIM
        self.no_tail = no_tail

    def wload(self, name, c0, ncols=128):
        i = self.ws_rr
        self.ws_rr = (self.ws_rr + 1) % self.NW
        src = self.wb_d[name].rearrange("(k p) n -> p k n", p=128)[:, :, c0:c0 + ncols]
        dst = self.wslot[i][:, :, 0:ncols]
        self.S.dma("sp", dst, src, reads=[self.wb_buf[name]], writes=[self.wslotb[i]])
        return dst, self.wslotb[i]

    def prefetch(self, items, LA=4):
        loaded = {}
        nxt = 0
        for i in range(len(items)):
            while nxt < min(len(items), i + LA):
                loaded[nxt] = self.wload(*items[nxt])
                nxt += 1
            ap, b = loaded.pop(i)
            yield i, ap, b

    def norm_tile(self, src, srcb, wcol, dst3, dstb, tcol):
        r = self.nrr
        self.nrr ^= 1
        st, stb, xn, xnb = self.st[r], self.stb[r], self.xn[r], self.xnb[r]
        self.act(self.junk, src, AF.Square, [srcb], [self.junkb, stb], accum_out=st[:, 0:1])
        self.act(st[:, 1:2], st[:, 0:1], AF.Ln, [stb] + self.CB, [stb], scale=1.0 / D, bias=self.epsc)
        self.act(st[:, 2:3], st[:, 1:2], AF.Exp, [stb], [stb], scale=-0.5)
        self.act(xn, src, AF.Copy, [srcb, stb], [xnb], scale=st[:, 2:3])
        for hf in range(2):
            pa, pb = self.pst()
            pv = pa.rearrange("p (k t) -> p k t", k=4)
            for k in range(4):
                kc = hf * 4 + k
                self.mm(pv[:, k, :], xn[:, kc * 128:(kc + 1) * 128], self.ident_bf, True, True,
                        [xnb] + self.CB, [pb])
            self.tt("dve", dst3[:, hf * 4:hf * 4 + 4, tcol:tcol + 128], pv,
                    wcol[:, hf * 4:hf * 4 + 4].unsqueeze(2).to_broadcast([128, 4, 128]), ALU.mult,
                    [pb] + self.CB, [dstb])

    def norm_from_dram(self, s, wname):
        for i in range(self.TT):
            r = i % 2
            self.S.dma("sp", self.xin[r], self.x_d[s, i * 128:(i + 1) * 128, :], writes=[self.xinb[r]])
            self.norm_tile(self.xin[r], self.xinb[r], self.cc(wname), self.hT, self.hTb[i], i * 128)

    def norm_from_x(self, wname):
        for i in range(self.TT):
            self.norm_tile(self.xTM[:, i, :], self.xb[i], self.cc(wname), self.hT, self.hTb[i], i * 128)

    def hblk(self, b):
        return [self.hTb[i] for i in range(b * self.TPB, (b + 1) * self.TPB)]

    def bsl(self, b):
        return slice(b * self.BLK, (b + 1) * self.BLK)

    def proj_fm(self, w, wb, rhs3, rhsb, b, n=None):
        pa, pb = self.pst()
        n = self.BLK if n is None else n
        o = pa[:, 0:n]
        nk = w.shape[1]
        for k in range(nk):
            self.mm(o, w[:, k, :], rhs3[:, k, b * self.BLK:b * self.BLK + n], k == 0, k == nk - 1,
                    [wb] + rhsb, [pb], inc=(k == nk - 1))
        return o, pb

    def phase_decay(self):
        S, TT = self.S, self.TT
        wab = self.bf(8 * 32).rearrange("p (k n) -> p k n", k=8)
        wabb = self.buf("wab")
        S.dma("sp", wab, self.wb_d["w_in"].rearrange("(k p) n -> p k n", p=128)[:, :, C_AB:C_AB + 32],
              reads=[self.wb_buf["w_in"]], writes=[wabb])
        n16 = TT * 16
        xa = self.f32(n16); ax = self.f32(n16); g = self.f32(n16); bt = self.f32(n16)
        tb = self.buf("dtmp")
        v3 = lambda a: a.rearrange("p (i c) -> p i c", c=16)
        pa, pb = self.pst()
        abp = pa[:, 0:TT * 32].rearrange("p (i c) -> p i c", c=32)
        for i in range(TT):
            for k in range(KC):
                self.mm(abp[:, i, :], self.hT[:, k, i * 128:(i + 1) * 128], wab[:, k, :], k == 0, k == KC - 1,
                        [self.hTb[i], wabb], [pb], inc=(k == KC - 1))
        dtb = self.cc("dtb").unsqueeze(1).to_broadcast([128, TT, 16])
        negab = self.nega.unsqueeze(1).to_broadcast([128, TT, 16])
        self.tt("dve", v3(xa), abp[:, :, 0:16], dtb, ALU.add, [pb] + self.CB, [tb])
        S.op("dve", lambda e: e.tensor_scalar(out=ax, in0=xa, scalar1=-1.0, scalar2=None, op0=ALU.mult),
             reads=[tb], writes=[tb])
        self.tt("dve", ax, ax, xa, ALU.min, [tb], [tb])
        self.act(ax, ax, AF.Exp, [tb], [tb])
        self.act(ax, ax, AF.Ln, [tb], [tb], bias=1.0)
        self.stt(xa, xa, 0.0, ax, ALU.max, ALU.add, [tb], [tb])
        self.tt("dve", v3(g), v3(xa), negab, ALU.mult, [tb] + self.CB, [tb])
        self.act(v3(bt), abp[:, :, 16:32], AF.Sigmoid, [pb], [tb])
        g3, b3 = v3(g), v3(bt)
        pg, pgb = self.pst()
        n8 = TT * 8
        gps = pg[:, 0:2 * n8].rearrange("p (d i h) -> p d i h", d=2, i=TT)
        tps = pg[:, 2 * n8:4 * n8].rearrange("p (d i h) -> p d i h", d=2, i=TT)
        for d in range(2):
            cum = self.cc("ucum") if d == 0 else self.cc("lcum")
            self.mm(gps[:, d], cum, g3[:, :, d * 8:(d + 1) * 8], True, True, [tb] + self.CB, [pgb])
            self.mm(tps[:, d], self.cc("ones"), g3[:, :, d * 8:(d + 1) * 8], True, True, [tb] + self.CB, [pgb])
        db = self.decb
        for d in range(2):
            self.cpy("dve", self.gam[:, d], gps[:, d], [pgb], [db])
            S.op("dve", lambda e, d=d: e.tensor_scalar(out=self.ngam[:, d], in0=gps[:, d], scalar1=-1.0,
                                                       scalar2=None, op0=ALU.mult), reads=[pgb], writes=[db])
            self.act(self.eg[:, d], gps[:, d], AF.Exp, [pgb], [db])
            self.tt("dve", self.ekd[:, d], tps[:, d], self.gam[:, d], ALU.subtract, [pgb, db], [db])
            self.act(self.ekd[:, d], self.ekd[:, d], AF.Exp, [db], [db])
            self.act(self.dl[:, d], tps[:, d], AF.Exp, [pgb], [db])
            self.cpy("dve", self.beta[:, d], b3[:, :, d * 8:(d + 1) * 8], [tb], [db])
            S.op("dve", lambda e, d=d: e.tensor_scalar(out=self.nbeta[:, d], in0=b3[:, :, d * 8:(d + 1) * 8],
                                                       scalar1=-1.0, scalar2=None, op0=ALU.mult),
                 reads=[tb], writes=[db])

    def mixer_a(self):
        S, T, TT, BLK, NB, TPB = self.S, self.T, self.TT, self.BLK, self.NB, self.TPB
        F32R = mybir.dt.float32r
        xtop = [4 * T]
        def xf32(n):
            a_ = self.R1[:, xtop[0]:xtop[0] + n]
            xtop[0] += n
            assert xtop[0] <= 8 * T
            return a_
        xbf = lambda n: xf32(n // 2).bitcast(BF16)
        pre = self.f32(T + 4); preb = self.buf("pre")
        oT = pre[:, 2:T + 2]; oTb = [self.buf("oT") for _ in range(TT)]
        cv = self.f32(T); cvb = self.buf("cv")
        qT = self.bf(T); kT = self.bf(T); vF = self.bf(T)
        qTb = [self.buf("qT") for _ in range(NB)]; kTb = [self.buf("kT") for _ in range(NB)]
        vFb = self.buf("vF")
        r3 = lambda a_: a_.rearrange("p (i d) -> p i d", d=128)
        ub = [r3(cv), r3(xf32(T))]
        ubb = [[self.buf("ub") for _ in range(NB)] for _ in range(2)]
        qg = [vF, xbf(T)]; qgb = [[self.buf("qg") for _ in range(NB)] for _ in range(2)]
        kd = [r3(self.bf(T)), r3(xbf(T))]; kdb = [self.buf("kd") for _ in range(2)]
        wT = [self.bf(T), xbf(T)]; wTb = [[self.buf("wT") for _ in range(NB)] for _ in range(2)]
        qkE = [self.bf(T), xbf(T)]; qkEb = [[self.buf("qkE") for _ in range(NB)] for _ in range(2)]
        kTM = r3(xbf(T)); kTMb = self.buf("kTM")
        vTM = r3(self.bf(T)); vTMb = self.buf("vTM")
        kg = r3(xbf(T)); kgb = self.buf("kg")
        Sst = [self.f32(128) for _ in range(2)]; Sbf = [self.bf(128) for _ in range(2)]
        Sb = [self.buf("S") for _ in range(2)]; Sbfb = [self.buf("Sbf") for _ in range(2)]
        vnew = [[self.bf(128) for _ in range(2)] for _ in range(2)]
        vnewb = [[self.buf("vnew") for _ in range(2)] for _ in range(2)]
        E = [self.bf(BLK) for _ in range(2)]; Eb = [self.buf("E") for _ in range(2)]
        Es = [self.bf(BLK) for _ in range(2)]; Esb = [self.buf("Es") for _ in range(2)]
        NG = 2
        Bm = [[self.bf(BLK) for _ in range(NG)] for e_ in range(2)]
        BTm = [[self.bf(BLK) for _ in range(NG)] for _ in range(2)]
        Pm = [[self.bf(BLK) for _ in range(NG)] for _ in range(2)]
        Bmb = [[self.buf("B") for _ in range(NG)] for _ in range(2)]
        BTmb = [[self.buf("BT") for _ in range(NG)] for _ in range(2)]
        Pmb = [[self.buf("P") for _ in range(NG)] for _ in range(2)]
        N1 = [self.bf(BLK) for _ in range(2)]; N1b = [self.buf("N1") for _ in range(2)]
        N2 = [self.bf(BLK) for _ in range(2)]; N2b = [self.buf("N2") for _ in range(2)]
        Tt = [self.bf(BLK) for _ in range(2)]; Ttb = [self.buf("Tt") for _ in range(2)]
        Xm = [self.bf(BLK) for _ in range(2)]; Xmb = [self.buf("Xm") for _ in range(2)]
        tmp = [self.f32(BLK) for _ in range(3)]; tmpb = [self.buf("tmpA") for _ in range(3)]
        sqb16 = self.bf(BLK); sqb16b = self.buf("sq16")
        S.op("pool", lambda e: e.memset(pre[:, 0:2], 0.0), writes=[preb])
        S.op("pool", lambda e: e.memset(pre[:, T + 2:T + 4], 0.0), writes=[preb])
        cw = self.cc("convw").rearrange("p (c j) -> p c j", j=5)
        CB = self.CB
        v4 = lambda a_: a_.rearrange("p (i c) -> p i c", c=128)
        HIER = not self.inv32
        NL = 4 if HIER else 6
        bc3 = lambda m_: m_.unsqueeze(1).to_broadcast([128, TPB, 128])

        for h in range(NH):
            items = [("w_in", C_QKV + t * 1024 + h * 128) for t in range(3)] + [("w_in", C_Z + h * 128)]
            wl = list(self.prefetch(items, LA=4))
            for t in range(3):
                _, w, wb = wl[t]
                for b in range(NB):
                    o, pb = self.proj_fm(w, wb, self.hT, self.hblk(b), b)
                    self.cpy("act", pre[:, 2 + b * BLK:2 + (b + 1) * BLK], o, [pb], [preb] + oTb)
                cch = t * 8 + h
                S.op("dve", lambda e: e.tensor_scalar(out=cv, in0=pre[:, 0:T], scalar1=cw[:, cch, 0:1], scalar2=None,
                                                      op0=ALU.mult), reads=[preb] + CB, writes=[cvb] + ubb[0])
                for j in range(1, 5):
                    self.stt(cv, pre[:, j:j + T], cw[:, cch, j:j + 1], cv, ALU.mult, ALU.add, [preb, cvb] + CB, [cvb])
                if t == 2:
                    self.act(vF, cv, AF.Silu, [cvb], [vFb] + qgb[0])
                    continue
                self.act(cv, cv, AF.Silu, [cvb], [cvb])
                dst, dstb = (qT, qTb) if t == 0 else (kT, kTb)
                for b in range(NB):
                    sl = self.bsl(b)
                    self.act(sqb16, cv[:, sl], AF.Square, [cvb], [sqb16b])
                    pa, pb = self.pst()
                    self.mm(pa[:, 0:BLK], self.ones_bf, sqb16, True, True, [sqb16b] + CB, [pb])
                    self.rsqrt_act(tmp[1], pa[:, 0:BLK], 1.0, [pb] + CB, [tmpb[1]], tmp[0], tmpb[0],
                                   post_bias=(1.0 if t == 0 else 0.0))
                    self.tt("dve", dst[:, sl], cv[:, sl], tmp[1], ALU.mult, [cvb, tmpb[1]], [dstb[b]])
            for b in range(NB):
                for (src, srcb, dst3, dstb) in ((kT, [kTb[b]], kTM, kTMb), (vF, [vFb], vTM, vTMb)):
                    pa, pb = self.pst()
                    pv = v4(pa[:, 0:BLK])
                    for j in range(TPB):
                        i = b * TPB + j
                        self.mm(pv[:, j, :], src[:, i * 128:(i + 1) * 128], self.ident_bf, True, True,
                                srcb + CB, [pb])
                    self.cpy("act", dst3[:, b * TPB:(b + 1) * TPB, :], pv, [pb], [dstb])
            for d in range(2):
                mask = self.cc("maskf") if d == 0 else self.cc("maskb")
                strm = self.strf_bf if d == 0 else self.strb_bf
                colb = lambda a_, i: a_[:, d, i, h:h + 1]
                ubw = (lambda b: [ubb[0][b], cvb]) if d == 0 else (lambda b: [ubb[1][b]])
                qgw = (lambda b: [qgb[0][b], vFb]) if d == 0 else (lambda b: [qgb[1][b]])
                self.tt("pool", kg, kTM, self.eg[:, d, :, h:h + 1].to_broadcast([128, TT, 128]), ALU.mult,
                        [kTMb, self.decb], [kgb])
                self.tt("pool", kd[d], kTM, self.ekd[:, d, :, h:h + 1].to_broadcast([128, TT, 128]), ALU.mult,
                        [kTMb, self.decb], [kdb[d]])
                for b0 in range(0, NB, 2):
                    blks = list(range(b0, min(NB, b0 + 2)))
                    for b in blks:
                        e_i = b % 2
                        pa, pb = self.pst()
                        pq, pqb = self.pst()
                        pv, pqv = v4(pa[:, 0:BLK]), v4(pq[:, 0:BLK])
                        for j in range(TPB):
                            i = b * TPB + j
                            self.mm(pv[:, j, :], colb(self.gam, i).to_broadcast([128, 128]), self.cc("ident"),
                                    True, False, [self.decb] + CB, [pb], inc=False)
                            self.mm(pv[:, j, :], self.cc("ident"), mask, False, True, CB, [pb])
                            self.mm(pqv[:, j, :], colb(self.eg, i).to_broadcast([128, 128]), self.cc("ident"),
                                    True, True, [self.decb] + CB, [pqb])
                        for j in range(TPB):
                            i = b * TPB + j
                            self.act(E[e_i][:, j * 128:(j + 1) * 128], pv[:, j, :], AF.Exp, [pb, self.decb],
                                     [Eb[e_i]], bias=colb(self.ngam, i), scale=1.0)
                        self.tt("dve", qg[d][:, self.bsl(b)], pq[:, 0:BLK], qT[:, self.bsl(b)], ALU.mult,
                                [pqb, qTb[b]], qgw(b))
                        self.tt("pool", Es[e_i], E[e_i], strm[:, 0:BLK], ALU.mult, [Eb[e_i]] + CB, [Esb[e_i]])
                        pk, pkb = self.pst()
                        pkv = v4(pk[:, 0:BLK])
                        for j in range(TPB):
                            i = b * TPB + j
                            ks = kT[:, i * 128:(i + 1) * 128]
                            self.mm(pkv[:, j, :], ks, ks, True, True, [kTb[b]], [pkb])
                        for j in range(TPB):
                            i = b * TPB + j
                            self.stt(BTm[e_i][0][:, j * 128:(j + 1) * 128], pkv[:, j, :], colb(self.nbeta, i),
                                     Es[e_i][:, j * 128:(j + 1) * 128], ALU.mult, ALU.mult,
                                     [pkb, Esb[e_i], self.decb], [BTmb[e_i][0]])
                        pk2, pk2b = self.pst()
                        pk2v = v4(pk2[:, 0:BLK])
                        for j in range(TPB):
                            i = b * TPB + j
                            self.mm(pk2v[:, j, :], kT[:, i * 128:(i + 1) * 128], qT[:, i * 128:(i + 1) * 128],
                                    True, True, [kTb[b], qTb[b]], [pk2b])
                        self.tt("dve", qkE[d][:, self.bsl(b)], pk2[:, 0:BLK], E[e_i], ALU.mult, [pk2b, Eb[e_i]],
                                [qkEb[d][b]])
                        pt, ptb = self.pst()
                        ptv = v4(pt[:, 0:BLK])
                        for j in range(TPB):
                            self.mm(ptv[:, j, :], BTm[e_i][0][:, j * 128:(j + 1) * 128], self.ident_bf, True, True,
                                    [BTmb[e_i][0]] + CB, [ptb])
                        self.cpy("act", Bm[e_i][0], pt[:, 0:BLK], [ptb], [Bmb[e_i][0]])
                        if HIER:
                            self.tt("pool", v4(N1[e_i]), v4(Bm[e_i][0]), bc3(self.m1_bf[d]), ALU.mult,
                                    [Bmb[e_i][0]] + CB, [N1b[e_i]])
                            self.tt("pool", v4(N2[e_i]), v4(Bm[e_i][0]), bc3(self.m2_bf[d]), ALU.mult,
                                    [Bmb[e_i][0]] + CB, [N2b[e_i]])
                            self.tt("pool", v4(Bm[e_i][0]), v4(Bm[e_i][0]), bc3(self.bd_bf), ALU.mult,
                                    [Bmb[e_i][0]] + CB, [Bmb[e_i][0]])
                            self.tt("pool", v4(BTm[e_i][0]), v4(BTm[e_i][0]), bc3(self.bd_bf), ALU.mult,
                                    [BTmb[e_i][0]] + CB, [BTmb[e_i][0]])
                        self.tt("pool", Pm[e_i][0], BTm[e_i][0], self.ident4_bf[:, 0:BLK], ALU.add,
                                [BTmb[e_i][0]] + CB, [Pmb[e_i][0]])
                    for lv in range(NL):
                        g0, g1 = lv % NG, (lv + 1) % NG
                        for b in blks:
                            e_i = b % 2
                            pB, pBb = self.pst()
                            pBv = v4(pB[:, 0:BLK])
                            for j in range(TPB):
                                sl = slice(j * 128, (j + 1) * 128)
                                self.mm(pBv[:, j, :], BTm[e_i][g0][:, sl], Bm[e_i][g0][:, sl], True, True,
                                        [BTmb[e_i][g0], Bmb[e_i][g0]], [pBb])
                            if lv < NL - 1:
                                pT, pTb = self.pst()
                                pTv = v4(pT[:, 0:BLK])
                                for j in range(TPB):
                                    sl = slice(j * 128, (j + 1) * 128)
                                    self.mm(pTv[:, j, :], Bm[e_i][g0][:, sl], BTm[e_i][g0][:, sl], True, True,
                                            [BTmb[e_i][g0], Bmb[e_i][g0]], [pTb])
                            self.cpy("act", Bm[e_i][g1], pB[:, 0:BLK], [pBb], [Bmb[e_i][g1]])
                            if lv < NL - 1:
                                self.cpy("dve", BTm[e_i][g1], pT[:, 0:BLK], [pTb], [BTmb[e_i][g1]])
                        for b in blks:
                            e_i = b % 2
                            pP, pPb = self.pst()
                            pPv = v4(pP[:, 0:BLK])
                            for j in range(TPB):
                                sl = slice(j * 128, (j + 1) * 128)
                                self.mm(pPv[:, j, :], self.ident_bf, Pm[e_i][g0][:, sl], True, False,
                                        [Pmb[e_i][g0]] + CB, [pPb], inc=False)
                                self.mm(pPv[:, j, :], Bm[e_i][g1][:, sl], Pm[e_i][g0][:, sl], False, True,
                                        [Pmb[e_i][g0], Bmb[e_i][g1]], [pPb])
                            self.cpy("act" if (b % 2 == 0) else "dve", Pm[e_i][g1], pP[:, 0:BLK], [pPb],
                                     [Pmb[e_i][g1]])
                    gcur = NL % NG
                    for (Nm, Nmb) in (((N1, N1b), (N2, N2b)) if HIER else ()):
                        gn = 1 - gcur
                        for b in blks:
                            e_i = b % 2
                            pt, ptb = self.pst()
                            ptv = v4(pt[:, 0:BLK])
                            px, pxb = self.pst()
                            pxv = v4(px[:, 0:BLK])
                            for j in range(TPB):
                                sl = slice(j * 128, (j + 1) * 128)
                                self.mm(ptv[:, j, :], Pm[e_i][gcur][:, sl], self.ident_bf, True, True,
                                        [Pmb[e_i][gcur]] + CB, [ptb])
                                self.mm(pxv[:, j, :], Nm[e_i][:, sl], Pm[e_i][gcur][:, sl], True, True,
                                        [Pmb[e_i][gcur], Nmb[e_i]], [pxb])
                            self.cpy("act", Tt[e_i], pt[:, 0:BLK], [ptb], [Ttb[e_i]])
                            self.cpy("dve", Xm[e_i], px[:, 0:BLK], [pxb], [Xmb[e_i]])
                        for b in blks:
                            e_i = b % 2
                            pP, pPb = self.pst()
                            pPv = v4(pP[:, 0:BLK])
                            for j in range(TPB):
                                sl = slice(j * 128, (j + 1) * 128)
                                self.mm(pPv[:, j, :], self.ident_bf, Pm[e_i][gcur][:, sl], True, False,
                                        [Pmb[e_i][gcur]] + CB, [pPb], inc=False)
                                self.mm(pPv[:, j, :], Tt[e_i][:, sl], Xm[e_i][:, sl], False, True,
                                        [Ttb[e_i], Xmb[e_i]], [pPb])
                            self.cpy("act" if (b % 2 == 0) else "dve", Pm[e_i][gn], pP[:, 0:BLK], [pPb],
                                     [Pmb[e_i][gn]])
                        gcur = gn
                    gf = gcur
                    for b in blks:
                        e_i = b % 2
                        Pf, Pfb = Pm[e_i][gf], Pmb[e_i][gf]
                        pu, pub = self.pst()
                        puv = v4(pu[:, 0:BLK])
                        pw, pwb = self.pst()
                        pwv = v4(pw[:, 0:BLK])
                        for j in range(TPB):
                            i = b * TPB + j
                            sl = slice(j * 128, (j + 1) * 128)
                            self.mm(puv[:, j, :], Pf[:, sl], vTM[:, i, :], True, True, [Pfb, vTMb], [pub])
                            self.mm(pwv[:, j, :], kg[:, i, :], Pf[:, sl], True, True, [Pfb, kgb], [pwb])
                        self.tt("dve", ub[d][:, b * TPB:(b + 1) * TPB, :], puv,
                                self.beta[:, d, b * TPB:(b + 1) * TPB, h:h + 1].to_broadcast([128, TPB, 128]),
                                ALU.mult, [pub, self.decb], ubw(b))
                        self.cpy("act", wT[d][:, self.bsl(b)], pw[:, 0:BLK], [pwb], [wTb[d][b]])
            for d in range(2):
                S.op("pool", lambda e: e.memset(Sst[d], 0.0), writes=[Sb[d]])
                S.op("pool", lambda e: e.memset(Sbf[d], 0.0), writes=[Sbfb[d]])
            written = [False] * TT
            for n_i in range(TT):
                for d in range(2):
                    i = n_i if d == 0 else TT - 1 - n_i
                    b = i // TPB
                    sl = slice(i * 128, (i + 1) * 128)
                    vn, vnb = vnew[d][n_i % 2], vnewb[d][n_i % 2]
                    colb = lambda a_, i_: a_[:, d, i_, h:h + 1]
                    p1, p1b = self.pst()
                    self.mm(p1[:, 0:128], wT[d][:, sl], Sbf[d], True, True, [wTb[d][b], Sbfb[d]], [p1b])
                    self.stt(vn, p1[:, 0:128], colb(self.nbeta, i), ub[d][:, i, :], ALU.mult, ALU.add,
                             [p1b, ubb[d][b], self.decb] + ([cvb] if d == 0 else []), [vnb])
                    p3, p3b = self.pst()
                    self.mm(p3[:, 0:128], kd[d][:, i, :], vn, True, True, [kdb[d], vnb], [p3b])
                    p2, p2b = self.pst()
                    self.mm(p2[:, 0:128], Sbf[d], qg[d][:, sl], True, False, [Sbfb[d], qgb[d][b]] + ([vFb] if d == 0 else []),
                            [p2b], inc=False)
                    self.mm(p2[:, 0:128], vn, qkE[d][:, sl], False, True, [vnb, qkEb[d][b]], [p2b])
                    self.stt(Sst[d], Sst[d], colb(self.dl, i), p3[:, 0:128], ALU.mult, ALU.add,
                             [Sb[d], p3b, self.decb], [Sb[d]])
                    self.cpy("act", Sbf[d], Sst[d], [Sb[d]], [Sbfb[d]])
                    if not written[i]:
                        self.cpy("act", oT[:, sl], p2[:, 0:128], [p2b], [oTb[i], preb])
                        written[i] = True
                    else:
                        self.tt("dve", oT[:, sl], p2[:, 0:128], oT[:, sl], ALU.add, [p2b, oTb[i]], [oTb[i], preb])
            _, wz, wzb = wl[3]
            for b in range(NB):
                sl = self.bsl(b)
                ob_ = [oTb[i] for i in range(b * TPB, (b + 1) * TPB)]
                self.act(sqb16, oT[:, sl], AF.Square, ob_, [sqb16b])
                pa, pb = self.pst()
                self.mm(pa[:, 0:BLK], self.ones_bf, sqb16, True, True, [sqb16b] + CB, [pb])
                self.rsqrt_act(tmp[1], pa[:, 0:BLK], 1.0 / HD, [pb] + CB, [tmpb[1]], tmp[0], tmpb[0])
                self.stt(oT[:, sl], oT[:, sl], self.cc("lanorm"), tmp[1], ALU.mult, ALU.mult,
                         ob_ + [tmpb[1]] + CB, ob_ + [preb])
            for b in range(NB):
                sl = self.bsl(b)
                ob_ = [oTb[i] for i in range(b * TPB, (b + 1) * TPB)]
                o, pzb = self.proj_fm(wz, wzb, self.hT, self.hblk(b), b)
                self.act(tmp[2], o, AF.Silu, [pzb], [tmpb[2]])
                self.tt("pool", self.oaT[:, h, sl], oT[:, sl], tmp[2], ALU.mult, ob_ + [tmpb[2]], [self.oab[h][b]])

    def rope_norm(self, o, pb, wname, dst, dstb, b, tmp, tmpb, t16, t16b):
        BLK, CB = self.BLK, self.CB
        self.act(t16[0], o, AF.Square, [pb], [t16b[0]])
        pa, pab = self.pst()
        self.mm(pa[:, 0:BLK], self.ones_bf, t16[0], True, True, [t16b[0]] + CB, [pab])
        self.rsqrt_act(tmp[1], pa[:, 0:BLK], 1.0 / HD, [pab] + CB, [tmpb[1]], tmp[0], tmpb[0])
        self.stt(tmp[0], o, self.cc(wname), tmp[1], ALU.mult, ALU.mult, [pb, tmpb[1], tmpb[0]] + CB, [tmpb[0]])
        self.cpy("act", t16[1], tmp[0], [tmpb[0]], [t16b[1]])
        ps_, psb_ = self.pst()
        self.mm(ps_[:, 0:BLK], self.perm_bf, t16[1], True, True, [t16b[1]] + CB, [psb_])
        nr = BLK // 64
        r0 = b * nr
        cosr = self.cc("cosr")[0:64, r0:r0 + nr].unsqueeze(2).to_broadcast([64, nr, 64])
        sinr = self.cc("sinr")[0:64, r0:r0 + nr].unsqueeze(2).to_broadcast([64, nr, 64])
        cosc = self.cc("cosc")[64:128, :].unsqueeze(1).to_broadcast([64, nr, 64])
        sinc = self.cc("sinc")[64:128, :].unsqueeze(1).to_broadcast([64, nr, 64])
        r3 = lambda a, lo: a[lo:lo + 64, 0:BLK].rearrange("p (r c) -> p r c", c=64)
        self.tt("pool", r3(tmp[1], 0), r3(tmp[0], 0), cosr, ALU.mult, [tmpb[0], tmpb[1]] + CB, [tmpb[1]])
        self.tt("pool", r3(tmp[1], 64), r3(tmp[0], 64), cosc, ALU.mult, [tmpb[0], tmpb[1]] + CB, [tmpb[1]])
        self.tt("dve", r3(tmp[2], 0), r3(ps_, 0), sinr, ALU.mult, [psb_] + CB, [tmpb[2]])
        self.tt("dve", r3(tmp[2], 64), r3(ps_, 64), sinc, ALU.mult, [psb_] + CB, [tmpb[2]])
        self.tt("pool", dst, tmp[1], tmp[2], ALU.add, [tmpb[1], tmpb[2]], dstb)

    def mixer_b(self):
        S, T, TT, BLK, NB, TPB, CB = self.S, self.T, self.TT, self.BLK, self.NB, self.TPB, self.CB
        kTr = self.bf(2 * T).rearrange("p (g t) -> p g t", g=2)
        kTrb = [[self.buf("kTr") for _ in range(NB)] for _ in range(2)]
        vB = self.bf(TT * 256).rearrange("p (i c) -> p i c", c=256)
        vBb = self.buf("vB")
        wv = self.bf(8 * 256).rearrange("p (k n) -> p k n", k=8); wvb = self.buf("wv")
        qTr = [self.bf(BLK) for _ in range(2)]; qTrb = [self.buf("qTr") for _ in range(2)]
        NE = 4
        Ej = [self.bf(BLK) for _ in range(NE)]; Ejb = [self.buf("Ej") for _ in range(NE)]
        tmp = [self.f32(BLK) for _ in range(3)]; tmpb = [self.buf("tmpB") for _ in range(3)]
        t16 = [self.bf(BLK) for _ in range(2)]; t16b = [self.buf("t16") for _ in range(2)]
        S.dma("sp", wv, self.wb_d["w_in"].rearrange("(k p) n -> p k n", p=128)[:, :, C_VB:C_VB + 256],
              reads=[self.wb_buf["w_in"]], writes=[wvb])
        items = [("w_in", C_KB + g * 128) for g in range(2)] + [("w_in", C_QB + hq * 128) for hq in range(8)]
        pf = self.prefetch(items, LA=3)
        for g in range(2):
            _, w, wb = next(pf)
            for b in range(NB):
                o, pb = self.proj_fm(w, wb, self.hT, self.hblk(b), b)
                self.rope_norm(o, pb, "knorm", kTr[:, g, self.bsl(b)], [kTrb[g][b]], b, tmp, tmpb, t16, t16b)
        for i in range(TT):
            pa, pb = self.pst()
            for k in range(KC):
                self.mm(pa[:, 0:256], self.hT[:, k, i * 128:(i + 1) * 128], wv[:, k, :], k == 0, k == KC - 1,
                        [self.hTb[i], wvb], [pb], inc=(k == KC - 1))
            self.cpy("act", vB[:, i, :], pa[:, 0:256], [pb], [vBb])
        self.ps_base, self.ps_n = 4, 4
        accs = [((self.ps[0][:], self.psb[0]), (self.ps[1][:], self.psb[1])),
                ((self.ps[2][:], self.psb[2]), (self.ps[3][:], self.psb[3]))]
        seq = [(hq, b) for hq in range(8) for b in range(NB)]
        wq = {}

        def prep(n):
            hq, b = seq[n]
            if b == 0:
                wq[hq] = next(pf)[1:]
            w, wb = wq[hq]
            o, pb = self.proj_fm(w, wb, self.hT, self.hblk(b), b)
            self.rope_norm(o, pb, "qnorm", qTr[n % 2], [qTrb[n % 2]], b, tmp, tmpb, t16, t16b)

        prep(0)
        for n in range(len(seq)):
            hq, b = seq[n]
            g = hq // 4
            qi = n % 2
            (acc0, acc0b), (acc1, acc1b) = accs[n % 2]

            def issue_sc(j):
                ei = j % NE
                sc, scb = self.pst()
                self.mm(sc[:, 0:BLK], kTr[:, g, j * 128:(j + 1) * 128], qTr[qi], True, True,
                        [kTrb[g][j // TPB], qTrb[qi]], [scb])
                self.act(Ej[ei], sc[:, 0:BLK], AF.Exp, [scb], [Ejb[ei]], scale=HD ** -0.5)

            issue_sc(0)
            if TT > 1:
                issue_sc(1)
            for j in range(TT):
                if j + 2 < TT:
                    issue_sc(j + 2)
                ei = j % NE
                self.mm(acc0[:, 0:BLK], vB[:, j, g * 128:(g + 1) * 128], Ej[ei], j == 0, j == TT - 1,
                        [vBb, Ejb[ei]], [acc0b], inc=(j == TT - 1))
                self.mm(acc1[:, 0:BLK], self.ones_bf, Ej[ei], j == 0, j == TT - 1,
                        [Ejb[ei]] + CB, [acc1b])
            if n + 1 < len(seq):
                prep(n + 1)
            rt, rtb = tmp[n % 2], tmpb[n % 2]
            self.act(rt, acc1[:, 0:BLK], AF.Ln, [acc1b], [rtb])
            self.act(rt, rt, AF.Exp, [rtb], [rtb], scale=-1.0)
            self.tt("dve", self.obT[:, hq, self.bsl(b)], acc0[:, 0:BLK], rt, ALU.mult,
                    [acc0b, rtb], [self.obb[hq][b]])
        self.ps_base, self.ps_n = 2, 6

    def post_tile(self, z0, z0b, z1, z1b, pnw, pnwb, i, tmp, tmpb):
        r = self.nrr
        self.nrr ^= 1
        st, stb = self.st[r], self.stb[r]
        self.act(self.junk[:, 0:512], z0, AF.Square, [z0b], [self.junkb, stb], accum_out=st[:, 0:1])
        self.act(self.junk[:, 512:1024], z1, AF.Square, [z1b], [self.junkb, stb], accum_out=st[:, 1:2])
        self.tt("dve", st[:, 2:3], st[:, 0:1], st[:, 1:2], ALU.add, [stb], [stb])
        self.act(st[:, 3:4], st[:, 2:3], AF.Ln, [stb] + self.CB, [stb], scale=1.0 / D, bias=self.epsc)
        self.act(st[:, 4:5], st[:, 3:4], AF.Exp, [stb], [stb], scale=-0.5)
        for hf, (z, zb) in enumerate(((z0, z0b), (z1, z1b))):
            self.stt(self.ptmp[hf], z, st[:, 4:5], pnw[:, hf * 512:(hf + 1) * 512], ALU.mult, ALU.mult,
                     [zb, stb, pnwb], [self.ptmpb[hf]])
            xs = self.xTM[:, i, hf * 512:(hf + 1) * 512]
            self.tt("pool", xs, xs, self.ptmp[hf], ALU.add, [self.xb[i], self.ptmpb[hf]], [self.xb[i]])

    def out_proj_tile(self, lhs3, lhsb, col0, wsb, wsbb, nk, pnw, pnwb, i, tmp, tmpb):
        accs = ((self.ps[0][:], self.psb[0]), (self.ps[1][:], self.psb[1]))
        for hf in range(2):
            z, zb = accs[hf]
            for k in range(nk):
                self.mm(z, lhs3[:, k, col0:col0 + 128], wsb[:, k, hf * 512:(hf + 1) * 512], k == 0, k == nk - 1,
                        lhsb + [wsbb], [zb], inc=(k == nk - 1))
        self.post_tile(accs[0][0], accs[0][1], accs[1][0], accs[1][1], pnw, pnwb, i, tmp, tmpb)

    def phase_merge(self, s):
        S, T, TT, BLK, NB, TPB, CB = self.S, self.T, self.TT, self.BLK, self.NB, self.TPB, self.CB
        mixT = self.bf(8 * T).rearrange("p (k t) -> p k t", k=8)
        mixb = [[self.buf("mix") for _ in range(NB)] for _ in range(8)]
        wout = self.bf(8 * D).rearrange("p (k n) -> p k n", k=8); woutb = self.buf("wout")
        pnw = self.f32(D); pnwb = self.buf("pnw")
        tmp = [self.f32(BLK) for _ in range(4)]; tmpb = [self.buf("tmpM") for _ in range(4)]
        S.dma("sp", wout, self.wb_d["w_out"].rearrange("(k p) n -> p k n", p=128), reads=[self.wb_buf["w_out"]],
              writes=[woutb])
        S.dma("sp", pnw, self.pnw_d[0], writes=[pnwb])
        items = []
        for m in range(8):
            items += [("w_out_a", m * 128), ("w_in", C_G + m * 128), ("w_out_b", m * 128), ("w_in", C_G + D + m * 128)]
        pf = self.prefetch(items, LA=1)
        bg = self.cc("bgate")
        for m in range(8):
            wa, wab_ = next(pf)[1:]
            wga, wgab = next(pf)[1:]
            wbm, wbb = next(pf)[1:]
            wgb, wgbb = next(pf)[1:]
            for b in range(NB):
                ya, yab = self.proj_fm(wa, wab_, self.oaT, [self.oab[k][b] for k in range(8)], b)
                ga, gab = self.proj_fm(wga, wgab, self.hT, self.hblk(b), b)
                self.act(tmp[0], ga, AF.Sigmoid, [gab] + CB, [tmpb[0]], bias=bg[:, m:m + 1])
                self.tt("dve", tmp[1], ya, tmp[0], ALU.mult, [yab, tmpb[0]], [tmpb[1]])
                yb, ybb = self.proj_fm(wbm, wbb, self.obT, [self.obb[k][b] for k in range(8)], b)
                gb, gbb = self.proj_fm(wgb, wgbb, self.hT, self.hblk(b), b)
                self.act(tmp[2], gb, AF.Sigmoid, [gbb] + CB, [tmpb[2]], bias=bg[:, 8 + m:9 + m])
                self.tt("dve", tmp[3], yb, tmp[2], ALU.mult, [ybb, tmpb[2]], [tmpb[3]])
                self.tt("pool", mixT[:, m, self.bsl(b)], tmp[1], tmp[3], ALU.add, [tmpb[1], tmpb[3]], [mixb[m][b]])
        if self.dbg and s == 0:
            for (src, dstd, bb) in ((self.oaT, self.dbg_oa, self.oab), (self.obT, self.dbg_ob, self.obb)):
                for k in range(8):
                    for b in range(NB):
                        self.cpy("dve", tmp[0], src[:, k, self.bsl(b)], [bb[k][b]], [tmpb[0]])
                        S.dma("sp", dstd[:, k, self.bsl(b)], tmp[0], reads=[tmpb[0]], is_output=True)
        S.barrier()
        for i in range(TT):
            S.dma("sp", self.xTM[:, i, :], self.x_d[s, i * 128:(i + 1) * 128, :], writes=[self.xb[i]])
        for i in range(TT):
            b = i // TPB
            self.out_proj_tile(mixT, [mixb[k][b] for k in range(8)], i * 128, wout, woutb, 8, pnw, pnwb, i, tmp, tmpb)
        if self.dbg and s == 0:
            for i in range(TT):
                S.dma("sp", self.dbg_x1[i * 128:(i + 1) * 128, :], self.xTM[:, i, :], reads=[self.xb[i]], is_output=True)

    def phase_mem(self, s):
        S, T, TT, BLK, NB, TPB, CB = self.S, self.T, self.TT, self.BLK, self.NB, self.TPB, self.CB
        memT = self.bf(8 * MEM).rearrange("p (k t) -> p k t", k=8); memTb = [self.buf("memT") for _ in range(2)]
        kmT = self.bf(8 * MEM).rearrange("p (k t) -> p k t", k=8); kmTb = self.buf("kmT")
        vm = self.bf(2 * D).rearrange("p (i c) -> p i c", c=D); vmb = self.buf("vm")
        wbig = self.bf(8 * D).rearrange("p (k n) -> p k n", k=8); wbigb = self.buf("wbig")
        pnw = self.f32(D); pnwb = self.buf("pnw2")
        qmT = self.bf(8 * BLK).rearrange("p (k t) -> p k t", k=8); qmTb = [self.buf("qmT") for _ in range(8)]
        omT = self.bf(8 * BLK).rearrange("p (k t) -> p k t", k=8); omTb = [self.buf("omT") for _ in range(8)]
        Ej = [self.bf(BLK) for _ in range(4)]; Ejb = [self.buf("Em") for _ in range(4)]
        tmp = [self.f32(BLK) for _ in range(2)]; tmpb = [self.buf("tmpC") for _ in range(2)]
        S.dma("sp", wbig, self.wb_d["w_mkv"].rearrange("(k p) n -> p k n", p=128)[:, :, D:2 * D],
              reads=[self.wb_buf["w_mkv"]], writes=[wbigb])
        S.dma("sp", pnw, self.pnw_d[1], writes=[pnwb])
        for mt in range(2):
            r = mt % 2
            S.dma("sp", self.xin[r], self.mem_d[s, mt * 128:(mt + 1) * 128, :], writes=[self.xinb[r]])
            self.norm_tile(self.xin[r], self.xinb[r], self.cc("wpre_kv"), memT, memTb[mt], mt * 128)
        items = [("w_mkv", c * 128) for c in range(8)]
        for _ in range(NB):
            items += [("w_mq", c * 128) for c in range(8)]
        pf = self.prefetch(items, LA=4)
        for c in range(8):
            _, w, wb = next(pf)
            pa, pb = self.pst()
            for k in range(KC):
                self.mm(pa[:, 0:MEM], w[:, k, :], memT[:, k, :], k == 0, k == KC - 1, [wb] + memTb, [pb],
                        inc=(k == KC - 1))
            self.cpy("act", kmT[:, c, :], pa[:, 0:MEM], [pb], [kmTb])
        for mt in range(2):
            for hf in range(2):
                pa, pb = self.pst()
                for k in range(KC):
                    self.mm(pa, memT[:, k, mt * 128:(mt + 1) * 128], wbig[:, k, hf * 512:(hf + 1) * 512], k == 0,
                            k == KC - 1, [memTb[mt], wbigb], [pb], inc=(k == KC - 1))
                self.cpy("act", vm[:, mt, hf * 512:(hf + 1) * 512], pa, [pb], [vmb])
        S.dma("sp", wbig, self.wb_d["w_mo"].rearrange("(k p) n -> p k n", p=128), reads=[self.wb_buf["w_mo"]],
              writes=[wbigb])
        self.norm_from_x("wpre_mem")
        for b in range(NB):
            for c in range(8):
                _, w, wb = next(pf)
                o, pb = self.proj_fm(w, wb, self.hT, self.hblk(b), b)
                self.cpy("act", qmT[:, c, :], o, [pb], [qmTb[c]])
            for hm in range(4):
                for mt in range(2):
                    sc, scb = self.pst()
                    for dc in range(2):
                        c = 2 * hm + dc
                        self.mm(sc[:, 0:BLK], kmT[:, c, mt * 128:(mt + 1) * 128], qmT[:, c, :], dc == 0, dc == 1,
                                [kmTb, qmTb[c]], [scb], inc=(dc == 1))
                    ei = (hm % 2) * 2 + mt
                    self.act(Ej[ei], sc[:, 0:BLK], AF.Exp, [scb], [Ejb[ei]], scale=256 ** -0.5)
                e0, e1 = (hm % 2) * 2, (hm % 2) * 2 + 1
                sm, smb = self.pst()
                for mt, ei in ((0, e0), (1, e1)):
                    self.mm(sm[:, 0:BLK], self.ones_bf, Ej[ei], mt == 0, mt == 1, [Ejb[ei]] + CB, [smb], inc=(mt == 1))
                self.act(tmp[0], sm[:, 0:BLK], AF.Ln, [smb], [tmpb[0]])
                self.act(tmp[0], tmp[0], AF.Exp, [tmpb[0]], [tmpb[0]], scale=-1.0)
                for ec in range(2):
                    c = 2 * hm + ec
                    po, pob = self.pst()
                    for mt, ei in ((0, e0), (1, e1)):
                        self.mm(po[:, 0:BLK], vm[:, mt, c * 128:(c + 1) * 128], Ej[ei], mt == 0, mt == 1,
                                [vmb, Ejb[ei]], [pob], inc=(mt == 1))
                    self.tt("dve", omT[:, c, :], po[:, 0:BLK], tmp[0], ALU.mult, [pob, tmpb[0]], [omTb[c]])
            for j in range(TPB):
                i = b * TPB + j
                self.out_proj_tile(omT, omTb, j * 128, wbig, wbigb, 8, pnw, pnwb, i, tmp, tmpb)
        if self.dbg and s == 0:
            for i in range(TT):
                S.dma("sp", self.dbg_x2[i * 128:(i + 1) * 128, :], self.xTM[:, i, :], reads=[self.xb[i]], is_output=True)

    def phase_ffn(self, s):
        S, T, TT, BLK, NB, TPB, CB = self.S, self.T, self.TT, self.BLK, self.NB, self.TPB, self.CB
        wfo = self.bf(FC * D).rearrange("p (k n) -> p k n", k=FC); wfob = self.buf("wfo")
        aT = self.bf(FC * BLK).rearrange("p (k t) -> p k t", k=FC); aTb = [self.buf("aT") for _ in range(FC)]
        pnw = self.f32(D); pnwb = self.buf("pnw3")
        tmp = [self.f32(BLK) for _ in range(2)]; tmpb = [self.buf("tmpF") for _ in range(2)]
        S.dma("sp", wfo, self.wb_d["w_ffn_out"].rearrange("(k p) n -> p k n", p=128), reads=[self.wb_buf["w_ffn_out"]],
              writes=[wfob])
        S.dma("sp", pnw, self.pnw_d[2], writes=[pnwb])
        self.norm_from_x("wpre_ffn")
        items = []
        for _ in range(NB):
            for fc in range(FC):
                items += [("w_ffn_in", fc * 128), ("w_ffn_in", DFF + fc * 128)]
        pf = self.prefetch(items, LA=3)
        for b in range(NB):
            for fc in range(FC):
                wg, wgb = next(pf)[1:]
                wu, wub = next(pf)[1:]
                g, gb = self.proj_fm(wg, wgb, self.hT, self.hblk(b), b)
                u, ub_ = self.proj_fm(wu, wub, self.hT, self.hblk(b), b)
                t_i = fc % 2
                self.act(tmp[t_i], g, AF.Silu, [gb], [tmpb[t_i]])
                self.tt("dve", aT[:, fc, :], u, tmp[t_i], ALU.mult, [ub_, tmpb[t_i]], [aTb[fc]])
            for j in range(TPB):
                i = b * TPB + j
                self.out_proj_tile(aT, aTb, j * 128, wfo, wfob, FC, pnw, pnwb, i, tmp, tmpb)
                S.dma("sp", self.out_d[s, i * 128:(i + 1) * 128, :], self.xTM[:, i, :], reads=[self.xb[i]],
                      is_output=True)

    def emit(self):
        self.setup()
        for s in range(self.NSEQ):
            self.newphase()
            self.alloc_norm_tmps(True)
            self.norm_from_dram(s, "wpre_mix")
            self.phase_decay()
            self.newphase()
            self.mixer_a()
            self.newphase()
            self.mixer_b()
            self.newphase()
            self.alloc_norm_tmps(False)
            self.phase_merge(s)
            self.newphase()
            self.alloc_norm_tmps(True)
            self.phase_mem(s)
            self.newphase()
            self.alloc_norm_tmps(False)
            self.phase_ffn(s)
        self.S.finish()
        self.es.close()
        return self.nc


_PROG_CACHE = {}


def _host_inputs(inputs, T):
    cp, cbf = _make_cpack(inputs, T)
    pnw = np.stack([np.tile(np.asarray(inputs[n][0], np.float32).reshape(1, D), (128, 1))
                    for n in ("mix_post_norm", "mem_post_norm", "ffn_post_norm")], 0)
    shared = {"cpack": cp, "cbf": cbf, "pnw": np.ascontiguousarray(pnw)}
    for n, r, c in W_SPECS:
        shared[n] = np.ascontiguousarray(np.asarray(inputs[n][0], np.float32))
    return shared


def run(inputs, n_cores=N_CORES, dbg=False, inv32=False):
    x = np.asarray(inputs["x"], np.float32)
    mem = np.asarray(inputs["mem"], np.float32)
    B, T, _ = x.shape
    assert B % n_cores == 0
    nseq = B // n_cores
    key = (nseq, T, dbg, inv32)
    if key not in _PROG_CACHE:
        _PROG_CACHE[key] = Kern(nseq, T, dbg=dbg, inv32=inv32).emit()
    nc = _PROG_CACHE[key]
    shared = _host_inputs(inputs, T)
    in_maps = []
    for c in range(n_cores):
        m = dict(shared)
        m["x"] = np.ascontiguousarray(x[c * nseq:(c + 1) * nseq])
        m["mem"] = np.ascontiguousarray(mem[c * nseq:(c + 1) * nseq])
        in_maps.append(m)
    res = run_bass_kernel_spmd(nc, in_maps, core_ids=list(range(n_cores)))
    out = np.concatenate([np.asarray(r["out"], np.float32) for r in res.results], axis=0)
    return out, res


def kernel(**inputs):
    out, _ = run(inputs)
    return out
```

```python
import math
import os
from contextlib import ExitStack
import numpy as np
import ml_dtypes
import concourse.bass as bass
import concourse.mybir as mybir
from concourse.alu_op_type import AluOpType as ALU
from concourse.bass_utils import run_bass_kernel_spmd

F32 = mybir.dt.float32
BF16 = mybir.dt.bfloat16
AF = mybir.ActivationFunctionType

D = 1024
KC = 8
HD = 128
NH = 8
DFF = 2816
FC = DFF // 128
MEM = 256
IN_DIM = 7712
C_QKV, C_Z, C_AB, C_QB, C_KB, C_VB, C_G = 0, 3072, 4096, 4128, 5152, 5408, 5664
EPS = 1e-6
NEG = -1.0e5
N_CORES = 8


class Buf:
    __slots__ = ("name", "writer", "readers", "dsem", "dcount")

    def __init__(self, name):
        self.name = name
        self.writer = None
        self.readers = {}
        self.dsem = None
        self.dcount = 0


class Sched:
    SAME_ENGINE_SYNC = True

    def __init__(self, nc, es):
        self.nc = nc
        self.es = es
        self.engs = {"pe": nc.tensor, "act": nc.scalar, "dve": nc.vector,
                     "pool": nc.gpsimd, "sp": nc.sync}
        self.sem = {}
        self.count = {}
        for e in self.engs:
            self.sem[e] = es.enter_context(nc.semaphore("sem_" + e))
            self.count[e] = 0
        self.pending = {e: False for e in self.engs}
        self.waited = {e: {} for e in self.engs}
        self.n_inst = 0
        self.n_wait = 0
        self.out_tokens = []
        self.dsems = {}

    def _wait(self, e, tok):
        key, sem, val, src = tok
        if src == e and (e == "pe" or e == "sp" or not self.SAME_ENGINE_SYNC):
            return
        w = self.waited[e]
        if w.get(key, 0) >= val:
            return
        self.engs[e].wait_ge(sem, val)
        self.n_wait += 1
        w[key] = val

    def _deps(self, e, reads, writes):
        for b in reads:
            if b.writer is not None:
                self._wait(e, b.writer)
        for b in writes:
            if b.writer is not None:
                self._wait(e, b.writer)
            for tok in b.readers.values():
                self._wait(e, tok)

    def _commit(self, tok, reads, writes):
        for b in reads:
            b.readers[tok[0]] = tok
        for b in writes:
            b.writer = tok
            b.readers = {}

    def op(self, e, fn, reads=(), writes=(), inc=True):
        self._deps(e, reads, writes)
        inst = fn(self.engs[e])
        if inc:
            self.count[e] += 1
            inst.then_inc(self.sem[e], 1)
            tok = (e, self.sem[e], self.count[e], e)
            self.pending[e] = False
        else:
            tok = (e, self.sem[e], self.count[e] + 1, e)
            self.pending[e] = True
        self._commit(tok, reads, writes)
        self.n_inst += 1
        return inst

    def dma(self, q, out_ap, in_ap, reads=(), writes=(), sembuf=None, is_output=False, **kw):
        self._deps(q, reads, writes)
        b = sembuf if sembuf is not None else (writes[0] if writes else reads[0])
        if b.dsem is None:
            b.dsem = self.es.enter_context(self.nc.semaphore("d_" + b.name))
            self.dsems[b.name] = b
        b.dcount += 16
        inst = self.engs[q].dma_start(out=out_ap, in_=in_ap, **kw)
        inst.then_inc(b.dsem, 16)
        tok = ("d_" + b.name, b.dsem, b.dcount, None)
        self._commit(tok, reads, writes)
        if is_output:
            self.out_tokens.append(tok)
        self.n_inst += 1
        return inst

    def barrier(self):
        assert not any(self.pending.values())
        toks = [(e, self.sem[e], self.count[e], e) for e in self.engs if self.count[e] > 0]
        for b in self.dsems.values():
            if not b.name.startswith("wb_"):
                toks.append(("d_" + b.name, b.dsem, b.dcount, None))
        for e in self.engs:
            for tok in toks:
                if tok[3] != e:
                    self._wait(e, tok)

    def finish(self):
        last = {}
        for tok in self.out_tokens:
            last[tok[0]] = tok
        for tok in last.values():
            self._wait("sp", tok)


def _cpack_layout(T):
    ent = [("ident", 128), ("ones", 128), ("ucum", 128), ("lcum", 128),
           ("maskf", 128), ("maskb", 128),
           ("cosr", T // 64), ("sinr", T // 64), ("cosc", 64), ("sinc", 64),
           ("wpre_mix", 8), ("wpre_mem", 8), ("wpre_ffn", 8), ("wpre_kv", 8),
           ("convw", 24 * 5), ("lanorm", 1), ("qnorm", 1), ("knorm", 1), ("bgate", 16),
           ("alog", 16), ("dtb", 16)]
    off, o = {}, 0
    for n, c in ent:
        off[n] = (o, c)
        o += c
    return off, o


def _make_cpack(p, T):
    off, ncol = _cpack_layout(T)
    cp = np.zeros((128, ncol), np.float32)

    def put(name, arr):
        o, c = off[name]
        cp[:, o:o + c] = np.asarray(arr, np.float32).reshape(128, c)

    idx = np.arange(128)
    put("ident", np.eye(128))
    put("ones", np.ones((128, 128)))
    partner = np.where((idx % 64) < 32, idx + 32, idx - 32)
    pm = np.zeros((128, 128), np.float32)
    pm[partner, idx] = 1.0
    pp, ff = idx[:, None], idx[None, :]
    put("ucum", (pp <= ff))
    put("lcum", (pp >= ff))
    put("maskf", np.where(ff >= pp, 0.0, NEG))
    put("maskb", np.where(ff <= pp, 0.0, NEG))
    pb_, fb_ = pp // 32, ff // 32
    bd = (pb_ == fb_)
    m1f = ((pb_ == 1) & (fb_ == 0)) | ((pb_ == 3) & (fb_ == 2))
    m2f = (pb_ >= 2) & (fb_ <= 1)
    cbf = np.concatenate([np.eye(128), np.ones((128, 128)), pm, np.tile(np.eye(128), (1, 4)),
                          np.tile((ff > pp).astype(np.float32), (1, 4)),
                          np.tile((ff < pp).astype(np.float32), (1, 4)),
                          bd, m1f, m2f, m1f.T, m2f.T], axis=1).astype(ml_dtypes.bfloat16)
    half = HD // 2
    inv_freq = (np.float32(10000.0) ** (-np.arange(0, half, 2, dtype=np.float32) / np.float32(half))).astype(np.float32)
    rows = np.arange(T // 64, dtype=np.float32)
    cols = np.arange(64, dtype=np.float32)
    cosr = np.zeros((128, T // 64), np.float32); sinr = np.zeros_like(cosr)
    cosc = np.zeros((128, 64), np.float32); sinc = np.zeros_like(cosc)
    for d in range(64):
        i = d % 32
        sg = -1.0 if d < 32 else 1.0
        ang_r = (rows * inv_freq[i]).astype(np.float32)
        ang_c = (cols * inv_freq[i]).astype(np.float32)
        cosr[d] = np.cos(ang_r); sinr[d] = sg * np.sin(ang_r)
        cosc[64 + d] = np.cos(ang_c); sinc[64 + d] = sg * np.sin(ang_c)
    put("cosr", cosr); put("sinr", sinr); put("cosc", cosc); put("sinc", sinc)
    fm = lambda v: np.asarray(v, np.float32).reshape(-1, 128).T
    put("wpre_mix", fm(p["mix_pre_norm"][0]))
    put("wpre_mem", fm(p["mem_pre_norm"][0]))
    put("wpre_ffn", fm(p["ffn_pre_norm"][0]))
    put("wpre_kv", fm(p["mem_kv_norm"][0]))
    cw = np.asarray(p["conv_w"][0], np.float32)
    put("convw", cw.reshape(5, 24, 128).transpose(2, 1, 0))
    put("lanorm", np.asarray(p["la_norm_w"][0]).reshape(128, 1))
    put("qnorm", np.asarray(p["q_norm_w"][0]).reshape(128, 1))
    put("knorm", np.asarray(p["k_norm_w"][0]).reshape(128, 1))
    put("bgate", fm(p["b_gate"][0]))
    put("alog", np.tile(np.asarray(p["la_a_log"][0], np.float32).reshape(1, 16), (128, 1)))
    put("dtb", np.tile(np.asarray(p["la_dt_bias"][0], np.float32).reshape(1, 16), (128, 1)))
    return cp, np.ascontiguousarray(cbf)


W_SPECS = [("w_in", D, IN_DIM), ("w_out_a", D, D), ("w_out_b", D, D), ("w_out", D, D),
           ("w_mq", D, D), ("w_mkv", D, 2 * D), ("w_mo", D, D),
           ("w_ffn_in", D, 2 * DFF), ("w_ffn_out", DFF, D)]


class Kern:
    def __init__(self, NSEQ, T, dbg=False, inv32=False):
        self.NSEQ, self.T, self.dbg, self.inv32 = NSEQ, T, dbg, inv32
        self.TT = T // 128
        self.BLK = min(512, T)
        self.NB = T // self.BLK
        self.TPB = self.BLK // 128
        self.nc = nc = bass.Bass("TRN2", target_bir_lowering=False)
        self.es = es = ExitStack()
        self.S = Sched(nc, es)
        self.off, self.ncol = _cpack_layout(T)
        dt = nc.dram_tensor
        self.x_d = dt("x", [NSEQ, T, D], F32, kind="ExternalInput").ap()
        self.mem_d = dt("mem", [NSEQ, MEM, D], F32, kind="ExternalInput").ap()
        self.cp_d = dt("cpack", [128, self.ncol], F32, kind="ExternalInput").ap()
        self.pnw_d = dt("pnw", [3, 128, D], F32, kind="ExternalInput").ap()
        self.cbf_d = dt("cbf", [128, 2560], BF16, kind="ExternalInput").ap()
        self.out_d = dt("out", [NSEQ, T, D], F32, kind="ExternalOutput").ap()
        self.w_d, self.wb_d, self.wb_buf = {}, {}, {}
        for n, r, c in W_SPECS:
            self.w_d[n] = dt(n, [r, c], F32, kind="ExternalInput").ap()
            self.wb_d[n] = dt(n + "_bf", [r, c], BF16, kind="Internal").ap()
            self.wb_buf[n] = Buf("wb_" + n)
        if dbg:
            self.dbg_oa = dt("dbg_oa", [128, 8, T], F32, kind="ExternalOutput").ap()
            self.dbg_ob = dt("dbg_ob", [128, 8, T], F32, kind="ExternalOutput").ap()
            self.dbg_x1 = dt("dbg_x1", [T, D], F32, kind="ExternalOutput").ap()
            self.dbg_x2 = dt("dbg_x2", [T, D], F32, kind="ExternalOutput").ap()
        self.ARENA = 53200
        ARENA_OFF = 16544
        self.R32W = 12 * self.BLK + 128
        self.LIM = self.ARENA
        self.arena = es.enter_context(nc.sbuf_tensor("arena", [128, self.ARENA], F32))
        self.top = 0
        self.topB = self.LIM
        self.no_tail = True
        self.ps = [es.enter_context(nc.psum_tensor("ps%d" % i, [128, 512], F32)) for i in range(8)]
        self.psb = [Buf("ps%d" % i) for i in range(8)]
        self.ps_rr = 0
        self.ps_base, self.ps_n = 2, 6
        self.uid = 0

    def f32(self, n):
        if self.top + n <= self.LIM:
            a = self.arena[:, self.top:self.top + n]
            self.top += n
            return a
        assert not self.no_tail and self.topB + n <= self.ARENA, ("SBUF arena overflow", self.top, self.topB, n)
        a = self.tailf[:, self.topB - self.LIM:self.topB - self.LIM + n]
        self.topB += n
        return a

    def bf(self, n):
        assert n % 2 == 0
        return self.f32(n // 2).bitcast(BF16)

    def buf(self, name):
        self.uid += 1
        return Buf("%s_%d" % (name, self.uid))

    def pst(self):
        i = self.ps_base + (self.ps_rr % self.ps_n)
        self.ps_rr += 1
        return self.ps[i][:], self.psb[i]

    def cc(self, name):
        o, c = self.off[name]
        return self.cp[:, o:o + c]

    def mm(self, out, lhsT, rhs, start, stop, R, W, inc=True):
        self.S.op("pe", lambda e: e.matmul(out, lhsT=lhsT, rhs=rhs, start=start, stop=stop),
                  reads=R, writes=W, inc=inc)

    def act(self, out, in_, func, R, W, **kw):
        self.S.op("act", lambda e: e.activation(out=out, in_=in_, func=func, **kw), reads=R, writes=W)

    def tt(self, eng, out, in0, in1, op, R, W):
        self.S.op(eng, lambda e: e.tensor_tensor(out=out, in0=in0, in1=in1, op=op), reads=R, writes=W)

    def stt(self, out, in0, scalar, in1, op0, op1, R, W):
        self.S.op("dve", lambda e: e.scalar_tensor_tensor(out=out, in0=in0, scalar=scalar, in1=in1,
                                                           op0=op0, op1=op1), reads=R, writes=W)

    def cpy(self, eng, out, in_, R, W):
        if eng == "act":
            self.act(out, in_, AF.Copy, R, W)
        else:
            self.S.op(eng, lambda e: e.tensor_copy(out=out, in_=in_), reads=R, writes=W)

    def rsqrt_act(self, out, in_, scale, R, W, tmp, tmpb, post_bias=0.0):
        self.act(tmp, in_, AF.Ln, R, [tmpb], scale=scale, bias=self.epsc)
        if post_bias == 0.0:
            self.act(out, tmp, AF.Exp, [tmpb], W, scale=-0.5)
        else:
            self.act(out, tmp, AF.Exp, [tmpb], W, scale=-0.5, bias=self.qsc)

    def setup(self):
        S = self.S
        self.cp = self.f32(self.ncol)
        self.cpb = Buf("cpack")
        S.dma("sp", self.cp, self.cp_d[:, :], writes=[self.cpb])
        for n, r, c in W_SPECS:
            for c0 in range(0, c, 2048):
                c1 = min(c, c0 + 2048)
                S.dma("pool", self.wb_d[n][:, c0:c1], self.w_d[n][:, c0:c1],
                      writes=[self.wb_buf[n]], sembuf=self.wb_buf[n])
        cbf = self.bf(2560)
        self.cbb = Buf("constbf")
        S.dma("sp", cbf, self.cbf_d[:, :], writes=[self.cbb])
        self.ident_bf = cbf[:, 0:128]; self.ones_bf = cbf[:, 128:256]; self.perm_bf = cbf[:, 256:384]
        self.ident4_bf = cbf[:, 384:896]; self.strf_bf = cbf[:, 896:1408]; self.strb_bf = cbf[:, 1408:1920]
        self.bd_bf = cbf[:, 1920:2048]
        self.m1_bf = [cbf[:, 2048:2176], cbf[:, 2304:2432]]
        self.m2_bf = [cbf[:, 2176:2304], cbf[:, 2432:2560]]
        cb, R = self.cbb, [self.cpb]
        sm = self.f32(24)
        self.epsc = sm[:, 0:1]; self.qsc = sm[:, 1:2]; self.nega = sm[:, 8:24]
        S.op("dve", lambda e: e.memset(self.epsc, EPS), writes=[cb])
        S.op("dve", lambda e: e.memset(self.qsc, math.log(HD ** -0.5)), writes=[cb])
        self.act(self.nega, self.cc("alog"), AF.Exp, R, [cb])
        S.op("dve", lambda e: e.tensor_scalar(out=self.nega, in0=self.nega, scalar1=-1.0, scalar2=None,
                                              op0=ALU.mult), reads=[cb], writes=[cb])
        self.CB = [self.cpb, self.cbb]
        T = self.T
        self.hT = self.bf(KC * T).rearrange("p (k t) -> p k t", k=KC)
        self.hTb = [Buf("hT%d" % i) for i in range(self.TT)]
        self.R1 = self.f32(8 * T)
        self.oaT = self.R1[:, 0:4 * T].bitcast(BF16).rearrange("p (k t) -> p k t", k=8)
        self.obT = self.R1[:, 4 * T:8 * T].bitcast(BF16).rearrange("p (k t) -> p k t", k=8)
        self.xTM = self.R1[:, 0:8 * T].rearrange("p (i c) -> p i c", c=D)
        self.oab = [[Buf("oa%d_%d" % (h, b)) for b in range(self.NB)] for h in range(8)]
        self.obb = [[Buf("ob%d_%d" % (h, b)) for b in range(self.NB)] for h in range(8)]
        self.xb = [Buf("x%d" % i) for i in range(self.TT)]
        n = 2 * self.TT * 8
        v4 = lambda a: a.rearrange("p (d i h) -> p d i h", d=2, i=self.TT)
        self.gam = v4(self.f32(n)); self.ngam = v4(self.f32(n)); self.eg = v4(self.f32(n))
        self.ekd = v4(self.f32(n)); self.dl = v4(self.f32(n)); self.beta = v4(self.f32(n))
        self.nbeta = v4(self.f32(n))
        self.decb = Buf("decay")
        self.NW = 4
        self.wslot = [self.bf(1024).rearrange("p (k n) -> p k n", k=8) for _ in range(self.NW)]
        self.wslotb = [Buf("ws%d" % i) for i in range(self.NW)]
        self.ws_rr = 0
        self.st = [self.f32(8) for _ in range(2)]
        self.stb = [Buf("st%d" % i) for i in range(2)]
        self.nrr = 0
        self.mark = self.top

    def alloc_norm_tmps(self, need_xin):
        if need_xin:
            self.xin = [self.f32(D) for _ in range(2)]
            self.xinb = [self.buf("xin") for i in range(2)]
        self.xn = [self.bf(D) for _ in range(2)]
        self.xnb = [self.buf("xn") for i in range(2)]
        self.junk = self.bf(D); self.junkb = self.buf("junk")
        self.ptmp = [self.f32(512) for _ in range(2)]
        self.ptmpb = [self.buf("ptmp") for _ in range(2)]

    def newphase(self, no_tail=False):
        self.S.barrier()
        self.top = self.mark
        self.topB = self.LIM
        self.no_tail = no_tail

    def wload(self, name, c0, ncols=128):
        i = self.ws_rr
        self.ws_rr = (self.ws_rr + 1) % self.NW
        src = self.wb_d[name].rearrange("(k p) n -> p k n", p=128)[:, :, c0:c0 + ncols]
        dst = self.wslot[i][:, :, 0:ncols]
        self.S.dma("sp", dst, src, reads=[self.wb_buf[name]], writes=[self.wslotb[i]])
        return dst, self.wslotb[i]

    def prefetch(self, items, LA=4):
        loaded = {}
        nxt = 0
        for i in range(len(items)):
            while nxt < min(len(items), i + LA):
                loaded[nxt] = self.wload(*items[nxt])
                nxt += 1
            ap, b = loaded.pop(i)
            yield i, ap, b

    def norm_tile(self, src, srcb, wcol, dst3, dstb, tcol):
        r = self.nrr
        self.nrr ^= 1
        st, stb, xn, xnb = self.st[r], self.stb[r], self.xn[r], self.xnb[r]
        self.act(self.junk, src, AF.Square, [srcb], [self.junkb, stb], accum_out=st[:, 0:1])
        self.act(st[:, 1:2], st[:, 0:1], AF.Ln, [stb] + self.CB, [stb], scale=1.0 / D, bias=self.epsc)
        self.act(st[:, 2:3], st[:, 1:2], AF.Exp, [stb], [stb], scale=-0.5)
        self.act(xn, src, AF.Copy, [srcb, stb], [xnb], scale=st[:, 2:3])
        for hf in range(2):
            pa, pb = self.pst()
            pv = pa.rearrange("p (k t) -> p k t", k=4)
            for k in range(4):
                kc = hf * 4 + k
                self.mm(pv[:, k, :], xn[:, kc * 128:(kc + 1) * 128], self.ident_bf, True, True,
                        [xnb] + self.CB, [pb])
            self.tt("dve", dst3[:, hf * 4:hf * 4 + 4, tcol:tcol + 128], pv,
                    wcol[:, hf * 4:hf * 4 + 4].unsqueeze(2).to_broadcast([128, 4, 128]), ALU.mult,
                    [pb] + self.CB, [dstb])

    def norm_from_dram(self, s, wname):
        for i in range(self.TT):
            r = i % 2
            self.S.dma("sp", self.xin[r], self.x_d[s, i * 128:(i + 1) * 128, :], writes=[self.xinb[r]])
            self.norm_tile(self.xin[r], self.xinb[r], self.cc(wname), self.hT, self.hTb[i], i * 128)

    def norm_from_x(self, wname):
        for i in range(self.TT):
            self.norm_tile(self.xTM[:, i, :], self.xb[i], self.cc(wname), self.hT, self.hTb[i], i * 128)

    def hblk(self, b):
        return [self.hTb[i] for i in range(b * self.TPB, (b + 1) * self.TPB)]

    def bsl(self, b):
        return slice(b * self.BLK, (b + 1) * self.BLK)

    def proj_fm(self, w, wb, rhs3, rhsb, b, n=None):
        pa, pb = self.pst()
        n = self.BLK if n is None else n
        o = pa[:, 0:n]
        nk = w.shape[1]
        for k in range(nk):
            self.mm(o, w[:, k, :], rhs3[:, k, b * self.BLK:b * self.BLK + n], k == 0, k == nk - 1,
                    [wb] + rhsb, [pb], inc=(k == nk - 1))
        return o, pb

    def phase_decay(self):
        S, TT = self.S, self.TT
        wab = self.bf(8 * 32).rearrange("p (k n) -> p k n", k=8)
        wabb = self.buf("wab")
        S.dma("sp", wab, self.wb_d["w_in"].rearrange("(k p) n -> p k n", p=128)[:, :, C_AB:C_AB + 32],
              reads=[self.wb_buf["w_in"]], writes=[wabb])
        n16 = TT * 16
        xa = self.f32(n16); ax = self.f32(n16); g = self.f32(n16); bt = self.f32(n16)
        tb = self.buf("dtmp")
        v3 = lambda a: a.rearrange("p (i c) -> p i c", c=16)
        pa, pb = self.pst()
        abp = pa[:, 0:TT * 32].rearrange("p (i c) -> p i c", c=32)
        for i in range(TT):
            for k in range(KC):
                self.mm(abp[:, i, :], self.hT[:, k, i * 128:(i + 1) * 128], wab[:, k, :], k == 0, k == KC - 1,
                        [self.hTb[i], wabb], [pb], inc=(k == KC - 1))
        dtb = self.cc("dtb").unsqueeze(1).to_broadcast([128, TT, 16])
        negab = self.nega.unsqueeze(1).to_broadcast([128, TT, 16])
        self.tt("dve", v3(xa), abp[:, :, 0:16], dtb, ALU.add, [pb] + self.CB, [tb])
        S.op("dve", lambda e: e.tensor_scalar(out=ax, in0=xa, scalar1=-1.0, scalar2=None, op0=ALU.mult),
             reads=[tb], writes=[tb])
        self.tt("dve", ax, ax, xa, ALU.min, [tb], [tb])
        self.act(ax, ax, AF.Exp, [tb], [tb])
        self.act(ax, ax, AF.Ln, [tb], [tb], bias=1.0)
        self.stt(xa, xa, 0.0, ax, ALU.max, ALU.add, [tb], [tb])
        self.tt("dve", v3(g), v3(xa), negab, ALU.mult, [tb] + self.CB, [tb])
        self.act(v3(bt), abp[:, :, 16:32], AF.Sigmoid, [pb], [tb])
        g3, b3 = v3(g), v3(bt)
        pg, pgb = self.pst()
        n8 = TT * 8
        gps = pg[:, 0:2 * n8].rearrange("p (d i h) -> p d i h", d=2, i=TT)
        tps = pg[:, 2 * n8:4 * n8].rearrange("p (d i h) -> p d i h", d=2, i=TT)
        for d in range(2):
            cum = self.cc("ucum") if d == 0 else self.cc("lcum")
            self.mm(gps[:, d], cum, g3[:, :, d * 8:(d + 1) * 8], True, True, [tb] + self.CB, [pgb])
            self.mm(tps[:, d], self.cc("ones"), g3[:, :, d * 8:(d + 1) * 8], True, True, [tb] + self.CB, [pgb])
        db = self.decb
        for d in range(2):
            self.cpy("dve", self.gam[:, d], gps[:, d], [pgb], [db])
            S.op("dve", lambda e, d=d: e.tensor_scalar(out=self.ngam[:, d], in0=gps[:, d], scalar1=-1.0,
                                                       scalar2=None, op0=ALU.mult), reads=[pgb], writes=[db])
            self.act(self.eg[:, d], gps[:, d], AF.Exp, [pgb], [db])
            self.tt("dve", self.ekd[:, d], tps[:, d], self.gam[:, d], ALU.subtract, [pgb, db], [db])
            self.act(self.ekd[:, d], self.ekd[:, d], AF.Exp, [db], [db])
            self.act(self.dl[:, d], tps[:, d], AF.Exp, [pgb], [db])
            self.cpy("dve", self.beta[:, d], b3[:, :, d * 8:(d + 1) * 8], [tb], [db])
            S.op("dve", lambda e, d=d: e.tensor_scalar(out=self.nbeta[:, d], in0=b3[:, :, d * 8:(d + 1) * 8],
                                                       scalar1=-1.0, scalar2=None, op0=ALU.mult),
                 reads=[tb], writes=[db])

    def mixer_a(self):
        S, T, TT, BLK, NB, TPB = self.S, self.T, self.TT, self.BLK, self.NB, self.TPB
        F32R = mybir.dt.float32r
        xtop = [4 * T]
        def xf32(n):
            a_ = self.R1[:, xtop[0]:xtop[0] + n]
            xtop[0] += n
            assert xtop[0] <= 8 * T
            return a_
        xbf = lambda n: xf32(n // 2).bitcast(BF16)
        pre = self.f32(T + 4); preb = self.buf("pre")
        oT = pre[:, 2:T + 2]; oTb = [self.buf("oT") for _ in range(TT)]
        cv = self.f32(T); cvb = self.buf("cv")
        qT = self.bf(T); kT = self.bf(T); vF = self.bf(T)
        qTb = [self.buf("qT") for _ in range(NB)]; kTb = [self.buf("kT") for _ in range(NB)]
        vFb = self.buf("vF")
        r3 = lambda a_: a_.rearrange("p (i d) -> p i d", d=128)
        ub = [r3(cv), r3(xf32(T))]
        ubb = [[self.buf("ub") for _ in range(NB)] for _ in range(2)]
        qg = [vF, xbf(T)]; qgb = [[self.buf("qg") for _ in range(NB)] for _ in range(2)]
        kd = [r3(self.bf(T)), r3(xbf(T))]; kdb = [self.buf("kd") for _ in range(2)]
        wT = [self.bf(T), xbf(T)]; wTb = [[self.buf("wT") for _ in range(NB)] for _ in range(2)]
        qkE = [self.bf(T), xbf(T)]; qkEb = [[self.buf("qkE") for _ in range(NB)] for _ in range(2)]
        kTM = r3(xbf(T)); kTMb = self.buf("kTM")
        vTM = r3(self.bf(T)); vTMb = self.buf("vTM")
        kg = r3(xbf(T)); kgb = self.buf("kg")
        Sst = [self.f32(128) for _ in range(2)]; Sbf = [self.bf(128) for _ in range(2)]
        Sb = [self.buf("S") for _ in range(2)]; Sbfb = [self.buf("Sbf") for _ in range(2)]
        vnew = [[self.bf(128) for _ in range(2)] for _ in range(2)]
        vnewb = [[self.buf("vnew") for _ in range(2)] for _ in range(2)]
        E = [self.bf(BLK) for _ in range(2)]; Eb = [self.buf("E") for _ in range(2)]
        Es = [self.bf(BLK) for _ in range(2)]; Esb = [self.buf("Es") for _ in range(2)]
        NG = 2
        Bm = [[self.bf(BLK) for _ in range(NG)] for e_ in range(2)]
        BTm = [[self.bf(BLK) for _ in range(NG)] for _ in range(2)]
        Pm = [[self.bf(BLK) for _ in range(NG)] for _ in range(2)]
        Bmb = [[self.buf("B") for _ in range(NG)] for _ in range(2)]
        BTmb = [[self.buf("BT") for _ in range(NG)] for _ in range(2)]
        Pmb = [[self.buf("P") for _ in range(NG)] for _ in range(2)]
        N1 = [self.bf(BLK) for _ in range(2)]; N1b = [self.buf("N1") for _ in range(2)]
        N2 = [self.bf(BLK) for _ in range(2)]; N2b = [self.buf("N2") for _ in range(2)]
        Tt = [self.bf(BLK) for _ in range(2)]; Ttb = [self.buf("Tt") for _ in range(2)]
        Xm = [self.bf(BLK) for _ in range(2)]; Xmb = [self.buf("Xm") for _ in range(2)]
        tmp = [self.f32(BLK) for _ in range(3)]; tmpb = [self.buf("tmpA") for _ in range(3)]
        sqb16 = self.bf(BLK); sqb16b = self.buf("sq16")
        S.op("pool", lambda e: e.memset(pre[:, 0:2], 0.0), writes=[preb])
        S.op("pool", lambda e: e.memset(pre[:, T + 2:T + 4], 0.0), writes=[preb])
        cw = self.cc("convw").rearrange("p (c j) -> p c j", j=5)
        CB = self.CB
        v4 = lambda a_: a_.rearrange("p (i c) -> p i c", c=128)
        HIER = not self.inv32
        NL = 4 if HIER else 6
        bc3 = lambda m_: m_.unsqueeze(1).to_broadcast([128, TPB, 128])

        for h in range(NH):
            items = [("w_in", C_QKV + t * 1024 + h * 128) for t in range(3)] + [("w_in", C_Z + h * 128)]
            wl = list(self.prefetch(items, LA=4))
            for t in range(3):
                _, w, wb = wl[t]
                for b in range(NB):
                    o, pb = self.proj_fm(w, wb, self.hT, self.hblk(b), b)
                    self.cpy("act", pre[:, 2 + b * BLK:2 + (b + 1) * BLK], o, [pb], [preb] + oTb)
                cch = t * 8 + h
                S.op("dve", lambda e: e.tensor_scalar(out=cv, in0=pre[:, 0:T], scalar1=cw[:, cch, 0:1], scalar2=None,
                                                      op0=ALU.mult), reads=[preb] + CB, writes=[cvb] + ubb[0])
                for j in range(1, 5):
                    self.stt(cv, pre[:, j:j + T], cw[:, cch, j:j + 1], cv, ALU.mult, ALU.add, [preb, cvb] + CB, [cvb])
                if t == 2:
                    self.act(vF, cv, AF.Silu, [cvb], [vFb] + qgb[0])
                    continue
                self.act(cv, cv, AF.Silu, [cvb], [cvb])
                dst, dstb = (qT, qTb) if t == 0 else (kT, kTb)
                for b in range(NB):
                    sl = self.bsl(b)
                    self.act(sqb16, cv[:, sl], AF.Square, [cvb], [sqb16b])
                    pa, pb = self.pst()
                    self.mm(pa[:, 0:BLK], self.ones_bf, sqb16, True, True, [sqb16b] + CB, [pb])
                    self.rsqrt_act(tmp[1], pa[:, 0:BLK], 1.0, [pb] + CB, [tmpb[1]], tmp[0], tmpb[0],
                                   post_bias=(1.0 if t == 0 else 0.0))
                    self.tt("dve", dst[:, sl], cv[:, sl], tmp[1], ALU.mult, [cvb, tmpb[1]], [dstb[b]])
            for b in range(NB):
                for (src, srcb, dst3, dstb) in ((kT, [kTb[b]], kTM, kTMb), (vF, [vFb], vTM, vTMb)):
                    pa, pb = self.pst()
                    pv = v4(pa[:, 0:BLK])
                    for j in range(TPB):
                        i = b * TPB + j
                        self.mm(pv[:, j, :], src[:, i * 128:(i + 1) * 128], self.ident_bf, True, True,
                                srcb + CB, [pb])
                    self.cpy("act", dst3[:, b * TPB:(b + 1) * TPB, :], pv, [pb], [dstb])
            for d in range(2):
                mask = self.cc("maskf") if d == 0 else self.cc("maskb")
                strm = self.strf_bf if d == 0 else self.strb_bf
                colb = lambda a_, i: a_[:, d, i, h:h + 1]
                ubw = (lambda b: [ubb[0][b], cvb]) if d == 0 else (lambda b: [ubb[1][b]])
                qgw = (lambda b: [qgb[0][b], vFb]) if d == 0 else (lambda b: [qgb[1][b]])
                self.tt("pool", kg, kTM, self.eg[:, d, :, h:h + 1].to_broadcast([128, TT, 128]), ALU.mult,
                        [kTMb, self.decb], [kgb])
                self.tt("pool", kd[d], kTM, self.ekd[:, d, :, h:h + 1].to_broadcast([128, TT, 128]), ALU.mult,
                        [kTMb, self.decb], [kdb[d]])
                for b0 in range(0, NB, 2):
                    blks = list(range(b0, min(NB, b0 + 2)))
                    for b in blks:
                        e_i = b % 2
                        pa, pb = self.pst()
                        pq, pqb = self.pst()
                        pv, pqv = v4(pa[:, 0:BLK]), v4(pq[:, 0:BLK])
                        for j in range(TPB):
                            i = b * TPB + j
                            self.mm(pv[:, j, :], colb(self.gam, i).to_broadcast([128, 128]), self.cc("ident"),
                                    True, False, [self.decb] + CB, [pb], inc=False)
                            self.mm(pv[:, j, :], self.cc("ident"), mask, False, True, CB, [pb])
                            self.mm(pqv[:, j, :], colb(self.eg, i).to_broadcast([128, 128]), self.cc("ident"),
                                    True, True, [self.decb] + CB, [pqb])
                        for j in range(TPB):
                            i = b * TPB + j
                            self.act(E[e_i][:, j * 128:(j + 1) * 128], pv[:, j, :], AF.Exp, [pb, self.decb],
                                     [Eb[e_i]], bias=colb(self.ngam, i), scale=1.0)
                        self.tt("dve", qg[d][:, self.bsl(b)], pq[:, 0:BLK], qT[:, self.bsl(b)], ALU.mult,
                                [pqb, qTb[b]], qgw(b))
                        self.tt("pool", Es[e_i], E[e_i], strm[:, 0:BLK], ALU.mult, [Eb[e_i]] + CB, [Esb[e_i]])
                        pk, pkb = self.pst()
                        pkv = v4(pk[:, 0:BLK])
                        for j in range(TPB):
                            i = b * TPB + j
                            ks = kT[:, i * 128:(i + 1) * 128]
                            self.mm(pkv[:, j, :], ks, ks, True, True, [kTb[b]], [pkb])
                        for j in range(TPB):
                            i = b * TPB + j
                            self.stt(BTm[e_i][0][:, j * 128:(j + 1) * 128], pkv[:, j, :], colb(self.nbeta, i),
                                     Es[e_i][:, j * 128:(j + 1) * 128], ALU.mult, ALU.mult,
                                     [pkb, Esb[e_i], self.decb], [BTmb[e_i][0]])
                        pk2, pk2b = self.pst()
                        pk2v = v4(pk2[:, 0:BLK])
                        for j in range(TPB):
                            i = b * TPB + j
                            self.mm(pk2v[:, j, :], kT[:, i * 128:(i + 1) * 128], qT[:, i * 128:(i + 1) * 128],
                                    True, True, [kTb[b], qTb[b]], [pk2b])
                        self.tt("dve", qkE[d][:, self.bsl(b)], pk2[:, 0:BLK], E[e_i], ALU.mult, [pk2b, Eb[e_i]],
                                [qkEb[d][b]])
                        pt, ptb = self.pst()
                        ptv = v4(pt[:, 0:BLK])
                        for j in range(TPB):
                            self.mm(ptv[:, j, :], BTm[e_i][0][:, j * 128:(j + 1) * 128], self.ident_bf, True, True,
                                    [BTmb[e_i][0]] + CB, [ptb])
                        if HIER:
                            self.cpy("act", N2[e_i], pt[:, 0:BLK], [ptb], [N2b[e_i]])
                            self.tt("dve", v4(Bm[e_i][0]), v4(N2[e_i]), bc3(self.bd_bf), ALU.mult, [N2b[e_i]] + CB,
                                    [Bmb[e_i][0]])
                            self.tt("pool", v4(BTm[e_i][0]), v4(BTm[e_i][0]), bc3(self.bd_bf), ALU.mult,
                                    [BTmb[e_i][0]] + CB, [BTmb[e_i][0]])
                        else:
                            self.cpy("act", Bm[e_i][0], pt[:, 0:BLK], [ptb], [Bmb[e_i][0]])
                        self.tt("pool", Pm[e_i][0], BTm[e_i][0], self.ident4_bf[:, 0:BLK], ALU.add,
                                [BTmb[e_i][0]] + CB, [Pmb[e_i][0]])
                        if HIER:
                            self.tt("pool", v4(N1[e_i]), v4(N2[e_i]), bc3(self.m1_bf[d]), ALU.mult,
                                    [N2b[e_i]] + CB, [N1b[e_i]])
                            self.tt("pool", v4(N2[e_i]), v4(N2[e_i]), bc3(self.m2_bf[d]), ALU.mult,
                                    [N2b[e_i]] + CB, [N2b[e_i]])
                    for lv in range(NL):
                        g0, g1 = lv % NG, (lv + 1) % NG
                        for b in blks:
                            e_i = b % 2
                            pB, pBb = self.pst()
                            pBv = v4(pB[:, 0:BLK])
                            for j in range(TPB):
                                sl = slice(j * 128, (j + 1) * 128)
                                self.mm(pBv[:, j, :], BTm[e_i][g0][:, sl], Bm[e_i][g0][:, sl], True, True,
                                        [BTmb[e_i][g0], Bmb[e_i][g0]], [pBb])
                            if lv < NL - 1:
                                pT, pTb = self.pst()
                                pTv = v4(pT[:, 0:BLK])
                                for j in range(TPB):
                                    sl = slice(j * 128, (j + 1) * 128)
                                    self.mm(pTv[:, j, :], Bm[e_i][g0][:, sl], BTm[e_i][g0][:, sl], True, True,
                                            [BTmb[e_i][g0], Bmb[e_i][g0]], [pTb])
                            self.cpy("act", Bm[e_i][g1], pB[:, 0:BLK], [pBb], [Bmb[e_i][g1]])
                            if lv < NL - 1:
                                self.cpy("dve", BTm[e_i][g1], pT[:, 0:BLK], [pTb], [BTmb[e_i][g1]])
                        for b in blks:
                            e_i = b % 2
                            pP, pPb = self.pst()
                            pPv = v4(pP[:, 0:BLK])
                            for j in range(TPB):
                                sl = slice(j * 128, (j + 1) * 128)
                                self.mm(pPv[:, j, :], self.ident_bf, Pm[e_i][g0][:, sl], True, False,
                                        [Pmb[e_i][g0]] + CB, [pPb], inc=False)
                                self.mm(pPv[:, j, :], Bm[e_i][g1][:, sl], Pm[e_i][g0][:, sl], False, True,
                                        [Pmb[e_i][g0], Bmb[e_i][g1]], [pPb])
                            self.cpy("act" if (b % 2 == 0) else "dve", Pm[e_i][g1], pP[:, 0:BLK], [pPb],
                                     [Pmb[e_i][g1]])
                    gcur = NL % NG
                    for (Nm, Nmb) in (((N1, N1b), (N2, N2b)) if HIER else ()):
                        gn = 1 - gcur
                        for b in blks:
                            e_i = b % 2
                            pt, ptb = self.pst()
                            ptv = v4(pt[:, 0:BLK])
                            px, pxb = self.pst()
                            pxv = v4(px[:, 0:BLK])
                            for j in range(TPB):
                                sl = slice(j * 128, (j + 1) * 128)
                                self.mm(ptv[:, j, :], Pm[e_i][gcur][:, sl], self.ident_bf, True, True,
                                        [Pmb[e_i][gcur]] + CB, [ptb])
                                self.mm(pxv[:, j, :], Nm[e_i][:, sl], Pm[e_i][gcur][:, sl], True, True,
                                        [Pmb[e_i][gcur], Nmb[e_i]], [pxb])
                            self.cpy("act", Tt[e_i], pt[:, 0:BLK], [ptb], [Ttb[e_i]])
                            self.cpy("dve", Xm[e_i], px[:, 0:BLK], [pxb], [Xmb[e_i]])
                        for b in blks:
                            e_i = b % 2
                            pP, pPb = self.pst()
                            pPv = v4(pP[:, 0:BLK])
                            for j in range(TPB):
                                sl = slice(j * 128, (j + 1) * 128)
                                self.mm(pPv[:, j, :], self.ident_bf, Pm[e_i][gcur][:, sl], True, False,
                                        [Pmb[e_i][gcur]] + CB, [pPb], inc=False)
                                self.mm(pPv[:, j, :], Tt[e_i][:, sl], Xm[e_i][:, sl], False, True,
                                        [Ttb[e_i], Xmb[e_i]], [pPb])
                            self.cpy("act" if (b % 2 == 0) else "dve", Pm[e_i][gn], pP[:, 0:BLK], [pPb],
                                     [Pmb[e_i][gn]])
                        gcur = gn
                    gf = gcur
                    for b in blks:
                        e_i = b % 2
                        Pf, Pfb = Pm[e_i][gf], Pmb[e_i][gf]
                        pu, pub = self.pst()
                        puv = v4(pu[:, 0:BLK])
                        pw, pwb = self.pst()
                        pwv = v4(pw[:, 0:BLK])
                        for j in range(TPB):
                            i = b * TPB + j
                            sl = slice(j * 128, (j + 1) * 128)
                            self.mm(puv[:, j, :], Pf[:, sl], vTM[:, i, :], True, True, [Pfb, vTMb], [pub])
                            self.mm(pwv[:, j, :], kg[:, i, :], Pf[:, sl], True, True, [Pfb, kgb], [pwb])
                        self.tt("dve", ub[d][:, b * TPB:(b + 1) * TPB, :], puv,
                                self.beta[:, d, b * TPB:(b + 1) * TPB, h:h + 1].to_broadcast([128, TPB, 128]),
                                ALU.mult, [pub, self.decb], ubw(b))
                        self.cpy("act", wT[d][:, self.bsl(b)], pw[:, 0:BLK], [pwb], [wTb[d][b]])
            for d in range(2):
                S.op("pool", lambda e: e.memset(Sst[d], 0.0), writes=[Sb[d]])
                S.op("pool", lambda e: e.memset(Sbf[d], 0.0), writes=[Sbfb[d]])
            written = [False] * TT
            for n_i in range(TT):
                for d in range(2):
                    i = n_i if d == 0 else TT - 1 - n_i
                    b = i // TPB
                    sl = slice(i * 128, (i + 1) * 128)
                    vn, vnb = vnew[d][n_i % 2], vnewb[d][n_i % 2]
                    colb = lambda a_, i_: a_[:, d, i_, h:h + 1]
                    p1, p1b = self.pst()
                    self.mm(p1[:, 0:128], wT[d][:, sl], Sbf[d], True, True, [wTb[d][b], Sbfb[d]], [p1b])
                    self.stt(vn, p1[:, 0:128], colb(self.nbeta, i), ub[d][:, i, :], ALU.mult, ALU.add,
                             [p1b, ubb[d][b], self.decb] + ([cvb] if d == 0 else []), [vnb])
                    p3, p3b = self.pst()
                    self.mm(p3[:, 0:128], kd[d][:, i, :], vn, True, True, [kdb[d], vnb], [p3b])
                    p2, p2b = self.pst()
                    self.mm(p2[:, 0:128], Sbf[d], qg[d][:, sl], True, False, [Sbfb[d], qgb[d][b]] + ([vFb] if d == 0 else []),
                            [p2b], inc=False)
                    self.mm(p2[:, 0:128], vn, qkE[d][:, sl], False, True, [vnb, qkEb[d][b]], [p2b])
                    if not os.environ.get("K_NOSBF"):
                        self.stt(Sbf[d], Sst[d], colb(self.dl, i), p3[:, 0:128], ALU.mult, ALU.add,
                                 [Sb[d], p3b, self.decb], [Sbfb[d]])
                    self.stt(Sst[d], Sst[d], colb(self.dl, i), p3[:, 0:128], ALU.mult, ALU.add,
                             [Sb[d], p3b, self.decb], [Sb[d]])
                    if os.environ.get("K_NOSBF"):
                        self.cpy("act", Sbf[d], Sst[d], [Sb[d]], [Sbfb[d]])
                    if not written[i]:
                        self.cpy("act", oT[:, sl], p2[:, 0:128], [p2b], [oTb[i], preb])
                        written[i] = True
                    else:
                        self.tt("dve", oT[:, sl], p2[:, 0:128], oT[:, sl], ALU.add, [p2b, oTb[i]], [oTb[i], preb])
            _, wz, wzb = wl[3]
            for b in range(NB):
                sl = self.bsl(b)
                ob_ = [oTb[i] for i in range(b * TPB, (b + 1) * TPB)]
                self.act(sqb16, oT[:, sl], AF.Square, ob_, [sqb16b])
                pa, pb = self.pst()
                self.mm(pa[:, 0:BLK], self.ones_bf, sqb16, True, True, [sqb16b] + CB, [pb])
                self.rsqrt_act(tmp[1], pa[:, 0:BLK], 1.0 / HD, [pb] + CB, [tmpb[1]], tmp[0], tmpb[0])
                self.stt(oT[:, sl], oT[:, sl], self.cc("lanorm"), tmp[1], ALU.mult, ALU.mult,
                         ob_ + [tmpb[1]] + CB, ob_ + [preb])
            for b in range(NB):
                sl = self.bsl(b)
                ob_ = [oTb[i] for i in range(b * TPB, (b + 1) * TPB)]
                o, pzb = self.proj_fm(wz, wzb, self.hT, self.hblk(b), b)
                self.act(tmp[2], o, AF.Silu, [pzb], [tmpb[2]])
                self.tt("pool", self.oaT[:, h, sl], oT[:, sl], tmp[2], ALU.mult, ob_ + [tmpb[2]], [self.oab[h][b]])

    def rope_norm_stages(self, proj, wname, dst, dstb, b, tmp, tmpb, t16, t16b, oraw=None):
        BLK, CB = self.BLK, self.CB
        st = {}

        def s1():
            o_, pb_ = proj()
            self.act(t16[0], o_, AF.Square, [pb_], [t16b[0]])
            if oraw is None:
                st["o"], st["pb"] = o_, pb_
            else:
                self.cpy("dve", oraw[0], o_, [pb_, t16b[0]], [oraw[1]])
                st["o"], st["pb"] = oraw

        def s2():
            pa, pab = self.pst()
            self.mm(pa[:, 0:BLK], self.ones_bf, t16[0], True, True, [t16b[0]] + CB, [pab])
            self.rsqrt_act(tmp[1], pa[:, 0:BLK], 1.0 / HD, [pab] + CB, [tmpb[1]], tmp[0], tmpb[0])

        def s3():
            self.stt(tmp[0], st["o"], self.cc(wname), tmp[1], ALU.mult, ALU.mult,
                     [st["pb"], tmpb[1], tmpb[0]] + CB, [tmpb[0]])
            self.cpy("act", t16[1], tmp[0], [tmpb[0]], [t16b[1]])

        def s4():
            ps_, psb_ = self.pst()
            self.mm(ps_[:, 0:BLK], self.perm_bf, t16[1], True, True, [t16b[1]] + CB, [psb_])
            nr = BLK // 64
            r0 = b * nr
            cosr = self.cc("cosr")[0:64, r0:r0 + nr].unsqueeze(2).to_broadcast([64, nr, 64])
            sinr = self.cc("sinr")[0:64, r0:r0 + nr].unsqueeze(2).to_broadcast([64, nr, 64])
            cosc = self.cc("cosc")[64:128, :].unsqueeze(1).to_broadcast([64, nr, 64])
            sinc = self.cc("sinc")[64:128, :].unsqueeze(1).to_broadcast([64, nr, 64])
            r3 = lambda a_, lo: a_[lo:lo + 64, 0:BLK].rearrange("p (r c) -> p r c", c=64)
            self.tt("pool", r3(tmp[1], 0), r3(tmp[0], 0), cosr, ALU.mult, [tmpb[0], tmpb[1]] + CB, [tmpb[1]])
            self.tt("pool", r3(tmp[1], 64), r3(tmp[0], 64), cosc, ALU.mult, [tmpb[0], tmpb[1]] + CB, [tmpb[1]])
            self.tt("dve", r3(tmp[2], 0), r3(ps_, 0), sinr, ALU.mult, [psb_] + CB, [tmpb[2]])
            self.tt("dve", r3(tmp[2], 64), r3(ps_, 64), sinc, ALU.mult, [psb_] + CB, [tmpb[2]])
            self.tt("pool", dst, tmp[1], tmp[2], ALU.add, [tmpb[1], tmpb[2]], dstb)

        return [s1, s2, s3, s4]

    def rope_norm(self, o, pb, wname, dst, dstb, b, tmp, tmpb, t16, t16b):
        for f in self.rope_norm_stages(lambda: (o, pb), wname, dst, dstb, b, tmp, tmpb, t16, t16b):
            f()

    def mixer_b(self):
        S, T, TT, BLK, NB, TPB, CB = self.S, self.T, self.TT, self.BLK, self.NB, self.TPB, self.CB
        kTr = self.bf(2 * T).rearrange("p (g t) -> p g t", g=2)
        kTrb = [[self.buf("kTr") for _ in range(NB)] for _ in range(2)]
        vB = self.bf(TT * 256).rearrange("p (i c) -> p i c", c=256)
        vBb = self.buf("vB")
        wv = self.bf(8 * 256).rearrange("p (k n) -> p k n", k=8); wvb = self.buf("wv")
        qTr = [self.bf(BLK) for _ in range(2)]; qTrb = [self.buf("qTr") for _ in range(2)]
        NE = 4
        Ej = [self.bf(BLK) for _ in range(NE)]; Ejb = [self.buf("Ej") for _ in range(NE)]
        tmp = [self.f32(BLK) for _ in range(3)]; tmpb = [self.buf("tmpB") for _ in range(3)]
        t16 = [self.bf(BLK) for _ in range(2)]; t16b = [self.buf("t16") for _ in range(2)]
        S.dma("sp", wv, self.wb_d["w_in"].rearrange("(k p) n -> p k n", p=128)[:, :, C_VB:C_VB + 256],
              reads=[self.wb_buf["w_in"]], writes=[wvb])
        items = [("w_in", C_KB + g * 128) for g in range(2)] + [("w_in", C_QB + hq * 128) for hq in range(8)]
        pf = self.prefetch(items, LA=3)
        for g in range(2):
            _, w, wb = next(pf)
            for b in range(NB):
                o, pb = self.proj_fm(w, wb, self.hT, self.hblk(b), b)
                self.rope_norm(o, pb, "knorm", kTr[:, g, self.bsl(b)], [kTrb[g][b]], b, tmp, tmpb, t16, t16b)
        for i in range(TT):
            pa, pb = self.pst()
            for k in range(KC):
                self.mm(pa[:, 0:256], self.hT[:, k, i * 128:(i + 1) * 128], wv[:, k, :], k == 0, k == KC - 1,
                        [self.hTb[i], wvb], [pb], inc=(k == KC - 1))
            self.cpy("act", vB[:, i, :], pa[:, 0:256], [pb], [vBb])
        self.ps_base, self.ps_n = 4, 4
        accs = [((self.ps[0][:], self.psb[0]), (self.ps[1][:], self.psb[1])),
                ((self.ps[2][:], self.psb[2]), (self.ps[3][:], self.psb[3]))]
        seq = [(hq, b) for hq in range(8) for b in range(NB)]
        wq = {}

        rfin = [self.f32(BLK) for _ in range(2)]; rfinb = [self.buf("rfin") for _ in range(2)]
        oraw = (self.f32(BLK), self.buf("oraw"))

        def prep_stages(n):
            hq, b = seq[n]
            if b == 0:
                wq[hq] = next(pf)[1:]
            w, wb = wq[hq]
            return self.rope_norm_stages(lambda: self.proj_fm(w, wb, self.hT, self.hblk(b), b), "qnorm",
                                         qTr[n % 2], [qTrb[n % 2]], b, tmp, tmpb, t16, t16b, oraw=oraw)

        for f in prep_stages(0):
            f()
        for n in range(len(seq)):
            hq, b = seq[n]
            g = hq // 4
            qi = n % 2
            (acc0, acc0b), (acc1, acc1b) = accs[n % 2]

            def issue_sc(j):
                ei = j % NE
                sc, scb = self.pst()
                self.mm(sc[:, 0:BLK], kTr[:, g, j * 128:(j + 1) * 128], qTr[qi], True, True,
                        [kTrb[g][j // TPB], qTrb[qi]], [scb])
                self.act(Ej[ei], sc[:, 0:BLK], AF.Exp, [scb], [Ejb[ei]], scale=HD ** -0.5)

            stages = prep_stages(n + 1) if n + 1 < len(seq) else []
            if os.environ.get("K_NOSTAGE"):
                for f in stages:
                    f()
                stages = []
            when = {}
            for si, f in enumerate(stages):
                when.setdefault(min(TT - 1, 1 + (si * max(1, TT - 2)) // len(stages)), []).append(f)
            issue_sc(0)
            if TT > 1:
                issue_sc(1)
            for j in range(TT):
                if j + 2 < TT:
                    issue_sc(j + 2)
                for f in when.get(j, []):
                    f()
                ei = j % NE
                self.mm(acc0[:, 0:BLK], vB[:, j, g * 128:(g + 1) * 128], Ej[ei], j == 0, j == TT - 1,
                        [vBb, Ejb[ei]], [acc0b], inc=(j == TT - 1))
                self.mm(acc1[:, 0:BLK], self.ones_bf, Ej[ei], j == 0, j == TT - 1,
                        [Ejb[ei]] + CB, [acc1b])
            rt, rtb = rfin[n % 2], rfinb[n % 2]
            self.act(rt, acc1[:, 0:BLK], AF.Ln, [acc1b], [rtb])
            self.act(rt, rt, AF.Exp, [rtb], [rtb], scale=-1.0)
            self.tt("dve", self.obT[:, hq, self.bsl(b)], acc0[:, 0:BLK], rt, ALU.mult,
                    [acc0b, rtb], [self.obb[hq][b]])
        self.ps_base, self.ps_n = 2, 6

    def post_tile(self, z0, z0b, z1, z1b, pnw, pnwb, i, tmp, tmpb):
        r = self.nrr
        self.nrr ^= 1
        st, stb = self.st[r], self.stb[r]
        self.act(self.junk[:, 0:512], z0, AF.Square, [z0b], [self.junkb, stb], accum_out=st[:, 0:1])
        self.act(self.junk[:, 512:1024], z1, AF.Square, [z1b], [self.junkb, stb], accum_out=st[:, 1:2])
        self.tt("dve", st[:, 2:3], st[:, 0:1], st[:, 1:2], ALU.add, [stb], [stb])
        self.act(st[:, 3:4], st[:, 2:3], AF.Ln, [stb] + self.CB, [stb], scale=1.0 / D, bias=self.epsc)
        self.act(st[:, 4:5], st[:, 3:4], AF.Exp, [stb], [stb], scale=-0.5)
        for hf, (z, zb) in enumerate(((z0, z0b), (z1, z1b))):
            self.stt(self.ptmp[hf], z, st[:, 4:5], pnw[:, hf * 512:(hf + 1) * 512], ALU.mult, ALU.mult,
                     [zb, stb, pnwb], [self.ptmpb[hf]])
            xs = self.xTM[:, i, hf * 512:(hf + 1) * 512]
            self.tt("pool", xs, xs, self.ptmp[hf], ALU.add, [self.xb[i], self.ptmpb[hf]], [self.xb[i]])

    def out_proj_tile(self, lhs3, lhsb, col0, wsb, wsbb, nk, pnw, pnwb, i, tmp, tmpb):
        accs = ((self.ps[0][:], self.psb[0]), (self.ps[1][:], self.psb[1]))
        for hf in range(2):
            z, zb = accs[hf]
            for k in range(nk):
                self.mm(z, lhs3[:, k, col0:col0 + 128], wsb[:, k, hf * 512:(hf + 1) * 512], k == 0, k == nk - 1,
                        lhsb + [wsbb], [zb], inc=(k == nk - 1))
        self.post_tile(accs[0][0], accs[0][1], accs[1][0], accs[1][1], pnw, pnwb, i, tmp, tmpb)

    def phase_merge(self, s):
        S, T, TT, BLK, NB, TPB, CB = self.S, self.T, self.TT, self.BLK, self.NB, self.TPB, self.CB
        mixT = self.bf(8 * T).rearrange("p (k t) -> p k t", k=8)
        mixb = [[self.buf("mix") for _ in range(NB)] for _ in range(8)]
        wout = self.bf(8 * D).rearrange("p (k n) -> p k n", k=8); woutb = self.buf("wout")
        pnw = self.f32(D); pnwb = self.buf("pnw")
        tmp = [self.f32(BLK) for _ in range(4)]; tmpb = [self.buf("tmpM") for _ in range(4)]
        S.dma("sp", wout, self.wb_d["w_out"].rearrange("(k p) n -> p k n", p=128), reads=[self.wb_buf["w_out"]],
              writes=[woutb])
        S.dma("sp", pnw, self.pnw_d[0], writes=[pnwb])
        items = []
        for m in range(8):
            items += [("w_out_a", m * 128), ("w_in", C_G + m * 128), ("w_out_b", m * 128), ("w_in", C_G + D + m * 128)]
        pf = self.prefetch(items, LA=1)
        bg = self.cc("bgate")
        for m in range(8):
            wa, wab_ = next(pf)[1:]
            wga, wgab = next(pf)[1:]
            wbm, wbb = next(pf)[1:]
            wgb, wgbb = next(pf)[1:]
            for b in range(NB):
                ya, yab = self.proj_fm(wa, wab_, self.oaT, [self.oab[k][b] for k in range(8)], b)
                ga, gab = self.proj_fm(wga, wgab, self.hT, self.hblk(b), b)
                self.act(tmp[0], ga, AF.Sigmoid, [gab] + CB, [tmpb[0]], bias=bg[:, m:m + 1])
                self.tt("dve", tmp[1], ya, tmp[0], ALU.mult, [yab, tmpb[0]], [tmpb[1]])
                yb, ybb = self.proj_fm(wbm, wbb, self.obT, [self.obb[k][b] for k in range(8)], b)
                gb, gbb = self.proj_fm(wgb, wgbb, self.hT, self.hblk(b), b)
                self.act(tmp[2], gb, AF.Sigmoid, [gbb] + CB, [tmpb[2]], bias=bg[:, 8 + m:9 + m])
                self.tt("dve", tmp[3], yb, tmp[2], ALU.mult, [ybb, tmpb[2]], [tmpb[3]])
                self.tt("pool", mixT[:, m, self.bsl(b)], tmp[1], tmp[3], ALU.add, [tmpb[1], tmpb[3]], [mixb[m][b]])
        if self.dbg and s == 0:
            for (src, dstd, bb) in ((self.oaT, self.dbg_oa, self.oab), (self.obT, self.dbg_ob, self.obb)):
                for k in range(8):
                    for b in range(NB):
                        self.cpy("dve", tmp[0], src[:, k, self.bsl(b)], [bb[k][b]], [tmpb[0]])
                        S.dma("sp", dstd[:, k, self.bsl(b)], tmp[0], reads=[tmpb[0]], is_output=True)
        S.barrier()
        for i in range(TT):
            S.dma("sp", self.xTM[:, i, :], self.x_d[s, i * 128:(i + 1) * 128, :], writes=[self.xb[i]])
        for i in range(TT):
            b = i // TPB
            self.out_proj_tile(mixT, [mixb[k][b] for k in range(8)], i * 128, wout, woutb, 8, pnw, pnwb, i, tmp, tmpb)
        if self.dbg and s == 0:
            for i in range(TT):
                S.dma("sp", self.dbg_x1[i * 128:(i + 1) * 128, :], self.xTM[:, i, :], reads=[self.xb[i]], is_output=True)

    def phase_mem(self, s):
        S, T, TT, BLK, NB, TPB, CB = self.S, self.T, self.TT, self.BLK, self.NB, self.TPB, self.CB
        memT = self.bf(8 * MEM).rearrange("p (k t) -> p k t", k=8); memTb = [self.buf("memT") for _ in range(2)]
        kmT = self.bf(8 * MEM).rearrange("p (k t) -> p k t", k=8); kmTb = self.buf("kmT")
        vm = self.bf(2 * D).rearrange("p (i c) -> p i c", c=D); vmb = self.buf("vm")
        wbig = self.bf(8 * D).rearrange("p (k n) -> p k n", k=8); wbigb = self.buf("wbig")
        pnw = self.f32(D); pnwb = self.buf("pnw2")
        qmT = self.bf(8 * BLK).rearrange("p (k t) -> p k t", k=8); qmTb = [self.buf("qmT") for _ in range(8)]
        omT = self.bf(8 * BLK).rearrange("p (k t) -> p k t", k=8); omTb = [self.buf("omT") for _ in range(8)]
        Ej = [self.bf(BLK) for _ in range(4)]; Ejb = [self.buf("Em") for _ in range(4)]
        tmp = [self.f32(BLK) for _ in range(2)]; tmpb = [self.buf("tmpC") for _ in range(2)]
        S.dma("sp", wbig, self.wb_d["w_mkv"].rearrange("(k p) n -> p k n", p=128)[:, :, D:2 * D],
              reads=[self.wb_buf["w_mkv"]], writes=[wbigb])
        S.dma("sp", pnw, self.pnw_d[1], writes=[pnwb])
        for mt in range(2):
            r = mt % 2
            S.dma("sp", self.xin[r], self.mem_d[s, mt * 128:(mt + 1) * 128, :], writes=[self.xinb[r]])
            self.norm_tile(self.xin[r], self.xinb[r], self.cc("wpre_kv"), memT, memTb[mt], mt * 128)
        items = [("w_mkv", c * 128) for c in range(8)]
        for _ in range(NB):
            items += [("w_mq", c * 128) for c in range(8)]
        pf = self.prefetch(items, LA=4)
        for c in range(8):
            _, w, wb = next(pf)
            pa, pb = self.pst()
            for k in range(KC):
                self.mm(pa[:, 0:MEM], w[:, k, :], memT[:, k, :], k == 0, k == KC - 1, [wb] + memTb, [pb],
                        inc=(k == KC - 1))
            self.cpy("act", kmT[:, c, :], pa[:, 0:MEM], [pb], [kmTb])
        for mt in range(2):
            for hf in range(2):
                pa, pb = self.pst()
                for k in range(KC):
                    self.mm(pa, memT[:, k, mt * 128:(mt + 1) * 128], wbig[:, k, hf * 512:(hf + 1) * 512], k == 0,
                            k == KC - 1, [memTb[mt], wbigb], [pb], inc=(k == KC - 1))
                self.cpy("act", vm[:, mt, hf * 512:(hf + 1) * 512], pa, [pb], [vmb])
        S.dma("sp", wbig, self.wb_d["w_mo"].rearrange("(k p) n -> p k n", p=128), reads=[self.wb_buf["w_mo"]],
              writes=[wbigb])
        self.norm_from_x("wpre_mem")
        for b in range(NB):
            for c in range(8):
                _, w, wb = next(pf)
                o, pb = self.proj_fm(w, wb, self.hT, self.hblk(b), b)
                self.cpy("act", qmT[:, c, :], o, [pb], [qmTb[c]])
            for hm in range(4):
                for mt in range(2):
                    sc, scb = self.pst()
                    for dc in range(2):
                        c = 2 * hm + dc
                        self.mm(sc[:, 0:BLK], kmT[:, c, mt * 128:(mt + 1) * 128], qmT[:, c, :], dc == 0, dc == 1,
                                [kmTb, qmTb[c]], [scb], inc=(dc == 1))
                    ei = (hm % 2) * 2 + mt
                    self.act(Ej[ei], sc[:, 0:BLK], AF.Exp, [scb], [Ejb[ei]], scale=256 ** -0.5)
                e0, e1 = (hm % 2) * 2, (hm % 2) * 2 + 1
                sm, smb = self.pst()
                for mt, ei in ((0, e0), (1, e1)):
                    self.mm(sm[:, 0:BLK], self.ones_bf, Ej[ei], mt == 0, mt == 1, [Ejb[ei]] + CB, [smb], inc=(mt == 1))
                self.act(tmp[0], sm[:, 0:BLK], AF.Ln, [smb], [tmpb[0]])
                self.act(tmp[0], tmp[0], AF.Exp, [tmpb[0]], [tmpb[0]], scale=-1.0)
                for ec in range(2):
                    c = 2 * hm + ec
                    po, pob = self.pst()
                    for mt, ei in ((0, e0), (1, e1)):
                        self.mm(po[:, 0:BLK], vm[:, mt, c * 128:(c + 1) * 128], Ej[ei], mt == 0, mt == 1,
                                [vmb, Ejb[ei]], [pob], inc=(mt == 1))
                    self.tt("dve", omT[:, c, :], po[:, 0:BLK], tmp[0], ALU.mult, [pob, tmpb[0]], [omTb[c]])
            for j in range(TPB):
                i = b * TPB + j
                self.out_proj_tile(omT, omTb, j * 128, wbig, wbigb, 8, pnw, pnwb, i, tmp, tmpb)
        if self.dbg and s == 0:
            for i in range(TT):
                S.dma("sp", self.dbg_x2[i * 128:(i + 1) * 128, :], self.xTM[:, i, :], reads=[self.xb[i]], is_output=True)

    def phase_ffn(self, s):
        S, T, TT, BLK, NB, TPB, CB = self.S, self.T, self.TT, self.BLK, self.NB, self.TPB, self.CB
        wfo = self.bf(FC * D).rearrange("p (k n) -> p k n", k=FC); wfob = self.buf("wfo")
        aT = self.bf(FC * BLK).rearrange("p (k t) -> p k t", k=FC); aTb = [self.buf("aT") for _ in range(FC)]
        pnw = self.f32(D); pnwb = self.buf("pnw3")
        tmp = [self.f32(BLK) for _ in range(2)]; tmpb = [self.buf("tmpF") for _ in range(2)]
        S.dma("sp", wfo, self.wb_d["w_ffn_out"].rearrange("(k p) n -> p k n", p=128), reads=[self.wb_buf["w_ffn_out"]],
              writes=[wfob])
        S.dma("sp", pnw, self.pnw_d[2], writes=[pnwb])
        self.norm_from_x("wpre_ffn")
        items = []
        for _ in range(NB):
            for fc in range(FC):
                items += [("w_ffn_in", fc * 128), ("w_ffn_in", DFF + fc * 128)]
        pf = self.prefetch(items, LA=3)
        for b in range(NB):
            for fc in range(FC):
                wg, wgb = next(pf)[1:]
                wu, wub = next(pf)[1:]
                g, gb = self.proj_fm(wg, wgb, self.hT, self.hblk(b), b)
                u, ub_ = self.proj_fm(wu, wub, self.hT, self.hblk(b), b)
                t_i = fc % 2
                self.act(tmp[t_i], g, AF.Silu, [gb], [tmpb[t_i]])
                self.tt("dve", aT[:, fc, :], u, tmp[t_i], ALU.mult, [ub_, tmpb[t_i]], [aTb[fc]])
            for j in range(TPB):
                i = b * TPB + j
                self.out_proj_tile(aT, aTb, j * 128, wfo, wfob, FC, pnw, pnwb, i, tmp, tmpb)
                S.dma("sp", self.out_d[s, i * 128:(i + 1) * 128, :], self.xTM[:, i, :], reads=[self.xb[i]],
                      is_output=True)

    def emit(self):
        self.setup()
        for s in range(self.NSEQ):
            self.newphase()
            self.alloc_norm_tmps(True)
            self.norm_from_dram(s, "wpre_mix")
            self.phase_decay()
            self.newphase()
            self.mixer_a()
            self.newphase()
            self.mixer_b()
            self.newphase()
            self.alloc_norm_tmps(False)
            self.phase_merge(s)
            self.newphase()
            self.alloc_norm_tmps(True)
            self.phase_mem(s)
            self.newphase()
            self.alloc_norm_tmps(False)
            self.phase_ffn(s)
        self.S.finish()
        self.es.close()
        return self.nc


_PROG_CACHE = {}


def _host_inputs(inputs, T):
    cp, cbf = _make_cpack(inputs, T)
    pnw = np.stack([np.tile(np.asarray(inputs[n][0], np.float32).reshape(1, D), (128, 1))
                    for n in ("mix_post_norm", "mem_post_norm", "ffn_post_norm")], 0)
    shared = {"cpack": cp, "cbf": cbf, "pnw": np.ascontiguousarray(pnw)}
    for n, r, c in W_SPECS:
        shared[n] = np.ascontiguousarray(np.asarray(inputs[n][0], np.float32))
    return shared


def run(inputs, n_cores=N_CORES, dbg=False, inv32=False):
    x = np.asarray(inputs["x"], np.float32)
    mem = np.asarray(inputs["mem"], np.float32)
    B, T, _ = x.shape
    assert B % n_cores == 0
    nseq = B // n_cores
    key = (nseq, T, dbg, inv32)
    if key not in _PROG_CACHE:
        _PROG_CACHE[key] = Kern(nseq, T, dbg=dbg, inv32=inv32).emit()
    nc = _PROG_CACHE[key]
    shared = _host_inputs(inputs, T)
    in_maps = []
    for c in range(n_cores):
        m = dict(shared)
        m["x"] = np.ascontiguousarray(x[c * nseq:(c + 1) * nseq])
        m["mem"] = np.ascontiguousarray(mem[c * nseq:(c + 1) * nseq])
        in_maps.append(m)
    res = run_bass_kernel_spmd(nc, in_maps, core_ids=list(range(n_cores)))
    out = np.concatenate([np.asarray(r["out"], np.float32) for r in res.results], axis=0)
    return out, res


def kernel(**inputs):
    out, _ = run(inputs)
    return out
```

```python
import math
import os
from contextlib import ExitStack
import numpy as np
import ml_dtypes
import concourse.bass as bass
import concourse.mybir as mybir
from concourse.alu_op_type import AluOpType as ALU
from concourse.bass_utils import run_bass_kernel_spmd

F32 = mybir.dt.float32
BF16 = mybir.dt.bfloat16
AF = mybir.ActivationFunctionType

D = 1024
KC = 8
HD = 128
NH = 8
DFF = 2816
FC = DFF // 128
MEM = 256
IN_DIM = 7712
C_QKV, C_Z, C_AB, C_QB, C_KB, C_VB, C_G = 0, 3072, 4096, 4128, 5152, 5408, 5664
EPS = 1e-6
NEG = -1.0e5
N_CORES = 8


class Buf:
    __slots__ = ("name", "writer", "readers", "dsem", "dcount")

    def __init__(self, name):
        self.name = name
        self.writer = None
        self.readers = {}
        self.dsem = None
        self.dcount = 0


class Sched:
    SAME_ENGINE_SYNC = True

    def __init__(self, nc, es):
        self.nc = nc
        self.es = es
        self.engs = {"pe": nc.tensor, "act": nc.scalar, "dve": nc.vector,
                     "pool": nc.gpsimd, "sp": nc.sync}
        self.sem = {}
        self.count = {}
        for e in self.engs:
            self.sem[e] = es.enter_context(nc.semaphore("sem_" + e))
            self.count[e] = 0
        self.pending = {e: False for e in self.engs}
        self.waited = {e: {} for e in self.engs}
        self.n_inst = 0
        self.n_wait = 0
        self.out_tokens = []
        self.dsems = {}

    def _wait(self, e, tok):
        key, sem, val, src = tok
        if src == e and (e == "pe" or e == "sp" or not self.SAME_ENGINE_SYNC):
            return
        w = self.waited[e]
        if w.get(key, 0) >= val:
            return
        self.engs[e].wait_ge(sem, val)
        self.n_wait += 1
        w[key] = val

    def _deps(self, e, reads, writes):
        for b in reads:
            if b.writer is not None:
                self._wait(e, b.writer)
        for b in writes:
            if b.writer is not None:
                self._wait(e, b.writer)
            for tok in b.readers.values():
                self._wait(e, tok)

    def _commit(self, tok, reads, writes):
        for b in reads:
            b.readers[tok[0]] = tok
        for b in writes:
            b.writer = tok
            b.readers = {}

    def op(self, e, fn, reads=(), writes=(), inc=True):
        self._deps(e, reads, writes)
        inst = fn(self.engs[e])
        if inc:
            self.count[e] += 1
            inst.then_inc(self.sem[e], 1)
            tok = (e, self.sem[e], self.count[e], e)
            self.pending[e] = False
        else:
            tok = (e, self.sem[e], self.count[e] + 1, e)
            self.pending[e] = True
        self._commit(tok, reads, writes)
        self.n_inst += 1
        return inst

    def dma(self, q, out_ap, in_ap, reads=(), writes=(), sembuf=None, is_output=False, **kw):
        self._deps(q, reads, writes)
        b = sembuf if sembuf is not None else (writes[0] if writes else reads[0])
        if b.dsem is None:
            b.dsem = self.es.enter_context(self.nc.semaphore("d_" + b.name))
            self.dsems[b.name] = b
        b.dcount += 16
        inst = self.engs[q].dma_start(out=out_ap, in_=in_ap, **kw)
        inst.then_inc(b.dsem, 16)
        tok = ("d_" + b.name, b.dsem, b.dcount, None)
        self._commit(tok, reads, writes)
        if is_output:
            self.out_tokens.append(tok)
        self.n_inst += 1
        return inst

    def barrier(self):
        assert not any(self.pending.values())
        toks = [(e, self.sem[e], self.count[e], e) for e in self.engs if self.count[e] > 0]
        for b in self.dsems.values():
            if not b.name.startswith("wb_"):
                toks.append(("d_" + b.name, b.dsem, b.dcount, None))
        for e in self.engs:
            for tok in toks:
                if tok[3] != e:
                    self._wait(e, tok)

    def finish(self):
        last = {}
        for tok in self.out_tokens:
            last[tok[0]] = tok
        for tok in last.values():
            self._wait("sp", tok)


def _cpack_layout(T):
    ent = [("ident", 128), ("ones", 128), ("ucum", 128), ("lcum", 128),
           ("maskf", 128), ("maskb", 128),
           ("cosr", T // 64), ("sinr", T // 64), ("cosc", 64), ("sinc", 64),
           ("wpre_mix", 8), ("wpre_mem", 8), ("wpre_ffn", 8), ("wpre_kv", 8),
           ("convw", 24 * 5), ("lanorm", 1), ("qnorm", 1), ("knorm", 1), ("bgate", 16),
           ("alog", 16), ("dtb", 16)]
    off, o = {}, 0
    for n, c in ent:
        off[n] = (o, c)
        o += c
    return off, o


def _make_cpack(p, T):
    off, ncol = _cpack_layout(T)
    cp = np.zeros((128, ncol), np.float32)

    def put(name, arr):
        o, c = off[name]
        cp[:, o:o + c] = np.asarray(arr, np.float32).reshape(128, c)

    idx = np.arange(128)
    put("ident", np.eye(128))
    put("ones", np.ones((128, 128)))
    partner = np.where((idx % 64) < 32, idx + 32, idx - 32)
    pm = np.zeros((128, 128), np.float32)
    pm[partner, idx] = 1.0
    pp, ff = idx[:, None], idx[None, :]
    put("ucum", (pp <= ff))
    put("lcum", (pp >= ff))
    put("maskf", np.where(ff >= pp, 0.0, NEG))
    put("maskb", np.where(ff <= pp, 0.0, NEG))
    pb_, fb_ = pp // 32, ff // 32
    bd = (pb_ == fb_)
    m1f = ((pb_ == 1) & (fb_ == 0)) | ((pb_ == 3) & (fb_ == 2))
    m2f = (pb_ >= 2) & (fb_ <= 1)
    cbf = np.concatenate([np.eye(128), np.ones((128, 128)), pm, np.tile(np.eye(128), (1, 4)),
                          np.tile((ff > pp).astype(np.float32), (1, 4)),
                          np.tile((ff < pp).astype(np.float32), (1, 4)),
                          bd, m1f, m2f, m1f.T, m2f.T], axis=1).astype(ml_dtypes.bfloat16)
    half = HD // 2
    inv_freq = (np.float32(10000.0) ** (-np.arange(0, half, 2, dtype=np.float32) / np.float32(half))).astype(np.float32)
    rows = np.arange(T // 64, dtype=np.float32)
    cols = np.arange(64, dtype=np.float32)
    cosr = np.zeros((128, T // 64), np.float32); sinr = np.zeros_like(cosr)
    cosc = np.zeros((128, 64), np.float32); sinc = np.zeros_like(cosc)
    for d in range(64):
        i = d % 32
        sg = -1.0 if d < 32 else 1.0
        ang_r = (rows * inv_freq[i]).astype(np.float32)
        ang_c = (cols * inv_freq[i]).astype(np.float32)
        cosr[d] = np.cos(ang_r); sinr[d] = sg * np.sin(ang_r)
        cosc[64 + d] = np.cos(ang_c); sinc[64 + d] = sg * np.sin(ang_c)
    put("cosr", cosr); put("sinr", sinr); put("cosc", cosc); put("sinc", sinc)
    fm = lambda v: np.asarray(v, np.float32).reshape(-1, 128).T
    put("wpre_mix", fm(p["mix_pre_norm"][0]))
    put("wpre_mem", fm(p["mem_pre_norm"][0]))
    put("wpre_ffn", fm(p["ffn_pre_norm"][0]))
    put("wpre_kv", fm(p["mem_kv_norm"][0]))
    cw = np.asarray(p["conv_w"][0], np.float32)
    put("convw", cw.reshape(5, 24, 128).transpose(2, 1, 0))
    put("lanorm", np.asarray(p["la_norm_w"][0]).reshape(128, 1))
    put("qnorm", np.asarray(p["q_norm_w"][0]).reshape(128, 1))
    put("knorm", np.asarray(p["k_norm_w"][0]).reshape(128, 1))
    put("bgate", fm(p["b_gate"][0]))
    put("alog", np.tile(np.asarray(p["la_a_log"][0], np.float32).reshape(1, 16), (128, 1)))
    put("dtb", np.tile(np.asarray(p["la_dt_bias"][0], np.float32).reshape(1, 16), (128, 1)))
    return cp, np.ascontiguousarray(cbf)


W_SPECS = [("w_in", D, IN_DIM), ("w_out_a", D, D), ("w_out_b", D, D), ("w_out", D, D),
           ("w_mq", D, D), ("w_mkv", D, 2 * D), ("w_mo", D, D),
           ("w_ffn_in", D, 2 * DFF), ("w_ffn_out", DFF, D)]


class Kern:
    def __init__(self, NSEQ, T, dbg=False, inv32=False):
        self.NSEQ, self.T, self.dbg, self.inv32 = NSEQ, T, dbg, inv32
        self.TT = T // 128
        self.BLK = min(512, T)
        self.NB = T // self.BLK
        self.TPB = self.BLK // 128
        self.nc = nc = bass.Bass("TRN2", target_bir_lowering=False)
        self.es = es = ExitStack()
        self.S = Sched(nc, es)
        self.off, self.ncol = _cpack_layout(T)
        dt = nc.dram_tensor
        self.x_d = dt("x", [NSEQ, T, D], F32, kind="ExternalInput").ap()
        self.mem_d = dt("mem", [NSEQ, MEM, D], F32, kind="ExternalInput").ap()
        self.cp_d = dt("cpack", [128, self.ncol], F32, kind="ExternalInput").ap()
        self.pnw_d = dt("pnw", [3, 128, D], F32, kind="ExternalInput").ap()
        self.cbf_d = dt("cbf", [128, 2560], BF16, kind="ExternalInput").ap()
        self.out_d = dt("out", [NSEQ, T, D], F32, kind="ExternalOutput").ap()
        self.w_d, self.wb_d, self.wb_buf = {}, {}, {}
        for n, r, c in W_SPECS:
            self.w_d[n] = dt(n, [r, c], F32, kind="ExternalInput").ap()
            self.wb_d[n] = dt(n + "_bf", [r, c], BF16, kind="Internal").ap()
            self.wb_buf[n] = Buf("wb_" + n)
        if dbg:
            self.dbg_oa = dt("dbg_oa", [128, 8, T], F32, kind="ExternalOutput").ap()
            self.dbg_ob = dt("dbg_ob", [128, 8, T], F32, kind="ExternalOutput").ap()
            self.dbg_x1 = dt("dbg_x1", [T, D], F32, kind="ExternalOutput").ap()
            self.dbg_x2 = dt("dbg_x2", [T, D], F32, kind="ExternalOutput").ap()
        self.ARENA = 53200
        ARENA_OFF = 16544
        self.R32W = 12 * self.BLK + 128
        self.LIM = self.ARENA
        self.arena = es.enter_context(nc.sbuf_tensor("arena", [128, self.ARENA], F32))
        self.top = 0
        self.topB = self.LIM
        self.no_tail = True
        self.ps = [es.enter_context(nc.psum_tensor("ps%d" % i, [128, 512], F32)) for i in range(8)]
        self.psb = [Buf("ps%d" % i) for i in range(8)]
        self.ps_rr = 0
        self.ps_base, self.ps_n = 2, 6
        self.acc_rr = 0
        self.uid = 0

    def f32(self, n):
        if self.top + n <= self.LIM:
            a = self.arena[:, self.top:self.top + n]
            self.top += n
            return a
        assert not self.no_tail and self.topB + n <= self.ARENA, ("SBUF arena overflow", self.top, self.topB, n)
        a = self.tailf[:, self.topB - self.LIM:self.topB - self.LIM + n]
        self.topB += n
        return a

    def bf(self, n):
        assert n % 2 == 0
        return self.f32(n // 2).bitcast(BF16)

    def buf(self, name):
        self.uid += 1
        return Buf("%s_%d" % (name, self.uid))

    def pst(self):
        i = self.ps_base + (self.ps_rr % self.ps_n)
        self.ps_rr += 1
        return self.ps[i][:], self.psb[i]

    def cc(self, name):
        o, c = self.off[name]
        return self.cp[:, o:o + c]

    def mm(self, out, lhsT, rhs, start, stop, R, W, inc=True):
        self.S.op("pe", lambda e: e.matmul(out, lhsT=lhsT, rhs=rhs, start=start, stop=stop),
                  reads=R, writes=W, inc=inc)

    def act(self, out, in_, func, R, W, **kw):
        self.S.op("act", lambda e: e.activation(out=out, in_=in_, func=func, **kw), reads=R, writes=W)

    def tt(self, eng, out, in0, in1, op, R, W):
        self.S.op(eng, lambda e: e.tensor_tensor(out=out, in0=in0, in1=in1, op=op), reads=R, writes=W)

    def stt(self, out, in0, scalar, in1, op0, op1, R, W):
        self.S.op("dve", lambda e: e.scalar_tensor_tensor(out=out, in0=in0, scalar=scalar, in1=in1,
                                                           op0=op0, op1=op1), reads=R, writes=W)

    def cpy(self, eng, out, in_, R, W):
        if eng == "act":
            self.act(out, in_, AF.Copy, R, W)
        else:
            self.S.op(eng, lambda e: e.tensor_copy(out=out, in_=in_), reads=R, writes=W)

    def rsqrt_act(self, out, in_, scale, R, W, tmp, tmpb, post_bias=0.0):
        self.act(tmp, in_, AF.Ln, R, [tmpb], scale=scale, bias=self.epsc)
        if post_bias == 0.0:
            self.act(out, tmp, AF.Exp, [tmpb], W, scale=-0.5)
        else:
            self.act(out, tmp, AF.Exp, [tmpb], W, scale=-0.5, bias=self.qsc)

    def setup(self):
        S = self.S
        self.cp = self.f32(self.ncol)
        self.cpb = Buf("cpack")
        S.dma("sp", self.cp, self.cp_d[:, :], writes=[self.cpb])
        for n, r, c in W_SPECS:
            for c0 in range(0, c, 2048):
                c1 = min(c, c0 + 2048)
                S.dma("pool", self.wb_d[n][:, c0:c1], self.w_d[n][:, c0:c1],
                      writes=[self.wb_buf[n]], sembuf=self.wb_buf[n])
        cbf = self.bf(2560)
        self.cbb = Buf("constbf")
        S.dma("sp", cbf, self.cbf_d[:, :], writes=[self.cbb])
        self.ident_bf = cbf[:, 0:128]; self.ones_bf = cbf[:, 128:256]; self.perm_bf = cbf[:, 256:384]
        self.ident4_bf = cbf[:, 384:896]; self.strf_bf = cbf[:, 896:1408]; self.strb_bf = cbf[:, 1408:1920]
        self.bd_bf = cbf[:, 1920:2048]
        self.m1_bf = [cbf[:, 2048:2176], cbf[:, 2304:2432]]
        self.m2_bf = [cbf[:, 2176:2304], cbf[:, 2432:2560]]
        cb, R = self.cbb, [self.cpb]
        sm = self.f32(24)
        self.epsc = sm[:, 0:1]; self.qsc = sm[:, 1:2]; self.nega = sm[:, 8:24]
        S.op("dve", lambda e: e.memset(self.epsc, EPS), writes=[cb])
        S.op("dve", lambda e: e.memset(self.qsc, math.log(HD ** -0.5)), writes=[cb])
        self.act(self.nega, self.cc("alog"), AF.Exp, R, [cb])
        S.op("dve", lambda e: e.tensor_scalar(out=self.nega, in0=self.nega, scalar1=-1.0, scalar2=None,
                                              op0=ALU.mult), reads=[cb], writes=[cb])
        self.CB = [self.cpb, self.cbb]
        T = self.T
        self.hT = self.bf(KC * T).rearrange("p (k t) -> p k t", k=KC)
        self.hTb = [Buf("hT%d" % i) for i in range(self.TT)]
        self.R1 = self.f32(8 * T)
        self.oaT = self.R1[:, 0:4 * T].bitcast(BF16).rearrange("p (k t) -> p k t", k=8)
        self.obT = self.R1[:, 4 * T:8 * T].bitcast(BF16).rearrange("p (k t) -> p k t", k=8)
        self.xTM = self.R1[:, 0:8 * T].rearrange("p (i c) -> p i c", c=D)
        self.oab = [[Buf("oa%d_%d" % (h, b)) for b in range(self.NB)] for h in range(8)]
        self.obb = [[Buf("ob%d_%d" % (h, b)) for b in range(self.NB)] for h in range(8)]
        self.xb = [Buf("x%d" % i) for i in range(self.TT)]
        n = 2 * self.TT * 8
        v4 = lambda a: a.rearrange("p (d i h) -> p d i h", d=2, i=self.TT)
        self.gam = v4(self.f32(n)); self.ngam = v4(self.f32(n)); self.eg = v4(self.f32(n))
        self.ekd = v4(self.f32(n)); self.dl = v4(self.f32(n)); self.beta = v4(self.f32(n))
        self.nbeta = v4(self.f32(n))
        self.decb = Buf("decay")
        self.NW = 4
        self.wslot = [self.bf(1024).rearrange("p (k n) -> p k n", k=8) for _ in range(self.NW)]
        self.wslotb = [Buf("ws%d" % i) for i in range(self.NW)]
        self.ws_rr = 0
        self.st = [self.f32(8) for _ in range(2)]
        self.stb = [Buf("st%d" % i) for i in range(2)]
        self.nrr = 0
        self.mark = self.top

    def alloc_norm_tmps(self, need_xin):
        if need_xin:
            self.xin = [self.f32(D) for _ in range(2)]
            self.xinb = [self.buf("xin") for i in range(2)]
        self.xn = [self.bf(D) for _ in range(2)]
        self.xnb = [self.buf("xn") for i in range(2)]
        self.junk = self.bf(D); self.junkb = self.buf("junk")
        self.ptmp = [self.f32(512) for _ in range(2)]
        self.ptmpb = [self.buf("ptmp") for _ in range(2)]

    def newphase(self, no_tail=False):
        self.S.barrier()
        self.ps_base, self.ps_n = 2, 6
        self.top = self.mark
        self.topB = self.LIM
        self.no_tail = no_tail

    def wload(self, name, c0, ncols=128):
        i = self.ws_rr
        self.ws_rr = (self.ws_rr + 1) % self.NW
        src = self.wb_d[name].rearrange("(k p) n -> p k n", p=128)[:, :, c0:c0 + ncols]
        dst = self.wslot[i][:, :, 0:ncols]
        self.S.dma("sp", dst, src, reads=[self.wb_buf[name]], writes=[self.wslotb[i]])
        return dst, self.wslotb[i]

    def prefetch(self, items, LA=4):
        loaded = {}
        nxt = 0
        for i in range(len(items)):
            while nxt < min(len(items), i + LA):
                loaded[nxt] = self.wload(*items[nxt])
                nxt += 1
            ap, b = loaded.pop(i)
            yield i, ap, b

    def norm_tile(self, src, srcb, wcol, dst3, dstb, tcol):
        r = self.nrr
        self.nrr ^= 1
        st, stb, xn, xnb = self.st[r], self.stb[r], self.xn[r], self.xnb[r]
        self.act(self.junk, src, AF.Square, [srcb], [self.junkb, stb], accum_out=st[:, 0:1])
        self.act(st[:, 1:2], st[:, 0:1], AF.Ln, [stb] + self.CB, [stb], scale=1.0 / D, bias=self.epsc)
        self.act(st[:, 2:3], st[:, 1:2], AF.Exp, [stb], [stb], scale=-0.5)
        self.act(xn, src, AF.Copy, [srcb, stb], [xnb], scale=st[:, 2:3])
        for hf in range(2):
            pa, pb = self.pst()
            pv = pa.rearrange("p (k t) -> p k t", k=4)
            for k in range(4):
                kc = hf * 4 + k
                self.mm(pv[:, k, :], xn[:, kc * 128:(kc + 1) * 128], self.ident_bf, True, True,
                        [xnb] + self.CB, [pb])
            self.tt("dve", dst3[:, hf * 4:hf * 4 + 4, tcol:tcol + 128], pv,
                    wcol[:, hf * 4:hf * 4 + 4].unsqueeze(2).to_broadcast([128, 4, 128]), ALU.mult,
                    [pb] + self.CB, [dstb])

    def norm_from_dram(self, s, wname):
        for i in range(self.TT):
            r = i % 2
            self.S.dma("sp", self.xin[r], self.x_d[s, i * 128:(i + 1) * 128, :], writes=[self.xinb[r]])
            self.norm_tile(self.xin[r], self.xinb[r], self.cc(wname), self.hT, self.hTb[i], i * 128)

    def norm_from_x(self, wname):
        for i in range(self.TT):
            self.norm_tile(self.xTM[:, i, :], self.xb[i], self.cc(wname), self.hT, self.hTb[i], i * 128)

    def hblk(self, b):
        return [self.hTb[i] for i in range(b * self.TPB, (b + 1) * self.TPB)]

    def bsl(self, b):
        return slice(b * self.BLK, (b + 1) * self.BLK)

    def proj_fm(self, w, wb, rhs3, rhsb, b, n=None):
        pa, pb = self.pst()
        n = self.BLK if n is None else n
        o = pa[:, 0:n]
        nk = w.shape[1]
        for k in range(nk):
            self.mm(o, w[:, k, :], rhs3[:, k, b * self.BLK:b * self.BLK + n], k == 0, k == nk - 1,
                    [wb] + rhsb, [pb], inc=(k == nk - 1))
        return o, pb

    def phase_decay(self):
        S, TT = self.S, self.TT
        wab = self.bf(8 * 32).rearrange("p (k n) -> p k n", k=8)
        wabb = self.buf("wab")
        S.dma("sp", wab, self.wb_d["w_in"].rearrange("(k p) n -> p k n", p=128)[:, :, C_AB:C_AB + 32],
              reads=[self.wb_buf["w_in"]], writes=[wabb])
        n16 = TT * 16
        xa = self.f32(n16); ax = self.f32(n16); g = self.f32(n16); bt = self.f32(n16)
        tb = self.buf("dtmp")
        v3 = lambda a: a.rearrange("p (i c) -> p i c", c=16)
        pa, pb = self.pst()
        abp = pa[:, 0:TT * 32].rearrange("p (i c) -> p i c", c=32)
        for i in range(TT):
            for k in range(KC):
                self.mm(abp[:, i, :], self.hT[:, k, i * 128:(i + 1) * 128], wab[:, k, :], k == 0, k == KC - 1,
                        [self.hTb[i], wabb], [pb], inc=(k == KC - 1))
        dtb = self.cc("dtb").unsqueeze(1).to_broadcast([128, TT, 16])
        negab = self.nega.unsqueeze(1).to_broadcast([128, TT, 16])
        self.tt("dve", v3(xa), abp[:, :, 0:16], dtb, ALU.add, [pb] + self.CB, [tb])
        S.op("dve", lambda e: e.tensor_scalar(out=ax, in0=xa, scalar1=-1.0, scalar2=None, op0=ALU.mult),
             reads=[tb], writes=[tb])
        self.tt("dve", ax, ax, xa, ALU.min, [tb], [tb])
        self.act(ax, ax, AF.Exp, [tb], [tb])
        self.act(ax, ax, AF.Ln, [tb], [tb], bias=1.0)
        self.stt(xa, xa, 0.0, ax, ALU.max, ALU.add, [tb], [tb])
        self.tt("dve", v3(g), v3(xa), negab, ALU.mult, [tb] + self.CB, [tb])
        self.act(v3(bt), abp[:, :, 16:32], AF.Sigmoid, [pb], [tb])
        g3, b3 = v3(g), v3(bt)
        pg, pgb = self.pst()
        n8 = TT * 8
        gps = pg[:, 0:2 * n8].rearrange("p (d i h) -> p d i h", d=2, i=TT)
        tps = pg[:, 2 * n8:4 * n8].rearrange("p (d i h) -> p d i h", d=2, i=TT)
        for d in range(2):
            cum = self.cc("ucum") if d == 0 else self.cc("lcum")
            self.mm(gps[:, d], cum, g3[:, :, d * 8:(d + 1) * 8], True, True, [tb] + self.CB, [pgb])
            self.mm(tps[:, d], self.cc("ones"), g3[:, :, d * 8:(d + 1) * 8], True, True, [tb] + self.CB, [pgb])
        db = self.decb
        for d in range(2):
            self.cpy("dve", self.gam[:, d], gps[:, d], [pgb], [db])
            S.op("dve", lambda e, d=d: e.tensor_scalar(out=self.ngam[:, d], in0=gps[:, d], scalar1=-1.0,
                                                       scalar2=None, op0=ALU.mult), reads=[pgb], writes=[db])
            self.act(self.eg[:, d], gps[:, d], AF.Exp, [pgb], [db])
            self.tt("dve", self.ekd[:, d], tps[:, d], self.gam[:, d], ALU.subtract, [pgb, db], [db])
            self.act(self.ekd[:, d], self.ekd[:, d], AF.Exp, [db], [db])
            self.act(self.dl[:, d], tps[:, d], AF.Exp, [pgb], [db])
            self.cpy("dve", self.beta[:, d], b3[:, :, d * 8:(d + 1) * 8], [tb], [db])
            S.op("dve", lambda e, d=d: e.tensor_scalar(out=self.nbeta[:, d], in0=b3[:, :, d * 8:(d + 1) * 8],
                                                       scalar1=-1.0, scalar2=None, op0=ALU.mult),
                 reads=[tb], writes=[db])

    def mixer_a(self):
        S, T, TT, BLK, NB, TPB = self.S, self.T, self.TT, self.BLK, self.NB, self.TPB
        F32R = mybir.dt.float32r
        xtop = [4 * T]
        def xf32(n):
            a_ = self.R1[:, xtop[0]:xtop[0] + n]
            xtop[0] += n
            assert xtop[0] <= 8 * T
            return a_
        xbf = lambda n: xf32(n // 2).bitcast(BF16)
        pre = self.f32(T + 4); preb = self.buf("pre")
        oT = pre[:, 2:T + 2]; oTb = [self.buf("oT") for _ in range(TT)]
        cv = self.f32(T); cvb = self.buf("cv")
        qT = self.bf(T); kT = self.bf(T); vF = self.bf(T)
        qTb = [self.buf("qT") for _ in range(NB)]; kTb = [self.buf("kT") for _ in range(NB)]
        vFb = self.buf("vF")
        r3 = lambda a_: a_.rearrange("p (i d) -> p i d", d=128)
        ub = [r3(cv), r3(xf32(T))]
        ubb = [[self.buf("ub") for _ in range(NB)] for _ in range(2)]
        qg = [vF, xbf(T)]; qgb = [[self.buf("qg") for _ in range(NB)] for _ in range(2)]
        kd = [r3(self.bf(T)), r3(xbf(T))]; kdb = [self.buf("kd") for _ in range(2)]
        wT = [self.bf(T), xbf(T)]; wTb = [[self.buf("wT") for _ in range(NB)] for _ in range(2)]
        qkE = [self.bf(T), xbf(T)]; qkEb = [[self.buf("qkE") for _ in range(NB)] for _ in range(2)]
        kTM = r3(xbf(T)); kTMb = self.buf("kTM")
        vTM = r3(self.bf(T)); vTMb = self.buf("vTM")
        kg = r3(xbf(T)); kgb = self.buf("kg")
        Sst = [self.f32(128) for _ in range(2)]; Sbf = [self.bf(128) for _ in range(2)]
        Sb = [self.buf("S") for _ in range(2)]; Sbfb = [self.buf("Sbf") for _ in range(2)]
        vnew = [[self.bf(128) for _ in range(2)] for _ in range(2)]
        vnewb = [[self.buf("vnew") for _ in range(2)] for _ in range(2)]
        E = [self.bf(BLK) for _ in range(2)]; Eb = [self.buf("E") for _ in range(2)]
        Es = [self.bf(BLK) for _ in range(2)]; Esb = [self.buf("Es") for _ in range(2)]
        NG = 2
        Bm = [[self.bf(BLK) for _ in range(NG)] for e_ in range(2)]
        BTm = [[self.bf(BLK) for _ in range(NG)] for _ in range(2)]
        Pm = [[self.bf(BLK) for _ in range(NG)] for _ in range(2)]
        Bmb = [[self.buf("B") for _ in range(NG)] for _ in range(2)]
        BTmb = [[self.buf("BT") for _ in range(NG)] for _ in range(2)]
        Pmb = [[self.buf("P") for _ in range(NG)] for _ in range(2)]
        N1 = [self.bf(BLK) for _ in range(2)]; N1b = [self.buf("N1") for _ in range(2)]
        N2 = [self.bf(BLK) for _ in range(2)]; N2b = [self.buf("N2") for _ in range(2)]
        Tt = [self.bf(BLK) for _ in range(2)]; Ttb = [self.buf("Tt") for _ in range(2)]
        Xm = [self.bf(BLK) for _ in range(2)]; Xmb = [self.buf("Xm") for _ in range(2)]
        tmp = [self.f32(BLK) for _ in range(3)]; tmpb = [self.buf("tmpA") for _ in range(3)]
        sqb16 = self.bf(BLK); sqb16b = self.buf("sq16")
        S.op("pool", lambda e: e.memset(pre[:, 0:2], 0.0), writes=[preb])
        S.op("pool", lambda e: e.memset(pre[:, T + 2:T + 4], 0.0), writes=[preb])
        cw = self.cc("convw").rearrange("p (c j) -> p c j", j=5)
        CB = self.CB
        v4 = lambda a_: a_.rearrange("p (i c) -> p i c", c=128)
        HIER = not self.inv32
        NL = 4 if HIER else 6
        bc3 = lambda m_: m_.unsqueeze(1).to_broadcast([128, TPB, 128])

        for h in range(NH):
            items = [("w_in", C_QKV + t * 1024 + h * 128) for t in range(3)] + [("w_in", C_Z + h * 128)]
            wl = list(self.prefetch(items, LA=4))
            for t in range(3):
                _, w, wb = wl[t]
                for b in range(NB):
                    o, pb = self.proj_fm(w, wb, self.hT, self.hblk(b), b)
                    self.cpy("act", pre[:, 2 + b * BLK:2 + (b + 1) * BLK], o, [pb], [preb] + oTb)
                cch = t * 8 + h
                S.op("dve", lambda e: e.tensor_scalar(out=cv, in0=pre[:, 0:T], scalar1=cw[:, cch, 0:1], scalar2=None,
                                                      op0=ALU.mult), reads=[preb] + CB, writes=[cvb] + ubb[0])
                for j in range(1, 5):
                    self.stt(cv, pre[:, j:j + T], cw[:, cch, j:j + 1], cv, ALU.mult, ALU.add, [preb, cvb] + CB, [cvb])
                if t == 2:
                    self.act(vF, cv, AF.Silu, [cvb], [vFb] + qgb[0])
                    continue
                self.act(cv, cv, AF.Silu, [cvb], [cvb])
                dst, dstb = (qT, qTb) if t == 0 else (kT, kTb)
                for b in range(NB):
                    sl = self.bsl(b)
                    self.act(sqb16, cv[:, sl], AF.Square, [cvb], [sqb16b])
                    pa, pb = self.pst()
                    self.mm(pa[:, 0:BLK], self.ones_bf, sqb16, True, True, [sqb16b] + CB, [pb])
                    self.rsqrt_act(tmp[1], pa[:, 0:BLK], 1.0, [pb] + CB, [tmpb[1]], tmp[0], tmpb[0],
                                   post_bias=(1.0 if t == 0 else 0.0))
                    self.tt("dve", dst[:, sl], cv[:, sl], tmp[1], ALU.mult, [cvb, tmpb[1]], [dstb[b]])
            for b in range(NB):
                for (src, srcb, dst3, dstb) in ((kT, [kTb[b]], kTM, kTMb), (vF, [vFb], vTM, vTMb)):
                    pa, pb = self.pst()
                    pv = v4(pa[:, 0:BLK])
                    for j in range(TPB):
                        i = b * TPB + j
                        self.mm(pv[:, j, :], src[:, i * 128:(i + 1) * 128], self.ident_bf, True, True,
                                srcb + CB, [pb])
                    self.cpy("act", dst3[:, b * TPB:(b + 1) * TPB, :], pv, [pb], [dstb])
            for d in range(2):
                mask = self.cc("maskf") if d == 0 else self.cc("maskb")
                strm = self.strf_bf if d == 0 else self.strb_bf
                colb = lambda a_, i: a_[:, d, i, h:h + 1]
                ubw = (lambda b: [ubb[0][b], cvb]) if d == 0 else (lambda b: [ubb[1][b]])
                qgw = (lambda b: [qgb[0][b], vFb]) if d == 0 else (lambda b: [qgb[1][b]])
                self.tt("pool", kg, kTM, self.eg[:, d, :, h:h + 1].to_broadcast([128, TT, 128]), ALU.mult,
                        [kTMb, self.decb], [kgb])
                self.tt("pool", kd[d], kTM, self.ekd[:, d, :, h:h + 1].to_broadcast([128, TT, 128]), ALU.mult,
                        [kTMb, self.decb], [kdb[d]])
                for b0 in range(0, NB, 2):
                    blks = list(range(b0, min(NB, b0 + 2)))
                    for b in blks:
                        e_i = b % 2
                        pa, pb = self.pst()
                        pq, pqb = self.pst()
                        pv, pqv = v4(pa[:, 0:BLK]), v4(pq[:, 0:BLK])
                        for j in range(TPB):
                            i = b * TPB + j
                            self.mm(pv[:, j, :], colb(self.gam, i).to_broadcast([128, 128]), self.cc("ident"),
                                    True, False, [self.decb] + CB, [pb], inc=False)
                            self.mm(pv[:, j, :], self.cc("ident"), mask, False, True, CB, [pb])
                            self.mm(pqv[:, j, :], colb(self.eg, i).to_broadcast([128, 128]), self.cc("ident"),
                                    True, True, [self.decb] + CB, [pqb])
                        for j in range(TPB):
                            i = b * TPB + j
                            self.act(E[e_i][:, j * 128:(j + 1) * 128], pv[:, j, :], AF.Exp, [pb, self.decb],
                                     [Eb[e_i]], bias=colb(self.ngam, i), scale=1.0)
                        self.tt("dve", qg[d][:, self.bsl(b)], pq[:, 0:BLK], qT[:, self.bsl(b)], ALU.mult,
                                [pqb, qTb[b]], qgw(b))
                        self.tt("pool", Es[e_i], E[e_i], strm[:, 0:BLK], ALU.mult, [Eb[e_i]] + CB, [Esb[e_i]])
                        pk, pkb = self.pst()
                        pkv = v4(pk[:, 0:BLK])
                        for j in range(TPB):
                            i = b * TPB + j
                            ks = kT[:, i * 128:(i + 1) * 128]
                            self.mm(pkv[:, j, :], ks, ks, True, True, [kTb[b]], [pkb])
                        for j in range(TPB):
                            i = b * TPB + j
                            self.stt(BTm[e_i][0][:, j * 128:(j + 1) * 128], pkv[:, j, :], colb(self.nbeta, i),
                                     Es[e_i][:, j * 128:(j + 1) * 128], ALU.mult, ALU.mult,
                                     [pkb, Esb[e_i], self.decb], [BTmb[e_i][0]])
                        pk2, pk2b = self.pst()
                        pk2v = v4(pk2[:, 0:BLK])
                        for j in range(TPB):
                            i = b * TPB + j
                            self.mm(pk2v[:, j, :], kT[:, i * 128:(i + 1) * 128], qT[:, i * 128:(i + 1) * 128],
                                    True, True, [kTb[b], qTb[b]], [pk2b])
                        self.tt("dve", qkE[d][:, self.bsl(b)], pk2[:, 0:BLK], E[e_i], ALU.mult, [pk2b, Eb[e_i]],
                                [qkEb[d][b]])
                        pt, ptb = self.pst()
                        ptv = v4(pt[:, 0:BLK])
                        for j in range(TPB):
                            self.mm(ptv[:, j, :], BTm[e_i][0][:, j * 128:(j + 1) * 128], self.ident_bf, True, True,
                                    [BTmb[e_i][0]] + CB, [ptb])
                        if HIER:
                            self.cpy("act", N2[e_i], pt[:, 0:BLK], [ptb], [N2b[e_i]])
                            self.tt("dve", v4(Bm[e_i][0]), v4(N2[e_i]), bc3(self.bd_bf), ALU.mult, [N2b[e_i]] + CB,
                                    [Bmb[e_i][0]])
                            self.tt("pool", v4(BTm[e_i][0]), v4(BTm[e_i][0]), bc3(self.bd_bf), ALU.mult,
                                    [BTmb[e_i][0]] + CB, [BTmb[e_i][0]])
                        else:
                            self.cpy("act", Bm[e_i][0], pt[:, 0:BLK], [ptb], [Bmb[e_i][0]])
                        self.tt("pool", Pm[e_i][0], BTm[e_i][0], self.ident4_bf[:, 0:BLK], ALU.add,
                                [BTmb[e_i][0]] + CB, [Pmb[e_i][0]])
                        if HIER:
                            self.tt("pool", v4(N1[e_i]), v4(N2[e_i]), bc3(self.m1_bf[d]), ALU.mult,
                                    [N2b[e_i]] + CB, [N1b[e_i]])
                            self.tt("pool", v4(N2[e_i]), v4(N2[e_i]), bc3(self.m2_bf[d]), ALU.mult,
                                    [N2b[e_i]] + CB, [N2b[e_i]])
                    for lv in range(NL):
                        g0, g1 = lv % NG, (lv + 1) % NG
                        for b in blks:
                            e_i = b % 2
                            pB, pBb = self.pst()
                            pBv = v4(pB[:, 0:BLK])
                            for j in range(TPB):
                                sl = slice(j * 128, (j + 1) * 128)
                                self.mm(pBv[:, j, :], BTm[e_i][g0][:, sl], Bm[e_i][g0][:, sl], True, True,
                                        [BTmb[e_i][g0], Bmb[e_i][g0]], [pBb])
                            if lv < NL - 1:
                                pT, pTb = self.pst()
                                pTv = v4(pT[:, 0:BLK])
                                for j in range(TPB):
                                    sl = slice(j * 128, (j + 1) * 128)
                                    self.mm(pTv[:, j, :], Bm[e_i][g0][:, sl], BTm[e_i][g0][:, sl], True, True,
                                            [BTmb[e_i][g0], Bmb[e_i][g0]], [pTb])
                            self.cpy("act", Bm[e_i][g1], pB[:, 0:BLK], [pBb], [Bmb[e_i][g1]])
                            if lv < NL - 1:
                                self.cpy("dve", BTm[e_i][g1], pT[:, 0:BLK], [pTb], [BTmb[e_i][g1]])
                        for b in blks:
                            e_i = b % 2
                            pP, pPb = self.pst()
                            pPv = v4(pP[:, 0:BLK])
                            for j in range(TPB):
                                sl = slice(j * 128, (j + 1) * 128)
                                self.mm(pPv[:, j, :], self.ident_bf, Pm[e_i][g0][:, sl], True, False,
                                        [Pmb[e_i][g0]] + CB, [pPb], inc=False)
                                self.mm(pPv[:, j, :], Bm[e_i][g1][:, sl], Pm[e_i][g0][:, sl], False, True,
                                        [Pmb[e_i][g0], Bmb[e_i][g1]], [pPb])
                            self.cpy("act" if (b % 2 == 0) else "dve", Pm[e_i][g1], pP[:, 0:BLK], [pPb],
                                     [Pmb[e_i][g1]])
                    gcur = NL % NG
                    for (Nm, Nmb) in (((N1, N1b), (N2, N2b)) if HIER else ()):
                        gn = 1 - gcur
                        for b in blks:
                            e_i = b % 2
                            pt, ptb = self.pst()
                            ptv = v4(pt[:, 0:BLK])
                            px, pxb = self.pst()
                            pxv = v4(px[:, 0:BLK])
                            for j in range(TPB):
                                sl = slice(j * 128, (j + 1) * 128)
                                self.mm(ptv[:, j, :], Pm[e_i][gcur][:, sl], self.ident_bf, True, True,
                                        [Pmb[e_i][gcur]] + CB, [ptb])
                                self.mm(pxv[:, j, :], Nm[e_i][:, sl], Pm[e_i][gcur][:, sl], True, True,
                                        [Pmb[e_i][gcur], Nmb[e_i]], [pxb])
                            self.cpy("act", Tt[e_i], pt[:, 0:BLK], [ptb], [Ttb[e_i]])
                            self.cpy("dve", Xm[e_i], px[:, 0:BLK], [pxb], [Xmb[e_i]])
                        for b in blks:
                            e_i = b % 2
                            pP, pPb = self.pst()
                            pPv = v4(pP[:, 0:BLK])
                            for j in range(TPB):
                                sl = slice(j * 128, (j + 1) * 128)
                                self.mm(pPv[:, j, :], self.ident_bf, Pm[e_i][gcur][:, sl], True, False,
                                        [Pmb[e_i][gcur]] + CB, [pPb], inc=False)
                                self.mm(pPv[:, j, :], Tt[e_i][:, sl], Xm[e_i][:, sl], False, True,
                                        [Ttb[e_i], Xmb[e_i]], [pPb])
                            self.cpy("act" if (b % 2 == 0) else "dve", Pm[e_i][gn], pP[:, 0:BLK], [pPb],
                                     [Pmb[e_i][gn]])
                        gcur = gn
                    gf = gcur
                    for b in blks:
                        e_i = b % 2
                        Pf, Pfb = Pm[e_i][gf], Pmb[e_i][gf]
                        pu, pub = self.pst()
                        puv = v4(pu[:, 0:BLK])
                        pw, pwb = self.pst()
                        pwv = v4(pw[:, 0:BLK])
                        for j in range(TPB):
                            i = b * TPB + j
                            sl = slice(j * 128, (j + 1) * 128)
                            self.mm(puv[:, j, :], Pf[:, sl], vTM[:, i, :], True, True, [Pfb, vTMb], [pub])
                            self.mm(pwv[:, j, :], kg[:, i, :], Pf[:, sl], True, True, [Pfb, kgb], [pwb])
                        self.tt("dve", ub[d][:, b * TPB:(b + 1) * TPB, :], puv,
                                self.beta[:, d, b * TPB:(b + 1) * TPB, h:h + 1].to_broadcast([128, TPB, 128]),
                                ALU.mult, [pub, self.decb], ubw(b))
                        self.cpy("act", wT[d][:, self.bsl(b)], pw[:, 0:BLK], [pwb], [wTb[d][b]])
            for d in range(2):
                S.op("pool", lambda e: e.memset(Sst[d], 0.0), writes=[Sb[d]])
                S.op("pool", lambda e: e.memset(Sbf[d], 0.0), writes=[Sbfb[d]])
            written = [False] * TT
            for n_i in range(TT):
                for d in range(2):
                    i = n_i if d == 0 else TT - 1 - n_i
                    b = i // TPB
                    sl = slice(i * 128, (i + 1) * 128)
                    vn, vnb = vnew[d][n_i % 2], vnewb[d][n_i % 2]
                    colb = lambda a_, i_: a_[:, d, i_, h:h + 1]
                    p1, p1b = self.pst()
                    self.mm(p1[:, 0:128], wT[d][:, sl], Sbf[d], True, True, [wTb[d][b], Sbfb[d]], [p1b])
                    self.stt(vn, p1[:, 0:128], colb(self.nbeta, i), ub[d][:, i, :], ALU.mult, ALU.add,
                             [p1b, ubb[d][b], self.decb] + ([cvb] if d == 0 else []), [vnb])
                    p3, p3b = self.pst()
                    self.mm(p3[:, 0:128], kd[d][:, i, :], vn, True, True, [kdb[d], vnb], [p3b])
                    p2, p2b = self.pst()
                    self.mm(p2[:, 0:128], Sbf[d], qg[d][:, sl], True, False, [Sbfb[d], qgb[d][b]] + ([vFb] if d == 0 else []),
                            [p2b], inc=False)
                    self.mm(p2[:, 0:128], vn, qkE[d][:, sl], False, True, [vnb, qkEb[d][b]], [p2b])
                    if not os.environ.get("K_NOSBF"):
                        self.stt(Sbf[d], Sst[d], colb(self.dl, i), p3[:, 0:128], ALU.mult, ALU.add,
                                 [Sb[d], p3b, self.decb], [Sbfb[d]])
                    self.stt(Sst[d], Sst[d], colb(self.dl, i), p3[:, 0:128], ALU.mult, ALU.add,
                             [Sb[d], p3b, self.decb], [Sb[d]])
                    if os.environ.get("K_NOSBF"):
                        self.cpy("act", Sbf[d], Sst[d], [Sb[d]], [Sbfb[d]])
                    if not written[i]:
                        self.cpy("act", oT[:, sl], p2[:, 0:128], [p2b], [oTb[i], preb])
                        written[i] = True
                    else:
                        self.tt("dve", oT[:, sl], p2[:, 0:128], oT[:, sl], ALU.add, [p2b, oTb[i]], [oTb[i], preb])
            _, wz, wzb = wl[3]
            for b in range(NB):
                sl = self.bsl(b)
                ob_ = [oTb[i] for i in range(b * TPB, (b + 1) * TPB)]
                self.act(sqb16, oT[:, sl], AF.Square, ob_, [sqb16b])
                pa, pb = self.pst()
                self.mm(pa[:, 0:BLK], self.ones_bf, sqb16, True, True, [sqb16b] + CB, [pb])
                self.rsqrt_act(tmp[1], pa[:, 0:BLK], 1.0 / HD, [pb] + CB, [tmpb[1]], tmp[0], tmpb[0])
                self.stt(oT[:, sl], oT[:, sl], self.cc("lanorm"), tmp[1], ALU.mult, ALU.mult,
                         ob_ + [tmpb[1]] + CB, ob_ + [preb])
            for b in range(NB):
                sl = self.bsl(b)
                ob_ = [oTb[i] for i in range(b * TPB, (b + 1) * TPB)]
                o, pzb = self.proj_fm(wz, wzb, self.hT, self.hblk(b), b)
                self.act(tmp[2], o, AF.Silu, [pzb], [tmpb[2]])
                self.tt("pool", self.oaT[:, h, sl], oT[:, sl], tmp[2], ALU.mult, ob_ + [tmpb[2]], [self.oab[h][b]])

    def rope_norm_stages(self, proj, wname, dst, dstb, b, tmp, tmpb, t16, t16b, oraw=None):
        BLK, CB = self.BLK, self.CB
        st = {}

        def s1():
            o_, pb_ = proj()
            self.act(t16[0], o_, AF.Square, [pb_], [t16b[0]])
            if oraw is None:
                st["o"], st["pb"] = o_, pb_
            else:
                self.cpy("dve", oraw[0], o_, [pb_, t16b[0]], [oraw[1]])
                st["o"], st["pb"] = oraw

        def s2():
            pa, pab = self.pst()
            self.mm(pa[:, 0:BLK], self.ones_bf, t16[0], True, True, [t16b[0]] + CB, [pab])
            self.rsqrt_act(tmp[1], pa[:, 0:BLK], 1.0 / HD, [pab] + CB, [tmpb[1]], tmp[0], tmpb[0])

        def s3():
            self.stt(tmp[0], st["o"], self.cc(wname), tmp[1], ALU.mult, ALU.mult,
                     [st["pb"], tmpb[1], tmpb[0]] + CB, [tmpb[0]])
            self.cpy("act", t16[1], tmp[0], [tmpb[0]], [t16b[1]])

        def s4():
            ps_, psb_ = self.pst()
            self.mm(ps_[:, 0:BLK], self.perm_bf, t16[1], True, True, [t16b[1]] + CB, [psb_])
            nr = BLK // 64
            r0 = b * nr
            cosr = self.cc("cosr")[0:64, r0:r0 + nr].unsqueeze(2).to_broadcast([64, nr, 64])
            sinr = self.cc("sinr")[0:64, r0:r0 + nr].unsqueeze(2).to_broadcast([64, nr, 64])
            cosc = self.cc("cosc")[64:128, :].unsqueeze(1).to_broadcast([64, nr, 64])
            sinc = self.cc("sinc")[64:128, :].unsqueeze(1).to_broadcast([64, nr, 64])
            r3 = lambda a_, lo: a_[lo:lo + 64, 0:BLK].rearrange("p (r c) -> p r c", c=64)
            self.tt("pool", r3(tmp[1], 0), r3(tmp[0], 0), cosr, ALU.mult, [tmpb[0], tmpb[1]] + CB, [tmpb[1]])
            self.tt("pool", r3(tmp[1], 64), r3(tmp[0], 64), cosc, ALU.mult, [tmpb[0], tmpb[1]] + CB, [tmpb[1]])
            self.tt("dve", r3(tmp[2], 0), r3(ps_, 0), sinr, ALU.mult, [psb_] + CB, [tmpb[2]])
            self.tt("dve", r3(tmp[2], 64), r3(ps_, 64), sinc, ALU.mult, [psb_] + CB, [tmpb[2]])
            self.tt("pool", dst, tmp[1], tmp[2], ALU.add, [tmpb[1], tmpb[2]], dstb)

        return [s1, s2, s3, s4]

    def rope_norm(self, o, pb, wname, dst, dstb, b, tmp, tmpb, t16, t16b):
        for f in self.rope_norm_stages(lambda: (o, pb), wname, dst, dstb, b, tmp, tmpb, t16, t16b):
            f()

    def mixer_b(self):
        S, T, TT, BLK, NB, TPB, CB = self.S, self.T, self.TT, self.BLK, self.NB, self.TPB, self.CB
        kTr = self.bf(2 * T).rearrange("p (g t) -> p g t", g=2)
        kTrb = [[self.buf("kTr") for _ in range(NB)] for _ in range(2)]
        vB = self.bf(TT * 256).rearrange("p (i c) -> p i c", c=256)
        vBb = self.buf("vB")
        wv = self.bf(8 * 256).rearrange("p (k n) -> p k n", k=8); wvb = self.buf("wv")
        qTr = [self.bf(BLK) for _ in range(2)]; qTrb = [self.buf("qTr") for _ in range(2)]
        NE = 4
        Ej = [self.bf(BLK) for _ in range(NE)]; Ejb = [self.buf("Ej") for _ in range(NE)]
        tmp = [self.f32(BLK) for _ in range(3)]; tmpb = [self.buf("tmpB") for _ in range(3)]
        t16 = [self.bf(BLK) for _ in range(2)]; t16b = [self.buf("t16") for _ in range(2)]
        S.dma("sp", wv, self.wb_d["w_in"].rearrange("(k p) n -> p k n", p=128)[:, :, C_VB:C_VB + 256],
              reads=[self.wb_buf["w_in"]], writes=[wvb])
        items = [("w_in", C_KB + g * 128) for g in range(2)] + [("w_in", C_QB + hq * 128) for hq in range(8)]
        pf = self.prefetch(items, LA=3)
        for g in range(2):
            _, w, wb = next(pf)
            for b in range(NB):
                o, pb = self.proj_fm(w, wb, self.hT, self.hblk(b), b)
                self.rope_norm(o, pb, "knorm", kTr[:, g, self.bsl(b)], [kTrb[g][b]], b, tmp, tmpb, t16, t16b)
        for i in range(TT):
            pa, pb = self.pst()
            for k in range(KC):
                self.mm(pa[:, 0:256], self.hT[:, k, i * 128:(i + 1) * 128], wv[:, k, :], k == 0, k == KC - 1,
                        [self.hTb[i], wvb], [pb], inc=(k == KC - 1))
            self.cpy("act", vB[:, i, :], pa[:, 0:256], [pb], [vBb])
        self.ps_base, self.ps_n = 4, 4
        accs = [((self.ps[0][:], self.psb[0]), (self.ps[1][:], self.psb[1])),
                ((self.ps[2][:], self.psb[2]), (self.ps[3][:], self.psb[3]))]
        seq = [(hq, b) for hq in range(8) for b in range(NB)]
        wq = {}

        rfin = [self.f32(BLK) for _ in range(2)]; rfinb = [self.buf("rfin") for _ in range(2)]
        oraw = (self.f32(BLK), self.buf("oraw"))

        def prep_stages(n):
            hq, b = seq[n]
            if b == 0:
                wq[hq] = next(pf)[1:]
            w, wb = wq[hq]
            return self.rope_norm_stages(lambda: self.proj_fm(w, wb, self.hT, self.hblk(b), b), "qnorm",
                                         qTr[n % 2], [qTrb[n % 2]], b, tmp, tmpb, t16, t16b, oraw=oraw)

        for f in prep_stages(0):
            f()
        for n in range(len(seq)):
            hq, b = seq[n]
            g = hq // 4
            qi = n % 2
            (acc0, acc0b), (acc1, acc1b) = accs[n % 2]

            def issue_sc(j):
                ei = j % NE
                sc, scb = self.pst()
                self.mm(sc[:, 0:BLK], kTr[:, g, j * 128:(j + 1) * 128], qTr[qi], True, True,
                        [kTrb[g][j // TPB], qTrb[qi]], [scb])
                self.act(Ej[ei], sc[:, 0:BLK], AF.Exp, [scb], [Ejb[ei]], scale=HD ** -0.5)

            stages = prep_stages(n + 1) if n + 1 < len(seq) else []
            if os.environ.get("K_NOSTAGE"):
                for f in stages:
                    f()
                stages = []
            when = {}
            for si, f in enumerate(stages):
                when.setdefault(min(TT - 1, 1 + (si * max(1, TT - 2)) // len(stages)), []).append(f)
            issue_sc(0)
            if TT > 1:
                issue_sc(1)
            for j in range(TT):
                if j + 2 < TT:
                    issue_sc(j + 2)
                for f in when.get(j, []):
                    f()
                ei = j % NE
                self.mm(acc0[:, 0:BLK], vB[:, j, g * 128:(g + 1) * 128], Ej[ei], j == 0, j == TT - 1,
                        [vBb, Ejb[ei]], [acc0b], inc=(j == TT - 1))
                self.mm(acc1[:, 0:BLK], self.ones_bf, Ej[ei], j == 0, j == TT - 1,
                        [Ejb[ei]] + CB, [acc1b])
            rt, rtb = rfin[n % 2], rfinb[n % 2]
            self.act(rt, acc1[:, 0:BLK], AF.Ln, [acc1b], [rtb])
            self.act(rt, rt, AF.Exp, [rtb], [rtb], scale=-1.0)
            self.tt("dve", self.obT[:, hq, self.bsl(b)], acc0[:, 0:BLK], rt, ALU.mult,
                    [acc0b, rtb], [self.obb[hq][b]])
        self.ps_base, self.ps_n = 2, 6

    def post_tile(self, z0, z0b, z1, z1b, pnw, pnwb, i, tmp, tmpb):
        r = self.nrr
        self.nrr ^= 1
        st, stb = self.st[r], self.stb[r]
        self.act(self.junk[:, 0:512], z0, AF.Square, [z0b], [self.junkb, stb], accum_out=st[:, 0:1])
        self.act(self.junk[:, 512:1024], z1, AF.Square, [z1b], [self.junkb, stb], accum_out=st[:, 1:2])
        self.tt("dve", st[:, 2:3], st[:, 0:1], st[:, 1:2], ALU.add, [stb], [stb])
        self.act(st[:, 3:4], st[:, 2:3], AF.Ln, [stb] + self.CB, [stb], scale=1.0 / D, bias=self.epsc)
        self.act(st[:, 4:5], st[:, 3:4], AF.Exp, [stb], [stb], scale=-0.5)
        for hf, (z, zb) in enumerate(((z0, z0b), (z1, z1b))):
            self.stt(self.ptmp[hf], z, st[:, 4:5], pnw[:, hf * 512:(hf + 1) * 512], ALU.mult, ALU.mult,
                     [zb, stb, pnwb], [self.ptmpb[hf]])
            xs = self.xTM[:, i, hf * 512:(hf + 1) * 512]
            self.tt("pool", xs, xs, self.ptmp[hf], ALU.add, [self.xb[i], self.ptmpb[hf]], [self.xb[i]])

    def out_proj_tile(self, lhs3, lhsb, col0, wsb, wsbb, nk, pnw, pnwb, i, tmp, tmpb):
        pr = self.acc_rr % 2
        self.acc_rr += 1
        accs = ((self.ps[2 * pr][:], self.psb[2 * pr]), (self.ps[2 * pr + 1][:], self.psb[2 * pr + 1]))
        for hf in range(2):
            z, zb = accs[hf]
            for k in range(nk):
                self.mm(z, lhs3[:, k, col0:col0 + 128], wsb[:, k, hf * 512:(hf + 1) * 512], k == 0, k == nk - 1,
                        lhsb + [wsbb], [zb], inc=(k == nk - 1))
        self.post_tile(accs[0][0], accs[0][1], accs[1][0], accs[1][1], pnw, pnwb, i, tmp, tmpb)

    def phase_merge(self, s):
        S, T, TT, BLK, NB, TPB, CB = self.S, self.T, self.TT, self.BLK, self.NB, self.TPB, self.CB
        mixT = self.bf(8 * T).rearrange("p (k t) -> p k t", k=8)
        mixb = [[self.buf("mix") for _ in range(NB)] for _ in range(8)]
        wout = self.bf(8 * D).rearrange("p (k n) -> p k n", k=8); woutb = self.buf("wout")
        pnw = self.f32(D); pnwb = self.buf("pnw")
        tmp = [self.f32(BLK) for _ in range(4)]; tmpb = [self.buf("tmpM") for _ in range(4)]
        S.dma("sp", wout, self.wb_d["w_out"].rearrange("(k p) n -> p k n", p=128), reads=[self.wb_buf["w_out"]],
              writes=[woutb])
        S.dma("sp", pnw, self.pnw_d[0], writes=[pnwb])
        items = []
        for m in range(8):
            items += [("w_out_a", m * 128), ("w_in", C_G + m * 128), ("w_out_b", m * 128), ("w_in", C_G + D + m * 128)]
        pf = self.prefetch(items, LA=1)
        bg = self.cc("bgate")
        for m in range(8):
            wa, wab_ = next(pf)[1:]
            wga, wgab = next(pf)[1:]
            wbm, wbb = next(pf)[1:]
            wgb, wgbb = next(pf)[1:]
            for b in range(NB):
                ya, yab = self.proj_fm(wa, wab_, self.oaT, [self.oab[k][b] for k in range(8)], b)
                ga, gab = self.proj_fm(wga, wgab, self.hT, self.hblk(b), b)
                self.act(tmp[0], ga, AF.Sigmoid, [gab] + CB, [tmpb[0]], bias=bg[:, m:m + 1])
                self.tt("dve", tmp[1], ya, tmp[0], ALU.mult, [yab, tmpb[0]], [tmpb[1]])
                yb, ybb = self.proj_fm(wbm, wbb, self.obT, [self.obb[k][b] for k in range(8)], b)
                gb, gbb = self.proj_fm(wgb, wgbb, self.hT, self.hblk(b), b)
                self.act(tmp[2], gb, AF.Sigmoid, [gbb] + CB, [tmpb[2]], bias=bg[:, 8 + m:9 + m])
                self.tt("dve", tmp[3], yb, tmp[2], ALU.mult, [ybb, tmpb[2]], [tmpb[3]])
                self.tt("pool", mixT[:, m, self.bsl(b)], tmp[1], tmp[3], ALU.add, [tmpb[1], tmpb[3]], [mixb[m][b]])
        if self.dbg and s == 0:
            for (src, dstd, bb) in ((self.oaT, self.dbg_oa, self.oab), (self.obT, self.dbg_ob, self.obb)):
                for k in range(8):
                    for b in range(NB):
                        self.cpy("dve", tmp[0], src[:, k, self.bsl(b)], [bb[k][b]], [tmpb[0]])
                        S.dma("sp", dstd[:, k, self.bsl(b)], tmp[0], reads=[tmpb[0]], is_output=True)
        S.barrier()
        for i in range(TT):
            S.dma("sp", self.xTM[:, i, :], self.x_d[s, i * 128:(i + 1) * 128, :], writes=[self.xb[i]])
        self.ps_base, self.ps_n = 4, 4
        for i in range(TT):
            b = i // TPB
            self.out_proj_tile(mixT, [mixb[k][b] for k in range(8)], i * 128, wout, woutb, 8, pnw, pnwb, i, tmp, tmpb)
        self.ps_base, self.ps_n = 2, 6
        if self.dbg and s == 0:
            for i in range(TT):
                S.dma("sp", self.dbg_x1[i * 128:(i + 1) * 128, :], self.xTM[:, i, :], reads=[self.xb[i]], is_output=True)

    def phase_mem(self, s):
        S, T, TT, BLK, NB, TPB, CB = self.S, self.T, self.TT, self.BLK, self.NB, self.TPB, self.CB
        self.ps_base, self.ps_n = 4, 4
        memT = self.bf(8 * MEM).rearrange("p (k t) -> p k t", k=8); memTb = [self.buf("memT") for _ in range(2)]
        kmT = self.bf(8 * MEM).rearrange("p (k t) -> p k t", k=8); kmTb = self.buf("kmT")
        vm = self.bf(2 * D).rearrange("p (i c) -> p i c", c=D); vmb = self.buf("vm")
        wbig = self.bf(8 * D).rearrange("p (k n) -> p k n", k=8); wbigb = self.buf("wbig")
        pnw = self.f32(D); pnwb = self.buf("pnw2")
        qmT = self.bf(8 * BLK).rearrange("p (k t) -> p k t", k=8); qmTb = [self.buf("qmT") for _ in range(8)]
        omT = self.bf(8 * BLK).rearrange("p (k t) -> p k t", k=8); omTb = [self.buf("omT") for _ in range(8)]
        Ej = [self.bf(BLK) for _ in range(4)]; Ejb = [self.buf("Em") for _ in range(4)]
        tmp = [self.f32(BLK) for _ in range(2)]; tmpb = [self.buf("tmpC") for _ in range(2)]
        S.dma("sp", wbig, self.wb_d["w_mkv"].rearrange("(k p) n -> p k n", p=128)[:, :, D:2 * D],
              reads=[self.wb_buf["w_mkv"]], writes=[wbigb])
        S.dma("sp", pnw, self.pnw_d[1], writes=[pnwb])
        for mt in range(2):
            r = mt % 2
            S.dma("sp", self.xin[r], self.mem_d[s, mt * 128:(mt + 1) * 128, :], writes=[self.xinb[r]])
            self.norm_tile(self.xin[r], self.xinb[r], self.cc("wpre_kv"), memT, memTb[mt], mt * 128)
        items = [("w_mkv", c * 128) for c in range(8)]
        for _ in range(NB):
            items += [("w_mq", c * 128) for c in range(8)]
        pf = self.prefetch(items, LA=4)
        for c in range(8):
            _, w, wb = next(pf)
            pa, pb = self.pst()
            for k in range(KC):
                self.mm(pa[:, 0:MEM], w[:, k, :], memT[:, k, :], k == 0, k == KC - 1, [wb] + memTb, [pb],
                        inc=(k == KC - 1))
            self.cpy("act", kmT[:, c, :], pa[:, 0:MEM], [pb], [kmTb])
        for mt in range(2):
            for hf in range(2):
                pa, pb = self.pst()
                for k in range(KC):
                    self.mm(pa, memT[:, k, mt * 128:(mt + 1) * 128], wbig[:, k, hf * 512:(hf + 1) * 512], k == 0,
                            k == KC - 1, [memTb[mt], wbigb], [pb], inc=(k == KC - 1))
                self.cpy("act", vm[:, mt, hf * 512:(hf + 1) * 512], pa, [pb], [vmb])
        S.dma("sp", wbig, self.wb_d["w_mo"].rearrange("(k p) n -> p k n", p=128), reads=[self.wb_buf["w_mo"]],
              writes=[wbigb])
        self.norm_from_x("wpre_mem")
        for b in range(NB):
            for c in range(8):
                _, w, wb = next(pf)
                o, pb = self.proj_fm(w, wb, self.hT, self.hblk(b), b)
                self.cpy("act", qmT[:, c, :], o, [pb], [qmTb[c]])
            for hm in range(4):
                for mt in range(2):
                    sc, scb = self.pst()
                    for dc in range(2):
                        c = 2 * hm + dc
                        self.mm(sc[:, 0:BLK], kmT[:, c, mt * 128:(mt + 1) * 128], qmT[:, c, :], dc == 0, dc == 1,
                                [kmTb, qmTb[c]], [scb], inc=(dc == 1))
                    ei = (hm % 2) * 2 + mt
                    self.act(Ej[ei], sc[:, 0:BLK], AF.Exp, [scb], [Ejb[ei]], scale=256 ** -0.5)
                e0, e1 = (hm % 2) * 2, (hm % 2) * 2 + 1
                sm, smb = self.pst()
                for mt, ei in ((0, e0), (1, e1)):
                    self.mm(sm[:, 0:BLK], self.ones_bf, Ej[ei], mt == 0, mt == 1, [Ejb[ei]] + CB, [smb], inc=(mt == 1))
                self.act(tmp[0], sm[:, 0:BLK], AF.Ln, [smb], [tmpb[0]])
                self.act(tmp[0], tmp[0], AF.Exp, [tmpb[0]], [tmpb[0]], scale=-1.0)
                for ec in range(2):
                    c = 2 * hm + ec
                    po, pob = self.pst()
                    for mt, ei in ((0, e0), (1, e1)):
                        self.mm(po[:, 0:BLK], vm[:, mt, c * 128:(c + 1) * 128], Ej[ei], mt == 0, mt == 1,
                                [vmb, Ejb[ei]], [pob], inc=(mt == 1))
                    self.tt("dve", omT[:, c, :], po[:, 0:BLK], tmp[0], ALU.mult, [pob, tmpb[0]], [omTb[c]])
            for j in range(TPB):
                i = b * TPB + j
                self.out_proj_tile(omT, omTb, j * 128, wbig, wbigb, 8, pnw, pnwb, i, tmp, tmpb)
        if self.dbg and s == 0:
            for i in range(TT):
                S.dma("sp", self.dbg_x2[i * 128:(i + 1) * 128, :], self.xTM[:, i, :], reads=[self.xb[i]], is_output=True)

    def phase_ffn(self, s):
        S, T, TT, BLK, NB, TPB, CB = self.S, self.T, self.TT, self.BLK, self.NB, self.TPB, self.CB
        self.ps_base, self.ps_n = 4, 4
        wfo = self.bf(FC * D).rearrange("p (k n) -> p k n", k=FC); wfob = self.buf("wfo")
        aT = self.bf(FC * BLK).rearrange("p (k t) -> p k t", k=FC); aTb = [self.buf("aT") for _ in range(FC)]
        pnw = self.f32(D); pnwb = self.buf("pnw3")
        tmp = [self.f32(BLK) for _ in range(2)]; tmpb = [self.buf("tmpF") for _ in range(2)]
        S.dma("sp", wfo, self.wb_d["w_ffn_out"].rearrange("(k p) n -> p k n", p=128), reads=[self.wb_buf["w_ffn_out"]],
              writes=[wfob])
        S.dma("sp", pnw, self.pnw_d[2], writes=[pnwb])
        self.norm_from_x("wpre_ffn")
        items = []
        for _ in range(NB):
            for fc in range(FC):
                items += [("w_ffn_in", fc * 128), ("w_ffn_in", DFF + fc * 128)]
        pf = self.prefetch(items, LA=3)
        for b in range(NB):
            for fc in range(FC):
                wg, wgb = next(pf)[1:]
                wu, wub = next(pf)[1:]
                g, gb = self.proj_fm(wg, wgb, self.hT, self.hblk(b), b)
                u, ub_ = self.proj_fm(wu, wub, self.hT, self.hblk(b), b)
                t_i = fc % 2
                self.act(tmp[t_i], g, AF.Silu, [gb], [tmpb[t_i]])
                self.tt("dve", aT[:, fc, :], u, tmp[t_i], ALU.mult, [ub_, tmpb[t_i]], [aTb[fc]])
            for j in range(TPB):
                i = b * TPB + j
                self.out_proj_tile(aT, aTb, j * 128, wfo, wfob, FC, pnw, pnwb, i, tmp, tmpb)
                S.dma("sp", self.out_d[s, i * 128:(i + 1) * 128, :], self.xTM[:, i, :], reads=[self.xb[i]],
                      is_output=True)

    def emit(self):
        self.setup()
        for s in range(self.NSEQ):
            self.newphase()
            self.alloc_norm_tmps(True)
            self.norm_from_dram(s, "wpre_mix")
            self.phase_decay()
            self.newphase()
            self.mixer_a()
            self.newphase()
            self.mixer_b()
            self.newphase()
            self.alloc_norm_tmps(False)
            self.phase_merge(s)
            self.newphase()
            self.alloc_norm_tmps(True)
            self.phase_mem(s)
            self.newphase()
            self.alloc_norm_tmps(False)
            self.phase_ffn(s)
        self.S.finish()
        self.es.close()
        return self.nc


_PROG_CACHE = {}


def _host_inputs(inputs, T):
    cp, cbf = _make_cpack(inputs, T)
    pnw = np.stack([np.tile(np.asarray(inputs[n][0], np.float32).reshape(1, D), (128, 1))
                    for n in ("mix_post_norm", "mem_post_norm", "ffn_post_norm")], 0)
    shared = {"cpack": cp, "cbf": cbf, "pnw": np.ascontiguousarray(pnw)}
    for n, r, c in W_SPECS:
        shared[n] = np.ascontiguousarray(np.asarray(inputs[n][0], np.float32))
    return shared


def run(inputs, n_cores=N_CORES, dbg=False, inv32=False):
    x = np.asarray(inputs["x"], np.float32)
    mem = np.asarray(inputs["mem"], np.float32)
    B, T, _ = x.shape
    assert B % n_cores == 0
    nseq = B // n_cores
    key = (nseq, T, dbg, inv32)
    if key not in _PROG_CACHE:
        _PROG_CACHE[key] = Kern(nseq, T, dbg=dbg, inv32=inv32).emit()
    nc = _PROG_CACHE[key]
    shared = _host_inputs(inputs, T)
    in_maps = []
    for c in range(n_cores):
        m = dict(shared)
        m["x"] = np.ascontiguousarray(x[c * nseq:(c + 1) * nseq])
        m["mem"] = np.ascontiguousarray(mem[c * nseq:(c + 1) * nseq])
        in_maps.append(m)
    res = run_bass_kernel_spmd(nc, in_maps, core_ids=list(range(n_cores)))
    out = np.concatenate([np.asarray(r["out"], np.float32) for r in res.results], axis=0)
    return out, res


def kernel(**inputs):
    out, _ = run(inputs)
    return out
```
